# Optimizing a Trainium2 kernel written in Bass

```python
import math
import jax, jax.numpy as jnp
from jax import lax
import numpy as np

D_MODEL = 1024
BATCH = 4
SEQ = 4096
DEPTH = 1
DEC_BATCH = 32
DEC_SEQ = 2048
PAST_LEN = 128

MIX_WIDTH = D_MODEL
RET_WIDTH = MIX_WIDTH // 2
ATT_WIDTH = MIX_WIDTH - RET_WIDTH
RET_HEADS = 4
RET_HEAD_DIM = RET_WIDTH // RET_HEADS
RET_CHUNK = 128
ATT_Q_HEADS = 8
ATT_KV_HEADS = 2
ATT_GROUP = ATT_Q_HEADS // ATT_KV_HEADS
ATT_HEAD_DIM = ATT_WIDTH // ATT_Q_HEADS
ATT_KV_WIDTH = ATT_KV_HEADS * ATT_HEAD_DIM
Q_BLOCK = 128
GRID_W = 64
ROPE_THETA = 10000.0
EPS = 1e-6
IN_SIZES = (RET_WIDTH, RET_WIDTH, RET_WIDTH, RET_WIDTH, ATT_WIDTH, ATT_KV_WIDTH, ATT_KV_WIDTH, ATT_WIDTH)
IN_COLS = sum(IN_SIZES)

kernel_name = "hybrid_retention_gqa_encoder"


def rms_norm(x, g):
    xf = x.astype(jnp.float32)
    y = xf * lax.rsqrt(jnp.mean(xf * xf, axis=-1, keepdims=True) + EPS)
    return (y * g.astype(jnp.float32)).astype(x.dtype)


def grid_angles(seq_len, head_dim):
    rows = seq_len // GRID_W
    row = jnp.repeat(jnp.arange(rows, dtype=jnp.float32), GRID_W)
    col = jnp.tile(jnp.arange(GRID_W, dtype=jnp.float32), rows)
    half = head_dim // 2
    freqs = ROPE_THETA ** (-jnp.arange(0, half, 2, dtype=jnp.float32) / half)
    return row[:, None] * freqs[None, :], col[:, None] * freqs[None, :]


def rotate(x, ang):
    cos = jnp.cos(ang)[None, :, None, :].astype(x.dtype)
    sin = jnp.sin(ang)[None, :, None, :].astype(x.dtype)
    x1, x2 = jnp.split(x, 2, axis=-1)
    return jnp.concatenate([x1 * cos - x2 * sin, x2 * cos + x1 * sin], axis=-1)


def axial_rope(x, ang_row, ang_col):
    xr, xc = jnp.split(x, 2, axis=-1)
    return jnp.concatenate([rotate(xr, ang_row), rotate(xc, ang_col)], axis=-1)


def retention_one_direction(q, k, v, log_gamma, include_diag):
    b, s, h, dk = q.shape
    dv = v.shape[-1]
    n = s // RET_CHUNK
    qc = q.reshape(b, n, RET_CHUNK, h, dk)
    kc = k.reshape(b, n, RET_CHUNK, h, dk)
    vc = v.reshape(b, n, RET_CHUNK, h, dv)
    pos = jnp.arange(RET_CHUNK, dtype=jnp.float32)
    diff = pos[:, None] - pos[None, :]
    mask = (diff >= 0) if include_diag else (diff > 0)
    dmat = jnp.where(mask[None], jnp.exp(log_gamma[:, None, None] * jnp.maximum(diff, 0.0)[None]), 0.0)
    scores = jnp.einsum('bnchd,bnmhd->bnhcm', qc, kc) * dmat
    intra = jnp.einsum('bnhcm,bnmhe->bnche', scores, vc)
    k_dec = jnp.exp((RET_CHUNK - 1 - pos)[:, None] * log_gamma[None, :])
    states = jnp.einsum('bnchd,bnche->bnhde', kc * k_dec[:, :, None], vc)
    chunk_decay = jnp.exp(RET_CHUNK * log_gamma)[:, None, None]

    def step(r, st):
        return r * chunk_decay + st, r

    _, r_prev = lax.scan(step, jnp.zeros_like(states[:, 0]), jnp.moveaxis(states, 1, 0))
    r_prev = jnp.moveaxis(r_prev, 0, 1)
    q_dec = jnp.exp((pos + 1.0)[:, None] * log_gamma[None, :])
    cross = jnp.einsum('bnchd,bnhde->bnche', qc * q_dec[:, :, None], r_prev)
    return (intra + cross).reshape(b, s, h, dv)


def bidirectional_retention(q, k, v, log_rate_fwd, log_rate_bwd, gn_g):
    dtype = v.dtype
    qf, kf, vf = q.astype(jnp.float32), k.astype(jnp.float32) * (RET_HEAD_DIM ** -0.5), v.astype(jnp.float32)
    lg_f = -jnp.exp(log_rate_fwd.astype(jnp.float32))
    lg_b = -jnp.exp(log_rate_bwd.astype(jnp.float32))
    fwd = retention_one_direction(qf, kf, vf, lg_f, True)
    bwd = jnp.flip(retention_one_direction(jnp.flip(qf, 1), jnp.flip(kf, 1), jnp.flip(vf, 1), lg_b, False), 1)
    o = fwd + bwd
    o = o * lax.rsqrt(jnp.mean(o * o, axis=-1, keepdims=True) + EPS) * gn_g.astype(jnp.float32)
    b, s = o.shape[:2]
    return o.reshape(b, s, RET_WIDTH).astype(dtype)


def blocked_gqa(q, k, v):
    b, s, _, d = q.shape
    nb = s // Q_BLOCK
    qb = jnp.moveaxis(q.reshape(b, nb, Q_BLOCK, ATT_KV_HEADS, ATT_GROUP, d), 1, 0)
    scale = 1.0 / math.sqrt(d)

    def one_block(qi):
        sc = jnp.einsum('bqkgd,bskd->bkgqs', qi, k).astype(jnp.float32) * scale
        p = jax.nn.softmax(sc, axis=-1).astype(v.dtype)
        return jnp.einsum('bkgqs,bskd->bqkgd', p, v)

    out = lax.map(one_block, qb)
    return jnp.moveaxis(out, 0, 1).reshape(b, s, ATT_WIDTH)


def hybrid_layer(x, c, norm_g, w_ada, b_ada, w_in, log_rate_fwd, log_rate_bwd, gn_g, q_norm_g, k_norm_g, w_out):
    b, s, _ = x.shape
    mod = jax.nn.silu(c) @ w_ada + b_ada
    shift, scale, gate = jnp.split(mod, 3, axis=-1)
    h = rms_norm(x, norm_g) * (1.0 + scale[:, None, :]) + shift[:, None, :]
    proj = h @ w_in
    splits = list(np.cumsum(IN_SIZES)[:-1])
    rq, rk, rv, rg, aq, ak, av, ag = jnp.split(proj, splits, axis=-1)

    ang_r_ret, ang_c_ret = grid_angles(s, RET_HEAD_DIM)
    ang_r_att, ang_c_att = grid_angles(s, ATT_HEAD_DIM)

    rq = axial_rope(rq.reshape(b, s, RET_HEADS, RET_HEAD_DIM), ang_r_ret, ang_c_ret)
    rk = axial_rope(rk.reshape(b, s, RET_HEADS, RET_HEAD_DIM), ang_r_ret, ang_c_ret)
    rv = rv.reshape(b, s, RET_HEADS, RET_HEAD_DIM)
    ret_out = bidirectional_retention(rq, rk, rv, log_rate_fwd, log_rate_bwd, gn_g)

    aq = axial_rope(rms_norm(aq.reshape(b, s, ATT_Q_HEADS, ATT_HEAD_DIM), q_norm_g), ang_r_att, ang_c_att)
    ak = axial_rope(rms_norm(ak.reshape(b, s, ATT_KV_HEADS, ATT_HEAD_DIM), k_norm_g), ang_r_att, ang_c_att)
    av = av.reshape(b, s, ATT_KV_HEADS, ATT_HEAD_DIM)
    att_out = blocked_gqa(aq, ak, av)

    mixed = jnp.concatenate([ret_out * jax.nn.silu(rg), att_out * jax.nn.silu(ag)], axis=-1) @ w_out
    return x + gate[:, None, :] * mixed


def trunk(x, c, norm_g, w_ada, b_ada, w_in, ret_log_rate_fwd, ret_log_rate_bwd, ret_gn_g, q_norm_g, k_norm_g, w_out):
    for l in range(DEPTH):
        x = hybrid_layer(x, c, norm_g[l], w_ada[l], b_ada[l], w_in[l], ret_log_rate_fwd[l], ret_log_rate_bwd[l],
                         ret_gn_g[l], q_norm_g[l], k_norm_g[l], w_out[l])
    return x


def setup_inputs(seed: int = 0) -> dict:
    key = jax.random.key(seed)
    ks = jax.random.split(key, 15)
    f32 = jnp.float32
    base_rate = jnp.log(-jnp.log1p(-(2.0 ** (-5.0 - jnp.arange(RET_HEADS, dtype=f32)))))
    return {
        "x_prompt": jax.random.normal(ks[0], (BATCH, SEQ, D_MODEL), f32),
        "x_sample": jax.random.normal(ks[1], (DEC_BATCH, DEC_SEQ, D_MODEL), f32),
        "c_prompt": jax.random.normal(ks[2], (BATCH, D_MODEL), f32),
        "c_sample": jax.random.normal(ks[3], (DEC_BATCH, D_MODEL), f32),
        "norm_g": 1.0 + 0.02 * jax.random.normal(ks[4], (DEPTH, D_MODEL), f32),
        "w_ada": 0.5 * D_MODEL ** -0.5 * jax.random.normal(ks[5], (DEPTH, D_MODEL, 3 * D_MODEL), f32),
        "b_ada": 0.02 * jax.random.normal(ks[6], (DEPTH, 3 * D_MODEL), f32),
        "w_in": D_MODEL ** -0.5 * jax.random.normal(ks[7], (DEPTH, D_MODEL, IN_COLS), f32),
        "ret_log_rate_fwd": base_rate[None, :] + 0.05 * jax.random.normal(ks[8], (DEPTH, RET_HEADS), f32),
        "ret_log_rate_bwd": base_rate[None, :] + 0.05 * jax.random.normal(ks[9], (DEPTH, RET_HEADS), f32),
        "ret_gn_g": 1.0 + 0.02 * jax.random.normal(ks[10], (DEPTH, RET_HEADS, RET_HEAD_DIM), f32),
        "q_norm_g": 1.0 + 0.02 * jax.random.normal(ks[11], (DEPTH, ATT_HEAD_DIM), f32),
        "k_norm_g": 1.0 + 0.02 * jax.random.normal(ks[12], (DEPTH, ATT_HEAD_DIM), f32),
        "w_out": MIX_WIDTH ** -0.5 * jax.random.normal(ks[13], (DEPTH, MIX_WIDTH, D_MODEL), f32),
    }


def reference(x_prompt, x_sample, c_prompt, c_sample, norm_g, w_ada, b_ada, w_in, ret_log_rate_fwd, ret_log_rate_bwd,
              ret_gn_g, q_norm_g, k_norm_g, w_out):
    y_prompt = trunk(x_prompt, c_prompt, norm_g, w_ada, b_ada, w_in, ret_log_rate_fwd, ret_log_rate_bwd, ret_gn_g,
                     q_norm_g, k_norm_g, w_out)
    y_sample = trunk(x_sample, c_sample, norm_g, w_ada, b_ada, w_in, ret_log_rate_fwd, ret_log_rate_bwd, ret_gn_g,
                     q_norm_g, k_norm_g, w_out)
    return (y_prompt, y_sample)
```

```python
import contextlib
import math
import numpy as np
import concourse.bass as bass
import concourse.mybir as mybir
from concourse.bass_utils import run_bass_kernel_spmd

F32 = mybir.dt.float32
BF16 = mybir.dt.bfloat16
AF = mybir.ActivationFunctionType
ALU = mybir.AluOpType
AX = mybir.AxisListType

D = 1024
KC = 8
INC = 3328
EPS = 1e-6
O_RQ, O_RK, O_RV, O_RG, O_AQ, O_AK, O_AV, O_AG = 0, 512, 1024, 1536, 2048, 2560, 2688, 2816

EPOCH = 24000
NDMASEM = 32


class _Op:
    __slots__ = ("eng", "fn", "deps", "idx", "dma", "signal", "sem", "val", "waits", "slot")


class Sched:
    ENGS = ("pe", "act", "dve", "pool", "sp")

    def __init__(self):
        self.ops = []
        self.last_w = {}
        self.readers = {}
        self.ndma = 0
        self.slot_last = {}

    def add(self, eng, fn, reads=(), writes=(), dma=False):
        op = _Op()
        op.eng, op.fn, op.dma = eng, fn, dma
        op.idx = len(self.ops)
        op.signal = False
        deps = {}
        for k in reads:
            w = self.last_w.get(k)
            if w is not None:
                deps[w] = True
        for k in writes:
            w = self.last_w.get(k)
            if w is not None and w not in deps:
                deps[w] = False
            for r in self.readers.get(k, ()):
                if r not in deps:
                    deps[r] = False
        if dma:
            slot = self.ndma % NDMASEM
            self.ndma += 1
            op.slot = slot
            prev = self.slot_last.get(slot)
            if prev is not None and prev not in deps:
                deps[prev] = True
            self.slot_last[slot] = op.idx
        op.deps = deps
        for k in writes:
            self.last_w[k] = op.idx
            self.readers[k] = []
        wset = set(writes)
        for k in reads:
            if k not in wset:
                lst = self.readers.setdefault(k, [])
                if not dma:
                    lst[:] = [r for r in lst if self.ops[r].dma or self.ops[r].eng != eng]
                lst.append(op.idx)
        self.ops.append(op)
        return op.idx

    def _resolve(self):
        ops = self.ops
        waited = {e: {e2: -1 for e2 in self.ENGS} for e in self.ENGS}
        dma_seen = {e: set() for e in self.ENGS}
        for x in ops:
            need = {}
            x.waits = []
            for p_idx, raw in x.deps.items():
                p = ops[p_idx]
                if p.dma:
                    if p_idx not in dma_seen[x.eng]:
                        dma_seen[x.eng].add(p_idx)
                        x.waits.append(p_idx)
                    continue
                if p.eng == x.eng and not raw and not x.dma and x.eng == "pe":
                    continue
                if waited[x.eng][p.eng] >= p_idx:
                    continue
                if p.eng not in need or need[p.eng] < p_idx:
                    need[p.eng] = p_idx
            for e2, p_idx in need.items():
                waited[x.eng][e2] = p_idx
                ops[p_idx].signal = True
                x.waits.append(p_idx)

    def emit(self, nc, ctx):
        self._resolve()
        ops = self.ops
        counts = {e: 0 for e in self.ENGS}
        eng_sems = {}
        ndma = 0
        dma_sems = [ctx.enter_context(nc.semaphore(f"dq{i}")) for i in range(NDMASEM)]
        dma_cnt = [0] * NDMASEM
        for x in ops:
            if x.dma:
                s = x.slot
                dma_cnt[s] += 16
                x.sem, x.val = dma_sems[s], dma_cnt[s]
            elif x.signal:
                c = counts[x.eng]
                key = (x.eng, c // EPOCH)
                if key not in eng_sems:
                    eng_sems[key] = ctx.enter_context(nc.semaphore(f"s_{x.eng}_{key[1]}"))
                x.sem, x.val = eng_sems[key], c % EPOCH + 1
                counts[x.eng] = c + 1
        self.streams = {e: [] for e in self.ENGS}
        for x in ops:
            self.streams[x.eng].append(x)

    def run_stream(self, eng_name, eng):
        ops = self.ops
        for x in self.streams[eng_name]:
            for p_idx in x.waits:
                p = ops[p_idx]
                eng.wait_ge(p.sem, p.val)
            m, kw = x.fn
            if m == "wait_only":
                continue
            ins = getattr(eng, m)(**kw)
            if x.dma:
                ins.then_inc(x.sem, 16)
            elif x.signal:
                ins.then_inc(x.sem, 1)


class Rot:
    def __init__(self, name, n):
        self.name, self.n, self.i = name, n, -1

    def next(self):
        self.i = (self.i + 1) % self.n
        return self.i, f"{self.name}{self.i}"


_STAGE = 99
NG1 = 8
NG2 = 3
N_ST = 4
BT = 2
NQ = BT * 128


def build_nc(NS, SS, SPO, SPX):
    NB = NS + 1
    nc = bass.Bass("TRN2", target_bir_lowering=False)
    dt_in = lambda name, shape: nc.dram_tensor(name, shape, F32, kind="ExternalInput").ap()
    xs_d = dt_in("xs", [NS * SS, D])
    xp_d = dt_in("xp", [SPO + SPX, D])
    tab_d = dt_in("tab", [SS + SPO + SPX, 384])
    cT_d = dt_in("cT", [128, KC * NB])
    wada_d = dt_in("w_ada", [D, 3 * D])
    bada_d = dt_in("b_ada", [1, 3 * D])
    win_d = dt_in("w_in", [D, INC])
    wout_d = dt_in("w_out", [D, D])
    ngT_d = dt_in("ngT", [128, KC])
    lrf_d = dt_in("lrf", [1, 4])
    lrb_d = dt_in("lrb", [1, 4])
    gng_d = dt_in("gng", [1, 512])
    qg_d = dt_in("qg", [1, 64])
    kg_d = dt_in("kg", [1, 64])
    NCST = 8 + 128 + 128 + 64 + 2
    cst_d = dt_in("cst", [128, NCST])
    ys_d = nc.dram_tensor("ys", [NS * SS, D], F32, kind="ExternalOutput").ap()
    yp_d = nc.dram_tensor("yp", [SPO, D], F32, kind="ExternalOutput").ap()
    mod_d = nc.dram_tensor("mod_scratch", [NB, 3 * D], F32, kind="Internal").ap()

    NT_OWN = max(SS, SPO) // 128
    NT_K = max(SS, SPO + SPX) // 128
    S = Sched()

    with contextlib.ExitStack() as ctx:
        sb_bytes = [0]

        def sb(name, shape, dt=F32):
            n = 1
            for d_ in shape[1:]:
                n *= d_
            sb_bytes[0] += ((n * (2 if dt == BF16 else 4) + 31) // 32) * 32
            return ctx.enter_context(nc.sbuf_tensor("s_" + name, shape, dt))

        def ps(name, shape, dt=F32):
            return ctx.enter_context(nc.psum_tensor("p_" + name, shape, dt))

        w_in = sb("w_in_sb", [128, KC, INC], BF16)
        w_out = sb("w_out_sb", [128, KC, D], BF16)
        akT = sb("akT", [128, NT_K * 128], BF16)
        av_ext = sb("av_ext", [128, NT_K, 192], BF16)
        Rfb = sb("Rfb", [128, NT_OWN, 4, 2, 128], BF16)
        ident = sb("ident", [128, 128], BF16)
        identf = sb("identf", [128, 16])
        cst = sb("cst", [128, NCST])
        posc = cst[:, 0:8]
        DPm = cst[:, 8:136]
        DNm = cst[:, 136:264]
        jtab = cst[:, 264:328]
        flags = cst[:, 328:330]
        eps_t = sb("eps_t", [128, 1])
        one_t = sb("one_t", [128, 1])
        lns_t = sb("lns_t", [128, 1])
        lg = sb("lg", [128, 8])
        dec = sb("dec", [128, 6, 4])
        wb = sb("wb", [128, 16, 4])
        mask = sb("mask", [128, 4, 128], BF16)
        gng = sb("gng", [128, 512])
        qg = sb("qg", [128, 64])
        kg = sb("kg", [128, 64])
        ngT = sb("ngT", [128, KC])
        AB = sb("AB", [128, 2, KC, NB])
        cT = sb("cT", [128, KC, NB])
        gate_rep = sb("gate_rep", [128, D])

        xbuf = [sb(f"xbuf{i}", [128, D]) for i in range(2)]
        xs_bf = [sb(f"xsbf{i}", [128, D], BF16) for i in range(1)]
        stat = [sb(f"stat{i}", [128, 16]) for i in range(4)]
        hT2 = [sb(f"hT{i}", [128, KC, NQ], BF16) for i in range(2)]
        hTc = [0]
        tabt = [sb(f"tabt{i}", [128, 384]) for i in range(2)]
        t1 = [sb(f"t1_{i}", [128, 512]) for i in range(2)]
        t2 = [sb(f"t2_{i}", [128, 512]) for i in range(2)]
        t3 = [sb(f"t3_{i}", [128, 512]) for i in range(2)]
        tok_bf = [sb(f"tokbf{i}", [128, 512], BF16) for i in range(4)]
        rqT2 = [sb(f"rqT{i}", [128, 4, 128], BF16) for i in range(2)]
        rkT2 = [sb(f"rkT_t{i}", [128, 4, 128], BF16) for i in range(2)]
        v_bf2 = [sb(f"v_bf{i}", [128, 4, 128], BF16) for i in range(2)]
        sgbuf = [sb(f"sgbuf{i}", [128, 512]) for i in range(2)]
        oabuf = [sb(f"oabuf{i}", [128, 512]) for i in range(2)]
        st_f = oabuf[0][:].rearrange("p (h e) -> p h e", h=4)
        vfb = oabuf[1][:].bitcast(BF16).rearrange("p (h f e) -> p h f e", h=4, f=2)
        st_b = [sgbuf[i][:].rearrange("p (h e) -> p h e", h=4) for i in range(2)]
        xo = [sb(f"xo{i}", [128, D]) for i in range(1)]
        aqT2 = [sb(f"aqT{i}", [128, 4, NQ], BF16) for i in range(2)]
        sgT2 = [sb(f"sgT{i}", [128, 4, NQ], BF16) for i in range(2)]
        pm2 = [sb(f"pm{i}", [128, 4, 128], BF16) for i in range(2)]
        mixT_r2 = [sb(f"mixT_r{i}", [128, 4, NQ], BF16) for i in range(2)]
        mixT_a = sb("mixT_a", [128, 4, NQ], BF16)
        pT = [sb(f"pT{i}", [128, 2, NQ], BF16) for i in range(4)]
        rec = sb("rec", [128, NQ])

        if _STAGE != 99:
            print("SBUF bytes/partition:", sb_bytes[0])
        sT = [ps(f"sT{i}", [128, 2, NQ]) for i in range(4)]
        oT = ps("oT", [128, 512])
        G = [ps(f"G{i}", [128, 512]) for i in range(3)]
        flat = lambda t: t[:].rearrange("p a q -> p (a q)")
        G = [g[:, :] for g in G] + [flat(sT[3]), flat(sT[0]), flat(sT[1]), flat(sT[2]), oT[:, :]]
        GKEYS = ["G0", "G1", "G2", "sT3", "sT0", "sT1", "sT2", "oT"]

        r_x = Rot("xbuf", 2)
        r_xs = Rot("xsbf", 1)
        r_stat = Rot("stat", 4)
        r_tab = Rot("tabt", 2)
        r_t1 = Rot("t1_", 2)
        r_t2 = Rot("t2_", 2)
        r_t3 = Rot("t3_", 2)
        r_tok = Rot("tokbf", 4)
        class RotG:
            def __init__(self):
                self.n, self.i = 3, -1

            def next(self):
                self.i = (self.i + 1) % self.n
                return self.i, GKEYS[self.i]

        r_G = RotG()
        r_sT = Rot("sT", N_ST)
        r_pT = Rot("pT", N_ST)
        store_keys = []

        PSK = {"G0", "G1", "G2", "sT0", "sT1", "sT2", "sT3", "oT"}
        evt = [0]
        REC = [None]
        gcount = [0]
        KALIAS = {"st_f": "oabuf_t0", "vfb": "oabuf_t1", "st_b0": "sgbuf_t0", "st_b1": "sgbuf_t1"}
        r_xo = Rot("xo", 1)

        def I(eng, method, r=(), w=(), dma=False, **kw):
            r = [KALIAS.get(k, k) for k in r]
            w = [KALIAS.get(k, k) for k in w]
            w = list(w) + [k for k in r if k in PSK and k not in w]
            if REC[0] is not None:
                REC[0][-1].append((eng, (method, kw), list(r), w, dma))
            else:
                S.add(eng, (method, kw), reads=list(r), writes=w, dma=dma)

        def bc(ap, shape, axis):
            return ap.unsqueeze(axis).to_broadcast(shape)

        def v3(ap, h):
            return ap.rearrange("p (h d) -> p h d", h=h)

        for _once in (0,):
            I("sp", "dma_start", w=["cst"], dma=True, out=cst[:], in_=cst_d)
            I("sp", "dma_start", w=["lg"], dma=True, out=lg[:, 0:4], in_=lrf_d.partition_broadcast(128))
            I("sp", "dma_start", w=["lg"], dma=True, out=lg[:, 4:8], in_=lrb_d.partition_broadcast(128))
            I("sp", "dma_start", w=["gng"], dma=True, out=gng[:], in_=gng_d.partition_broadcast(128))
            I("sp", "dma_start", w=["qg"], dma=True, out=qg[:], in_=qg_d.partition_broadcast(128))
            I("sp", "dma_start", w=["kg"], dma=True, out=kg[:], in_=kg_d.partition_broadcast(128))
            I("sp", "dma_start", w=["ngT"], dma=True, out=ngT[:], in_=ngT_d)
            I("sp", "dma_start", w=["cT"], dma=True, out=cT[:].rearrange("p k b -> p (k b)"), in_=cT_d)

            I("pool", "memset", w=["eps"], ap=eps_t[:], constant=EPS)
            I("pool", "memset", w=["one"], ap=one_t[:], constant=1.0)
            I("pool", "memset", w=["lns"], ap=lns_t[:], constant=math.log(128.0 ** -0.5))
            I("pool", "memset", w=["av_all"], ap=av_ext[:], constant=1.0)
            I("pool", "memset", w=["t1_0"], ap=t1[0][:, 0:128], constant=1.0)
            I("pool", "affine_select", r=["t1_0"], w=["t1_0"], out=t1[0][:, 0:128], in_=t1[0][:, 0:128], pattern=[[-1, 128]],
              compare_op=ALU.is_equal, fill=0.0, base=0, channel_multiplier=1)
            I("pool", "tensor_copy", r=["t1_0"], w=["ident"], out=ident[:], in_=t1[0][:, 0:128])
            I("pool", "tensor_copy", r=["t1_0"], w=["identf"], out=identf[0:16, :], in_=t1[0][0:16, 0:16])

            if _STAGE < 0.2 and _STAGE < 1:
                break
            I("act", "activation", r=["lg"], w=["lg"], out=lg[:], in_=lg[:], func=AF.Exp)
            I("dve", "tensor_scalar_mul", r=["lg"], w=["lg"], out=lg[:], in0=lg[:], scalar1=-1.0)
            for i, (half, col, use_s) in enumerate([(0, 0, True), (1, 1, True), (0, 2, False), (1, 3, False),
                                                    (0, 4, False), (1, 4, False)]):
                kw = dict(out=dec[:, i, :], in_=lg[:, half * 4:half * 4 + 4], func=AF.Exp, scale=posc[:, col:col + 1])
                if use_s:
                    kw["bias"] = lns_t[:]
                I("act", "activation", r=["lg", "cst", "lns"], w=["dec"], **kw)
            I("dve", "tensor_tensor", r=["cst", "lg"], w=["wb"], out=wb[:], in0=jtab.rearrange("p (j h) -> p j h", h=4),
              in1=bc(lg[:, 4:8], [128, 16, 4], 1), op=ALU.mult)
            I("act", "activation", r=["wb"], w=["wb"], out=wb[:], in_=wb[:], func=AF.Exp)
            for h in range(4):
                hsl = slice(h * 128, (h + 1) * 128)
                I("dve", "tensor_scalar_mul", r=["cst", "lg"], w=["t2_0"], out=t2[0][:, hsl], in0=DPm, scalar1=lg[:, h:h + 1])
                I("dve", "scalar_tensor_tensor", r=["cst", "lg", "t2_0"], w=["t2_0"], out=t2[0][:, hsl], in0=DNm,
                  scalar=lg[:, 4 + h:5 + h], in1=t2[0][:, hsl], op0=ALU.mult, op1=ALU.add)
            I("act", "activation", r=["t2_0", "lns"], w=["mask"], out=mask[:].rearrange("p h c -> p (h c)"), in_=t2[0][:],
              func=AF.Exp, bias=lns_t[:])

            if _STAGE < 0.3 and _STAGE < 1:
                break
            NSTG = NT_OWN * 512
            STG = Rfb[:].rearrange("p n h f e -> p (n h f e)").bitcast(F32)
            PW = min(INC, NSTG // 2)
            slot_i = [0]

            def stage_slot(width):
                i = slot_i[0] % 2
                slot_i[0] += 1
                return STG[:, i * (NSTG // 2):i * (NSTG // 2) + width], f"stg{i}"

            ckeys = []
            for k in range(KC):
                for c0 in range(0, INC, PW):
                    c1 = min(INC, c0 + PW)
                    st, stk = stage_slot(c1 - c0)
                    I("sp", "dma_start", w=[stk], dma=True, out=st, in_=win_d[k * 128:(k + 1) * 128, c0:c1])
                    wd = c1 - c0
                    cuts = [0, (wd // 3) // 64 * 64, (2 * wd // 3) // 64 * 64, wd]
                    for ei, eng in enumerate(("dve", "pool", "act")):
                        lo, hi = cuts[ei], cuts[ei + 1]
                        if hi <= lo:
                            continue
                        ck = f"w_in_c{len(ckeys)}"
                        ckeys.append(ck)
                        if eng == "act":
                            I("act", "activation", r=[stk], w=[ck], out=w_in[:, k, c0 + lo:c0 + hi], in_=st[:, lo:hi], func=AF.Copy)
                        else:
                            I(eng, "tensor_copy", r=[stk], w=[ck], out=w_in[:, k, c0 + lo:c0 + hi], in_=st[:, lo:hi])
            I("dve", "memset", r=ckeys, w=["w_in"], ap=stat[0][:, 15:16], constant=0.0)
            ckeys = []
            for k in range(KC):
                st, stk = stage_slot(D)
                I("sp", "dma_start", w=[stk], dma=True, out=st, in_=wout_d[k * 128:(k + 1) * 128, :])
                for ei, eng in enumerate(("dve", "pool")):
                    ck = f"w_out_c{len(ckeys)}"
                    ckeys.append(ck)
                    I(eng, "tensor_copy", r=[stk], w=[ck], out=w_out[:, k, ei * 512:(ei + 1) * 512], in_=st[:, ei * 512:(ei + 1) * 512])
            I("dve", "memset", r=ckeys, w=["w_out"], ap=stat[0][:, 14:15], constant=0.0)

            if _STAGE < 0.4 and _STAGE < 1:
                break
            cTf = cT[:].rearrange("p k b -> p (k b)")
            nkb = KC * NB
            I("act", "activation", r=["cT"], w=["rec0", "rec1"], out=rec[:, 0:nkb], in_=cTf, func=AF.Exp, scale=-1.0)
            I("dve", "tensor_scalar_add", r=["rec0", "rec1"], w=["rec0", "rec1"], out=rec[:, 0:nkb], in0=rec[:, 0:nkb], scalar1=1.0)
            I("dve", "reciprocal", r=["rec0", "rec1"], w=["rec0", "rec1"], out=rec[:, 0:nkb], in_=rec[:, 0:nkb])
            I("dve", "tensor_tensor", r=["rec0", "rec1", "cT"], w=["cT"], out=cTf, in0=cTf, in1=rec[:, 0:nkb], op=ALU.mult)
            segs = [(t1[0], "t1_0"), (t1[1], "t1_1"), (t2[0], "t2_0"), (t2[1], "t2_1"), (t3[0], "t3_0"), (t3[1], "t3_1")]
            for cc in range(6):
                seg, segk = segs[cc]
                I("sp", "dma_start", r=["ident", "mask"], w=[segk], dma=True, out=seg[0:NB, :],
                  in_=bada_d[:, cc * 512:(cc + 1) * 512].partition_broadcast(NB))
                gi, gk = r_G.next()
                kper = min(KC, (NSTG // 2) // 512)
                for k0 in range(0, KC, kper):
                    st, stk = stage_slot(kper * 512)
                    st3 = st.rearrange("p (k c) -> p k c", k=kper)
                    I("sp", "dma_start", w=[stk], dma=True, out=st3,
                      in_=wada_d[k0 * 128:(k0 + kper) * 128, cc * 512:(cc + 1) * 512].rearrange("(k p) c -> p k c", p=128))
                    for kk in range(kper):
                        k = k0 + kk
                        I("pe", "matmul", r=[stk, "cT"], w=[gk], out=G[gi][0:NB, :], lhsT=cT[:, k, :], rhs=st3[:, kk, :],
                          start=(k == 0), stop=(k == KC - 1))
                I("dve", "tensor_tensor", r=[gk, segk], w=[segk], out=seg[0:NB, :], in0=G[gi][0:NB, :], in1=seg[0:NB, :], op=ALU.add)
                I("sp", "dma_start", r=[segk], w=["mod_d"], dma=True, out=mod_d[:, cc * 512:(cc + 1) * 512], in_=seg[0:NB, :])
                if cc < 4:
                    gi2, gk2 = r_G.next()
                    for j in range(4):
                        I("pe", "transpose", r=[segk, "identf"], w=[gk2], out=G[gi2][:, j * NB:(j + 1) * NB],
                          in_=seg[0:NB, j * 128:(j + 1) * 128], identity=identf[0:NB, 0:NB])
                    for j in range(4):
                        kk = cc * 4 + j
                        I("dve", "tensor_copy", r=[gk2], w=["AB"], out=AB[:, kk // 8, kk % 8, :], in_=G[gi2][:, j * NB:(j + 1) * NB])
            if _STAGE < 0.5 and _STAGE < 1:
                break
            I("dve", "tensor_scalar_add", r=["AB"], w=["AB"], out=AB[:, 1], in0=AB[:, 1], scalar1=1.0)
            I("dve", "tensor_tensor", r=["AB", "ngT"], w=["AB"], out=AB[:, 1], in0=AB[:, 1], in1=bc(ngT[:], [128, KC, NB], 2),
              op=ALU.mult)

        xseq = []
        xpos = [0]

        def next_x_load(t):
            k = xpos[0] + (1 if t == BT - 1 else 0)
            nxt = xpos[0] + 1
            if t == BT - 1:
                xpos[0] += 1
            if nxt < len(xseq):
                src, r0 = xseq[nxt]
                I("sp", "dma_start", w=[f"xbuf{t}"], dma=True, out=xbuf[t][:], in_=src[r0 + t * 128:r0 + (t + 1) * 128, :])

        def front(x_src, row0, ntile, b):
            for t in range(ntile):
                xi, xk = t, f"xbuf{t}"
                si, sk = r_xs.next()
                ti, tk = r_stat.next()
                I("act", "activation", r=[xk], w=[sk, tk], out=xs_bf[si][:], in_=xbuf[xi][:], func=AF.Square,
                  accum_out=stat[ti][:, 0:1])
                I("act", "activation", r=[tk, "eps"], w=[tk], out=stat[ti][:, 1:2], in_=stat[ti][:, 0:1], func=AF.Ln,
                  scale=1.0 / D, bias=eps_t[:])
                I("act", "activation", r=[tk], w=[tk], out=stat[ti][:, 2:3], in_=stat[ti][:, 1:2], func=AF.Exp, scale=-0.5)
                I("act", "activation", r=[xk, tk], w=[sk], out=xs_bf[si][:], in_=xbuf[xi][:], func=AF.Copy, scale=stat[ti][:, 2:3])
                next_x_load(t)
                hp_ = hTc[0]
                for k in range(KC):
                    I("sp", "dma_start_transpose", r=[sk], w=[f"hT{hp_}_{t}_{k}"], dma=True, out=hT2[hp_][:, k, t * 128:(t + 1) * 128],
                      in_=xs_bf[si][:, k * 128:(k + 1) * 128])
                for k in range(KC):
                    hk = f"hT{hp_}_{t}_{k}"
                    hsl = hT2[hp_][:, k, t * 128:(t + 1) * 128]
                    if k % 2 == 0:
                        I("act", "activation", r=[hk, "AB"], w=[hk], out=hsl, in_=hsl, func=AF.Identity,
                          scale=AB[:, 1, k, b:b + 1], bias=AB[:, 0, k, b:b + 1])
                    else:
                        I("dve", "tensor_scalar", r=[hk, "AB"], w=[hk], out=hsl, in0=hsl, scalar1=AB[:, 1, k, b:b + 1],
                          scalar2=AB[:, 0, k, b:b + 1], op0=ALU.mult, op1=ALU.add)

        def front_g(x_src, row0, ntile, b):
            for t in range(ntile):
                xi, xk = t, f"xbuf{t}"
                si, sk = r_xs.next()
                ti, tk = r_stat.next()
                I("act", "activation", r=[xk], w=[sk, tk], out=xs_bf[si][:], in_=xbuf[xi][:], func=AF.Square,
                  accum_out=stat[ti][:, 0:1])
                I("act", "activation", r=[tk, "eps"], w=[tk], out=stat[ti][:, 1:2], in_=stat[ti][:, 0:1], func=AF.Ln,
                  scale=1.0 / D, bias=eps_t[:])
                I("act", "activation", r=[tk], w=[tk], out=stat[ti][:, 2:3], in_=stat[ti][:, 1:2], func=AF.Exp, scale=-0.5)
                I("act", "activation", r=[xk, tk], w=[sk], out=xs_bf[si][:], in_=xbuf[xi][:], func=AF.Copy, scale=stat[ti][:, 2:3])
                next_x_load(t)
                hp_ = hTc[0]
                for k in range(KC):
                    I("sp", "dma_start_transpose", r=[sk], w=[f"hT{hp_}_{t}_{k}"], dma=True, out=hT2[hp_][:, k, t * 128:(t + 1) * 128],
                      in_=xs_bf[si][:, k * 128:(k + 1) * 128])
                for k in range(KC):
                    hk = f"hT{hp_}_{t}_{k}"
                    hsl = hT2[hp_][:, k, t * 128:(t + 1) * 128]
                    if k % 2 == 0:
                        I("act", "activation", r=[hk, "AB"], w=[hk], out=hsl, in_=hsl, func=AF.Identity,
                          scale=AB[:, 1, k, b:b + 1], bias=AB[:, 0, k, b:b + 1])
                    else:
                        I("dve", "tensor_scalar", r=[hk, "AB"], w=[hk], out=hsl, in0=hsl, scalar1=AB[:, 1, k, b:b + 1],
                          scalar2=AB[:, 0, k, b:b + 1], op0=ALU.mult, op1=ALU.add)
                yield

        def proj_tok(t, c0, ncol):
            gi, gk = r_G.next()
            for k in range(KC):
                I("pe", "matmul", r=[f"hT{hTc[0]}_{t}_{k}", "w_in"], w=[gk], out=G[gi][:, 0:ncol], lhsT=hT2[hTc[0]][:, k, t * 128:(t + 1) * 128],
                  rhs=w_in[:, k, c0:c0 + ncol], start=(k == 0), stop=(k == KC - 1))
            return gi, gk

        def load_tab(row):
            ti, tk = r_tab.next()
            I("sp", "dma_start", w=[tk], dma=True, out=tabt[ti][:], in_=tab_d[row:row + 128, :])
            return ti, tk

        def rope(src, src_keys, src_psum, nh, hd, cosv, sinv, tkey, out_ap, out_key):
            q = hd // 4
            a, ak_ = r_t1.next()
            bq, bk_ = r_t2.next()
            n = nh * hd
            I("dve", "tensor_tensor", r=list(src_keys) + [tkey], w=[ak_],
              out=v3(t1[a][:, 0:n], nh), in0=v3(src, nh), in1=bc(cosv, [128, nh, hd], 1), op=ALU.mult)
            s5 = src.rearrange("p (h f j i) -> p h f j i", h=nh, f=2, j=2, i=q)
            o5 = t2[bq][:, 0:n].rearrange("p (h f j i) -> p h f j i", h=nh, f=2, j=2, i=q)
            sn4 = sinv.rearrange("p (f j i) -> p f j i", f=2, j=2, i=q)
            for j in range(2):
                I("dve", "tensor_tensor", r=list(src_keys) + [tkey], w=[bk_],
                  out=o5[:, :, :, j, :], in0=s5[:, :, :, 1 - j, :], in1=bc(sn4[:, :, j, :], [128, nh, 2, q], 1), op=ALU.mult)
            I("dve", "tensor_tensor", r=[ak_, bk_], w=[out_key], out=out_ap, in0=t1[a][:, 0:n], in1=t2[bq][:, 0:n], op=ALU.add)

        def head_rstd(src_ap, src_key, nh, hd, src_psum=True):
            a, ak_ = r_t3.next()
            n = nh * hd
            ti, tk = r_stat.next()
            I("act", "activation", r=[src_key], w=[ak_], out=t3[a][:, 0:n], in_=src_ap, func=AF.Square)
            I("dve", "tensor_reduce", r=[ak_], w=[tk], out=stat[ti][:, 4:4 + nh], in_=v3(t3[a][:, 0:n], nh), axis=AX.X, op=ALU.add)
            I("act", "activation", r=[tk, "eps"], w=[tk], out=stat[ti][:, 4:4 + nh], in_=stat[ti][:, 4:4 + nh], func=AF.Ln,
              scale=1.0 / hd, bias=eps_t[:])
            I("act", "activation", r=[tk], w=[tk], out=stat[ti][:, 4:4 + nh], in_=stat[ti][:, 4:4 + nh], func=AF.Exp, scale=-0.5)
            return stat[ti][:, 4:4 + nh], tk

        def qk_norm_rope(gi, gk, nh, gain, gain_key, ti, tk, out_ap, out_key):
            n = nh * 64
            src = G[gi][:, 0:n]
            rs, rsk = head_rstd(src, gk, nh, 64)
            a, ak_ = r_t3.next()
            I("dve", "tensor_tensor", r=[gk, rsk], w=[ak_], out=v3(t3[a][:, 0:n], nh), in0=v3(src, nh),
              in1=bc(rs, [128, nh, 64], 2), op=ALU.mult)
            I("dve", "tensor_tensor", r=[ak_, gain_key], w=[ak_], out=v3(t3[a][:, 0:n], nh), in0=v3(t3[a][:, 0:n], nh),
              in1=bc(gain, [128, nh, 64], 1), op=ALU.mult)
            rope(t3[a][:, 0:n], [ak_], False, nh, 64, tabt[ti][:, 256:320], tabt[ti][:, 320:384], tk, out_ap, out_key)

        def transpose_to(src_bf, src_key, nblk, dst_fn, dst_key, evac_engs=("act", "dve")):
            gi, gk = r_G.next()
            gbf = G[gi][:].bitcast(BF16)
            for i in range(nblk):
                I("pe", "transpose", r=[src_key, "ident"], w=[gk], out=gbf[:, i * 128:(i + 1) * 128],
                  in_=src_bf[:, i * 128:(i + 1) * 128], identity=ident[:])
            evt[0] += 1
            for i in range(nblk):
                eng = evac_engs[evt[0] % len(evac_engs)]
                if eng == "act":
                    I("act", "activation", r=[gk], w=[dst_key], out=dst_fn(i), in_=gbf[:, i * 128:(i + 1) * 128], func=AF.Copy)
                else:
                    I("dve", "tensor_copy", r=[gk], w=[dst_key], out=dst_fn(i), in_=gbf[:, i * 128:(i + 1) * 128])

        def ret_k(t, ti, tk):
            gi, gk = proj_tok(t, O_RK, 512)
            ri, rkk = r_tok.next()
            rope(G[gi][:, :], [gk], True, 4, 128, tabt[ti][:, 0:128], tabt[ti][:, 128:256], tk, tok_bf[ri][:], rkk)
            return ri, rkk

        def g_ktile(t, tp, tab_row, kslot, own_n, other_j):
            ti, tk = tp, f"tabt{tp}"
            I("sp", "dma_start", w=[tk], dma=True, out=tabt[ti][:], in_=tab_d[tab_row:tab_row + 128, :])
            gi, gk = proj_tok(t, O_RK, 512)
            ri, rkk = tp * 2, f"tokbf{tp * 2}"
            rope(G[gi][:, :], [gk], True, 4, 128, tabt[ti][:, 0:128], tabt[ti][:, 128:256], tk, tok_bf[ri][:], rkk)
            yield
            gi, gk = proj_tok(t, O_RV, 512)
            g4 = v3(G[gi][:, :], 4)
            for d_ in range(2):
                I("dve", "tensor_tensor", r=[gk, "dec"], w=["vfb"], out=vfb[:, :, d_, :], in0=g4,
                  in1=bc(dec[:, d_, :], [128, 4, 128], 2), op=ALU.mult)
            sg_ = []
            for hp in range(2):
                g2, g2k = r_G.next()
                for hh in range(2):
                    h = hp * 2 + hh
                    I("pe", "matmul", r=[rkk, "vfb"], w=[g2k], out=G[g2][:, hh * 256:(hh + 1) * 256],
                      lhsT=tok_bf[ri][:, h * 128:(h + 1) * 128], rhs=vfb[:, h].rearrange("p f e -> p (f e)"), start=True, stop=True)
                sg_.append((g2, g2k))
            for hp, (g2, g2k) in enumerate(sg_):
                S4 = G[g2][:, :].rearrange("p (h f e) -> p h f e", h=2, f=2)
                hs = slice(hp * 2, hp * 2 + 2)
                if own_n is not None:
                    I("dve", "tensor_copy", r=["st_f"], w=[f"Rf{own_n}"], out=Rfb[:, own_n, hs, 0, :], in_=st_f[:, hs, :])
                    I("act", "activation", r=[g2k], w=[f"Rb{own_n}"], out=Rfb[:, own_n, hs, 1, :], in_=S4[:, :, 1, :], func=AF.Copy)
                else:
                    for hh in range(2):
                        h = hp * 2 + hh
                        I("dve", "scalar_tensor_tensor", r=[g2k, "wb", "st_b0"], w=["st_b0"], out=st_b[0][:, h, :], in0=S4[:, hh, 1, :],
                          scalar=wb[:, other_j, h:h + 1], in1=st_b[0][:, h, :], op0=ALU.mult, op1=ALU.add)
                for hh in range(2):
                    h = hp * 2 + hh
                    I("dve", "scalar_tensor_tensor", r=[g2k, "st_f", "dec"], w=["st_f"], out=st_f[:, h, :], in0=st_f[:, h, :],
                      scalar=dec[:, 4, h:h + 1], in1=S4[:, hh, 0, :], op0=ALU.mult, op1=ALU.add)
            yield
            gi, gk = proj_tok(t, O_AK, 256)
            ai, akk = tp * 2 + 1, f"tokbf{tp * 2 + 1}"
            qk_norm_rope(gi, gk, 2, kg[:], "kg", ti, tk, tok_bf[ai][:, 0:128], akk)
            I("act", "activation", r=[gk, "av_all"], w=[f"av{kslot}"], out=av_ext[:, kslot, 0:64], in_=G[gi][:, 128:192], func=AF.Copy)
            I("act", "activation", r=[gk, "av_all"], w=[f"av{kslot}"], out=av_ext[:, kslot, 128:192], in_=G[gi][:, 192:256], func=AF.Copy)
            yield
            transpose_to(tok_bf[ai], akk, 1, lambda i: akT[:, kslot * 128:(kslot + 1) * 128], f"akT{kslot}", evac_engs=("dve",))
            yield

        def g_FPhead(x_src, row0, ntile, b, par):
            nq = ntile * 128
            sgT = sgT2[par]
            P = f"_{par}"
            for _ in front_g(x_src, row0, ntile, b):
                yield
            for i in range(4):
                gi, gk = r_G.next()
                for k in range(KC):
                    I("pe", "matmul", r=[f"hT{hTc[0]}_{t_}_{k}" for t_ in range(ntile)] + ["w_in"], w=[gk], out=G[gi][:, 0:nq], lhsT=w_in[:, k, O_AG + i * 128:O_AG + (i + 1) * 128],
                      rhs=hT2[hTc[0]][:, k, 0:nq], start=(k == 0), stop=(k == KC - 1))
                a, ak_ = r_t3.next()
                I("act", "activation", r=[gk], w=[ak_], out=t3[a][:, 0:nq], in_=G[gi][:, 0:nq], func=AF.Exp, scale=-1.0)
                I("act", "activation", r=[ak_, "one"], w=[ak_], out=t3[a][:, 0:nq], in_=t3[a][:, 0:nq], func=AF.Ln, bias=one_t[:])
                I("act", "activation", r=[ak_], w=[ak_], out=t3[a][:, 0:nq], in_=t3[a][:, 0:nq], func=AF.Exp, scale=-1.0)
                I("dve", "tensor_tensor", r=[gk, ak_], w=[f"sgT{i}" + P], out=sgT[:, i, 0:nq], in0=G[gi][:, 0:nq], in1=t3[a][:, 0:nq],
                  op=ALU.mult)
                yield

        def g_FPtile(t, tp, tab_row, n, par):
            aqT, mixT_r = aqT2[par], mixT_r2[par]
            P = f"_{par}"
            T = f"_t{tp}"
            rqT, rkT, v_bf, pm = rqT2[tp], rkT2[tp], v_bf2[tp], pm2[tp]
            sgb, oab = sgbuf[tp], oabuf[tp]
            ts_ = slice(t * 128, (t + 1) * 128)
            ti, tk = tp, f"tabt{tp}"
            I("sp", "dma_start", w=[tk], dma=True, out=tabt[ti][:], in_=tab_d[tab_row:tab_row + 128, :])
            tk0, tk1 = f"tokbf{tp * 2}", f"tokbf{tp * 2 + 1}"
            tb0, tb1 = tok_bf[tp * 2], tok_bf[tp * 2 + 1]
            gi, gk = proj_tok(t, O_RQ, 512)
            rope(G[gi][:, :], [gk], True, 4, 128, tabt[ti][:, 0:128], tabt[ti][:, 128:256], tk, tb0[:], tk0)
            yield
            transpose_to(tb0, tk0, 4, lambda i: rqT[:, i, :], "rqT" + T)
            gi, gk = proj_tok(t, O_RK, 512)
            rope(G[gi][:, :], [gk], True, 4, 128, tabt[ti][:, 0:128], tabt[ti][:, 128:256], tk, tb1[:], tk1)
            yield
            transpose_to(tb1, tk1, 4, lambda i: rkT[:, i, :], "rkT" + T)
            gi, gk = proj_tok(t, O_RV, 512)
            I("act", "activation", r=[gk], w=["v_bf" + T], out=v_bf[:].rearrange("p h e -> p (h e)"), in_=G[gi][:, :], func=AF.Copy)
            gi, gk = proj_tok(t, O_AQ, 512)
            qk_norm_rope(gi, gk, 8, qg[:], "qg", ti, tk, tb0[:], tk0)
            yield
            transpose_to(tb0, tk0, 4, lambda i: aqT[:, i, ts_], "aqT" + P)
            gi, gk = proj_tok(t, O_RG, 512)
            a, ak_ = r_t3.next()
            sgk = "sgbuf" + T
            I("act", "activation", r=[gk], w=[ak_], out=t3[a][:], in_=G[gi][:, :], func=AF.Exp, scale=-1.0)
            I("act", "activation", r=[ak_, "one"], w=[ak_], out=t3[a][:], in_=t3[a][:], func=AF.Ln, bias=one_t[:])
            I("act", "activation", r=[ak_], w=[ak_], out=t3[a][:], in_=t3[a][:], func=AF.Exp, scale=-1.0)
            I("dve", "tensor_tensor", r=[gk, ak_], w=[sgk], out=sgb[:], in0=G[gi][:, :], in1=t3[a][:], op=ALU.mult)
            I("dve", "tensor_tensor", r=[sgk, "gng"], w=[sgk], out=sgb[:], in0=sgb[:], in1=gng[:], op=ALU.mult)
            gi, gk = r_G.next()
            for h in range(4):
                I("pe", "matmul", r=["rkT" + T, "rqT" + T], w=[gk], out=G[gi][:, h * 128:(h + 1) * 128], lhsT=rkT[:, h, :], rhs=rqT[:, h, :],
                  start=True, stop=True)
            I("dve", "tensor_tensor", r=[gk, "mask"], w=["pm" + T], out=pm[:].rearrange("p h c -> p (h c)"), in0=G[gi][:, :],
              in1=mask[:].rearrange("p h c -> p (h c)"), op=ALU.mult)
            yield
            oak = "oabuf" + T
            crs = []
            for hp in range(2):
                g2, g2k = r_G.next()
                for hh in range(2):
                    h = hp * 2 + hh
                    I("pe", "matmul", r=["rqT" + T, f"Rf{n}", f"Rb{n}"], w=[g2k], out=G[g2][:, hh * 256:(hh + 1) * 256],
                      lhsT=rqT[:, h, :], rhs=Rfb[:, n, h].rearrange("p f e -> p (f e)"), start=True, stop=True)
                crs.append((g2, g2k))
            gi, gk = r_G.next()
            for h in range(4):
                I("pe", "matmul", r=["pm" + T, "v_bf" + T], w=[gk], out=G[gi][:, h * 128:(h + 1) * 128], lhsT=pm[:, h, :], rhs=v_bf[:, h, :],
                  start=True, stop=True)
            u, uk = r_t3.next()
            for hp, (g2, g2k) in enumerate(crs):
                C4 = G[g2][:, :].rearrange("p (h f e) -> p h f e", h=2, f=2)
                hs = slice(hp * 2, hp * 2 + 2)
                o3 = v3(oab[:, hp * 256:(hp + 1) * 256], 2)
                u3 = v3(t3[u][:, hp * 256:(hp + 1) * 256], 2)
                I("dve", "tensor_tensor", r=[g2k, "dec"], w=[oak], out=o3, in0=C4[:, :, 0, :],
                  in1=bc(dec[:, 2, hs], [128, 2, 128], 2), op=ALU.mult)
                I("dve", "tensor_tensor", r=[g2k, "dec"], w=[uk], out=u3, in0=C4[:, :, 1, :],
                  in1=bc(dec[:, 3, hs], [128, 2, 128], 2), op=ALU.mult)
            I("dve", "tensor_tensor", r=[oak, uk], w=[oak], out=oab[:], in0=oab[:], in1=t3[u][:], op=ALU.add)
            I("dve", "tensor_tensor", r=[gk, oak], w=[oak], out=oab[:], in0=G[gi][:, :], in1=oab[:], op=ALU.add)
            sq, sqk = r_t3.next()
            st_i, stk = r_stat.next()
            I("dve", "tensor_tensor", r=[oak], w=[sqk], out=t3[sq][:], in0=oab[:], in1=oab[:], op=ALU.mult)
            I("dve", "tensor_reduce", r=[sqk], w=[stk], out=stat[st_i][:, 4:8], in_=v3(t3[sq][:], 4), axis=AX.X, op=ALU.add)
            I("act", "activation", r=[stk, "eps"], w=[stk], out=stat[st_i][:, 4:8], in_=stat[st_i][:, 4:8], func=AF.Ln,
              scale=1.0 / 128, bias=eps_t[:])
            I("act", "activation", r=[stk], w=[stk], out=stat[st_i][:, 4:8], in_=stat[st_i][:, 4:8], func=AF.Exp, scale=-0.5)
            I("dve", "tensor_tensor", r=[oak, stk], w=[oak], out=v3(oab[:], 4), in0=v3(oab[:], 4),
              in1=bc(stat[st_i][:, 4:8], [128, 4, 128], 2), op=ALU.mult)
            I("dve", "tensor_tensor", r=[oak, sgk], w=[tk1], out=tb1[:], in0=oab[:], in1=sgb[:], op=ALU.mult)
            yield
            transpose_to(tb1, tk1, 4, lambda i: mixT_r[:, i, ts_], "mixT_r" + P)
            yield

        def g_AT(ntile, nkt, par):
            LA = N_ST - 1
            nq = ntile * 128
            aqT, sgT = aqT2[par], sgT2[par]
            P = f"_{par}"
            its = [(g, ip, j) for g in range(2) for ip in range(2) for j in range(nkt)]

            def qk(it):
                g, ip, j = it
                rows = slice(g * 64, (g + 1) * 64)
                si, sk = r_sT.next()
                I("pe", "matmul", r=[f"akT{j}", "aqT" + P], w=[sk], out=sT[si][:, :, 0:nq], lhsT=akT[rows, j * 128:(j + 1) * 128],
                  rhs=aqT[rows, 2 * ip:2 * ip + 2, 0:nq], start=True, stop=True)
                pi, pk = r_pT.next()
                I("act", "activation", r=[sk], w=[pk], out=pT[pi][:, :, 0:nq], in_=sT[si][:, :, 0:nq], func=AF.Exp, scale=0.125)
                return pi, pk

            pend = [qk(its[k]) for k in range(min(LA, len(its)))]
            for n_it, (g, ip, j) in enumerate(its):
                pi, pk = pend.pop(0)
                if n_it + LA < len(its):
                    pend.append(qk(its[n_it + LA]))
                rows = slice(g * 64, (g + 1) * 64)
                orow = slice((1 - g) * 64, (2 - g) * 64)
                ext = slice(g * 64, g * 64 + 128)
                o3 = oT[:, :].rearrange("p (a q) -> p a q", a=2)
                I("pe", "matmul", r=[pk, f"av{j}"], w=["oT"], out=o3[:, :, 0:nq], lhsT=av_ext[:, j, ext], rhs=pT[pi][:, :, 0:nq],
                  start=(j == 0), stop=(j == nkt - 1))
                yield
                if j == nkt - 1:
                    a, ak_ = r_t1.next()
                    r3 = t1[a][:, :].rearrange("p (a q) -> p a q", a=2)
                    I("dve", "reciprocal", r=["oT"], w=[ak_], out=r3[orow, :, 0:nq], in_=o3[orow, :, 0:nq])
                    I("dve", "tensor_tensor", r=["oT", ak_], w=[ak_], out=r3[rows, :, 0:nq], in0=o3[rows, :, 0:nq],
                      in1=r3[orow, :, 0:nq], op=ALU.mult)
                    I("dve", "tensor_tensor", r=[ak_, f"sgT{2 * ip}" + P, f"sgT{2 * ip + 1}" + P], w=["mixT_a"],
                      out=mixT_a[rows, 2 * ip:2 * ip + 2, 0:nq], in0=r3[rows, :, 0:nq], in1=sgT[rows, 2 * ip:2 * ip + 2, 0:nq], op=ALU.mult)
                    yield

        def g_O(x_src, row0, ntile, y_dst, par):
            mixT_r = mixT_r2[par]
            P = f"_{par}"
            for t in range(ntile):
                ts_ = slice(t * 128, (t + 1) * 128)
                xi, xk = r_xo.next()
                I("sp", "dma_start", w=[xk], dma=True, out=xo[xi][:], in_=x_src[row0 + t * 128:row0 + (t + 1) * 128, :])
                for half in range(2):
                    gi, gk = r_G.next()
                    cs = slice(half * 512, (half + 1) * 512)
                    for k in range(KC):
                        lhs = mixT_r[:, k, ts_] if k < 4 else mixT_a[:, k - 4, ts_]
                        I("pe", "matmul", r=["mixT_r" + P, "mixT_a", "w_out"], w=[gk], out=G[gi][:, :], lhsT=lhs, rhs=w_out[:, k, cs],
                          start=(k == 0), stop=(k == KC - 1))
                    a, ak_ = r_t1.next()
                    I("dve", "tensor_tensor", r=[gk, "gate_rep"], w=[ak_], out=t1[a][:], in0=G[gi][:, :], in1=gate_rep[:, cs],
                      op=ALU.mult)
                    I("dve", "tensor_tensor", r=[ak_, xk], w=[xk], out=xo[xi][:, cs], in0=xo[xi][:, cs], in1=t1[a][:],
                      op=ALU.add)
                skey = f"YOUT{len(store_keys)}"
                store_keys.append(skey)
                I("sp", "dma_start", r=[xk], w=[skey], dma=True, out=y_dst[row0 + t * 128:row0 + (t + 1) * 128, :], in_=xo[xi][:])
                yield

        def record(gen):
            REC[0] = [[]]
            for _ in gen:
                REC[0].append([])
            chunks = [c for c in REC[0] if c]
            REC[0] = None
            return chunks

        def replay(chunks):
            for c in chunks:
                for (eng, fn, r, w, dma) in c:
                    S.add(eng, fn, reads=r, writes=w, dma=dma)

        def split_pe(chunks):
            out = []
            for c in chunks:
                cur, cur_pe = [], None
                for op in c:
                    is_pe = op[0] == "pe"
                    if cur and is_pe and not cur_pe:
                        out.append(cur)
                        cur = []
                    if cur and (not is_pe) and cur_pe:
                        out.append(cur)
                        cur = []
                    cur.append(op)
                    cur_pe = is_pe
                if cur:
                    out.append(cur)
            return out

        def zip_chunks(a, b):
            out = []
            for i in range(max(len(a), len(b))):
                if i < len(a):
                    out.append(a[i])
                if i < len(b):
                    out.append(b[i])
            return out

        def chunk_cost(c):
            t = 0.0
            prev = None
            for (eng, (m, kw), r, w, dma) in c:
                if eng == "pe":
                    t += 0.05
                    continue
                n = 1
                o = kw.get("out", kw.get("ap"))
                if o is not None:
                    for d_ in o.shape[1:]:
                        n *= d_
                if dma:
                    t += 2.0
                elif eng == "act":
                    t += 0.25 + n / 1300.0
                elif eng == "dve":
                    t += 0.12 + n / 900.0
                else:
                    t += 0.15 + n / 480.0
                if prev is not None and prev != eng:
                    t += 0.3
                prev = eng
            return t

        def merge_weighted(main, other):
            if not other:
                return list(main)
            wts = [chunk_cost(c) for c in other]
            tot = sum(wts) or 1.0
            out = []
            mi = 0
            cum = 0.0
            for c, wv in zip(other, wts):
                out.append(c)
                cum += wv
                tgt = int(round(cum / tot * len(main)))
                while mi < tgt and mi < len(main):
                    out.append(main[mi])
                    mi += 1
            out.extend(main[mi:])
            return out

        def merge_even(main, other):
            if not other:
                return list(main)
            out = []
            acc = 0.0
            oi = 0
            ratio = len(other) / float(len(main))
            for c in main:
                out.append(c)
                acc += ratio
                while acc >= 1.0 and oi < len(other):
                    out.append(other[oi])
                    oi += 1
                    acc -= 1.0
            out.extend(other[oi:])
            return out

        jobs = []
        for j in range(NS):
            jobs.append(dict(x=xs_d, row0=j * SS, tab0=0, n_own=SS // 128, n_oth=0, b=j, y=ys_d))
        jobs.append(dict(x=xp_d, row0=0, tab0=SS, n_own=SPO // 128, n_oth=SPX // 128, b=NS, y=yp_d))
        for jb in jobs:
            n_own, n_oth = jb["n_own"], jb["n_oth"]
            for b0 in range(0, n_oth, BT):
                xseq.append((jb["x"], jb["row0"] + (n_own + b0) * 128))
            for b0 in range(0, n_own, BT):
                xseq.append((jb["x"], jb["row0"] + b0 * 128))
            for b0 in range(0, n_own, BT):
                xseq.append((jb["x"], jb["row0"] + b0 * 128))
        for t in range(BT):
            I("sp", "dma_start", w=[f"xbuf{t}"], dma=True, out=xbuf[t][:], in_=xseq[0][0][xseq[0][1] + t * 128:xseq[0][1] + (t + 1) * 128, :])
        for jb in (jobs if _STAGE >= 1 else []):
            b = jb["b"]
            n_own, n_oth = jb["n_own"], jb["n_oth"]
            nkt = n_own + n_oth
            assert nkt % 4 == 0 and n_own % BT == 0 and n_oth % BT == 0 and BT == 2
            I("sp", "dma_start", r=["mod_d"], w=["gate_rep"], dma=True, out=gate_rep[:],
              in_=mod_d[b:b + 1, 2 * D:3 * D].partition_broadcast(128))
            I("pool", "memset", w=["st_f"], ap=st_f[:], constant=0.0)
            I("pool", "memset", w=["st_b0"], ap=st_b[0][:], constant=0.0)
            order = [("oth", i) for i in range(n_oth)] + [("own", i) for i in range(n_own)]
            p1 = []
            for b0 in range(0, len(order), BT):
                blk = order[b0:b0 + BT]
                kind, first = blk[0]
                off = (n_own * 128 if kind == "oth" else 0) + first * 128
                p1.append((blk, kind, first, off))
            r_G.n = NG1
            hTc[0] = gcount[0] % 2
            gcount[0] += 1
            front(jb["x"], jb["row0"] + p1[0][3], BT, b)
            for k, (blk, kind, first, off) in enumerate(p1):
                if kind == "own" and first == 0 and n_oth > 0:
                    I("pool", "tensor_scalar_mul", r=["st_f", "cst"], w=["st_f"], out=st_f[:], in0=st_f[:], scalar1=flags[:, 0:1])
                    I("pool", "tensor_scalar_mul", r=["st_b0", "cst"], w=["st_b0"], out=st_b[0][:], in0=st_b[0][:], scalar1=flags[:, 1:2])
                tiles = []
                for t, (kd, i) in enumerate(blk):
                    if kd == "own":
                        tiles.append(record(g_ktile(t, t, jb["tab0"] + off + t * 128, i, i, None)))
                    else:
                        tiles.append(record(g_ktile(t, t, jb["tab0"] + off + t * 128, n_own + i, None, i)))
                nxt = []
                if k + 1 < len(p1):
                    hTc[0] = gcount[0] % 2
                    gcount[0] += 1
                    nxt = record(front_g(jb["x"], jb["row0"] + p1[k + 1][3], BT, b))
                kz = zip_chunks(tiles[0], tiles[1])
                mg = []
                for ci, c in enumerate(kz):
                    mg.append(c)
                    if ci < len(nxt):
                        mg.append(nxt[ci])
                mg.extend(nxt[len(kz):])
                replay(mg)
            if _STAGE < 2:
                break
            cur = 0
            for n in range(n_own - 1, -1, -1):
                nxt = 1 - cur
                for h in range(4):
                    I("dve", "scalar_tensor_tensor", r=[f"st_b{cur}", "dec", f"Rb{n}"], w=[f"st_b{nxt}"], out=st_b[nxt][:, h, :],
                      in0=st_b[cur][:, h, :], scalar=dec[:, 5, h:h + 1], in1=Rfb[:, n, h, 1, :], op0=ALU.mult, op1=ALU.add)
                I("dve", "tensor_copy", r=[f"st_b{cur}"], w=[f"Rb{n}"], out=Rfb[:, n, :, 1, :], in_=st_b[cur][:])
                cur = nxt
            blocks = list(range(0, n_own, BT))
            r_G.n = NG2
            r_G.i = -1

            def rec_fp(bi):
                b0 = blocks[bi]
                par = bi % 2
                hTc[0] = gcount[0] % 2
                gcount[0] += 1
                head = record(g_FPhead(jb["x"], jb["row0"] + b0 * 128, BT, b, par))
                tl = [record(g_FPtile(t, t, jb["tab0"] + (b0 + t) * 128, b0 + t, par)) for t in range(BT)]
                return head + zip_chunks(tl[0], tl[1])

            replay(rec_fp(0))
            o_prev = []
            for bi, b0 in enumerate(blocks):
                at = record(g_AT(BT, nkt, bi % 2))
                fp = rec_fp(bi + 1) if bi + 1 < len(blocks) else []
                nhead = min(len(o_prev), nkt - 1)
                headc = []
                for k in range(nhead):
                    headc.append(at[k])
                    headc.append(o_prev[k])
                headc.extend(o_prev[nhead:])
                replay(headc)
                replay(merge_weighted(at[nhead:], split_pe(fp)))
                o_prev = record(g_O(jb["x"], jb["row0"] + b0 * 128, BT, jb["y"], bi % 2))
            replay(o_prev)
        I("sp", "wait_only", r=store_keys)
        S.emit(nc, ctx)
        with nc.Block() as block:
            @block.sync
            def _(e):
                S.run_stream("sp", e)

            @block.scalar
            def _(e):
                S.run_stream("act", e)

            @block.vector
            def _(e):
                S.run_stream("dve", e)

            @block.gpsimd
            def _(e):
                S.run_stream("pool", e)

            @block.tensor
            def _(e):
                S.run_stream("pe", e)
    return nc


def _rope_tab(pos):
    pos = np.asarray(pos)
    row = (pos // 64).astype(np.float32)
    col = (pos % 64).astype(np.float32)
    out = np.zeros((len(pos), 384), np.float32)

    def fill(hd, c_off, s_off):
        half = hd // 2
        freqs = (np.float32(10000.0) ** (-np.arange(0, half, 2, dtype=np.float32) / np.float32(half))).astype(np.float32)
        ar = row[:, None] * freqs[None, :]
        ac = col[:, None] * freqs[None, :]
        q = half // 2
        out[:, c_off:c_off + hd] = np.concatenate([np.cos(ar), np.cos(ar), np.cos(ac), np.cos(ac)], 1)
        out[:, s_off:s_off + hd] = np.concatenate([-np.sin(ar), np.sin(ar), -np.sin(ac), np.sin(ac)], 1)
    fill(128, 0, 128)
    fill(64, 256, 320)
    return out


def _consts(flag_f):
    p = np.arange(128, dtype=np.float32)
    cst = np.zeros((128, 8 + 128 + 128 + 64 + 2), np.float32)
    cst[:, 0] = 127 - p
    cst[:, 1] = p
    cst[:, 2] = p + 1
    cst[:, 3] = 128 - p
    cst[:, 4] = 128
    c = p[None, :]
    m = p[:, None]
    cst[:, 8:136] = np.maximum(c - m, 0)
    cst[:, 136:264] = np.maximum(m - c, 0)
    cst[:, 264:328] = np.repeat(128.0 * np.arange(16, dtype=np.float32), 4)[None, :]
    cst[:, 328] = flag_f
    cst[:, 329] = 1.0 - flag_f
    return cst


_PAIR = np.array([0, 4, 1, 5, 2, 6, 3, 7])


def _perm_cols():
    idx = np.arange(INC)
    for off in (O_AQ, O_AG):
        blk = idx[off:off + 512].reshape(8, 64)[_PAIR].reshape(-1)
        idx[off:off + 512] = blk
    return idx


def run(inputs, n_cores, NS, SS, SP, trace=False):
    f32 = lambda a: np.ascontiguousarray(np.asarray(a, dtype=np.float32))
    x_prompt, x_sample = f32(inputs["x_prompt"]), f32(inputs["x_sample"])
    c_prompt, c_sample = f32(inputs["c_prompt"]), f32(inputs["c_sample"])
    half = SP // 2
    cols = _perm_cols()
    w_in_p = f32(inputs["w_in"][0][:, cols])
    rows = np.arange(D)
    rows[512:] = 512 + np.arange(512).reshape(8, 64)[_PAIR].reshape(-1)
    w_out_p = f32(inputs["w_out"][0][rows, :])
    ngT = f32(inputs["norm_g"][0].reshape(KC, 128).T)
    common = {
        "w_ada": f32(inputs["w_ada"][0]), "b_ada": f32(inputs["b_ada"][0][None, :]),
        "w_in": w_in_p, "w_out": w_out_p, "ngT": ngT,
        "lrf": f32(inputs["ret_log_rate_fwd"][0][None, :]), "lrb": f32(inputs["ret_log_rate_bwd"][0][None, :]),
        "gng": f32(inputs["ret_gn_g"][0].reshape(1, 512)), "qg": f32(inputs["q_norm_g"][0][None, :]),
        "kg": f32(inputs["k_norm_g"][0][None, :]),
    }
    in_maps = []
    for c in range(n_cores):
        pj, hf = c // 2, c % 2
        own = np.arange(hf * half, (hf + 1) * half)
        oth = np.arange((1 - hf) * half, (2 - hf) * half)
        xp = np.concatenate([x_prompt[pj, own], x_prompt[pj, oth]], 0)
        tab = np.concatenate([_rope_tab(np.arange(SS)), _rope_tab(own), _rope_tab(oth)], 0)
        cb = np.concatenate([c_sample[c * NS:(c + 1) * NS], c_prompt[pj:pj + 1]], 0)
        cT = cb.T.reshape(KC, 128, NS + 1).transpose(1, 0, 2).reshape(128, KC * (NS + 1))
        m = dict(common)
        m.update({"xs": f32(x_sample[c * NS:(c + 1) * NS].reshape(NS * SS, D)), "xp": f32(xp), "tab": f32(tab),
                  "cT": f32(cT), "cst": _consts(float(hf))})
        in_maps.append(m)
    nc = build_nc(NS, SS, half, half)
    res = run_bass_kernel_spmd(nc, in_maps, core_ids=list(range(n_cores)), trace=trace)
    y_s = np.concatenate([r["ys"].reshape(NS, SS, D) for r in res.results], 0)
    y_p = np.zeros((n_cores // 2, SP, D), np.float32)
    for c in range(n_cores):
        y_p[c // 2, (c % 2) * half:(c % 2 + 1) * half] = res.results[c]["yp"]
    return (y_p, y_s), res


def kernel(x_prompt, x_sample, c_prompt, c_sample, norm_g, w_ada, b_ada, w_in, ret_log_rate_fwd, ret_log_rate_bwd,
           ret_gn_g, q_norm_g, k_norm_g, w_out):
    inputs = dict(x_prompt=x_prompt, x_sample=x_sample, c_prompt=c_prompt, c_sample=c_sample, norm_g=norm_g, w_ada=w_ada,
                  b_ada=b_ada, w_in=w_in, ret_log_rate_fwd=ret_log_rate_fwd, ret_log_rate_bwd=ret_log_rate_bwd,
                  ret_gn_g=ret_gn_g, q_norm_g=q_norm_g, k_norm_g=k_norm_g, w_out=w_out)
    inputs = {k: np.asarray(v) for k, v in inputs.items()}
    (y_p, y_s), _ = run(inputs, 8, 4, 2048, 4096)
    return (y_p.astype(np.float32), y_s.astype(np.float32))
```

```python
import contextlib
import math
import numpy as np
import concourse.bass as bass
import concourse.mybir as mybir
from concourse.bass_utils import run_bass_kernel_spmd

F32 = mybir.dt.float32
BF16 = mybir.dt.bfloat16
AF = mybir.ActivationFunctionType
ALU = mybir.AluOpType
AX = mybir.AxisListType

D = 1024
KC = 8
INC = 3328
EPS = 1e-6
O_RQ, O_RK, O_RV, O_RG, O_AQ, O_AK, O_AV, O_AG = 0, 512, 1024, 1536, 2048, 2560, 2688, 2816

EPOCH = 24000
NDMASEM = 32


class _Op:
    __slots__ = ("eng", "fn", "deps", "idx", "dma", "signal", "sem", "val", "waits", "slot")


class Sched:
    ENGS = ("pe", "act", "dve", "pool", "sp")

    def __init__(self):
        self.ops = []
        self.last_w = {}
        self.readers = {}
        self.ndma = 0
        self.slot_last = {}

    def add(self, eng, fn, reads=(), writes=(), dma=False):
        op = _Op()
        op.eng, op.fn, op.dma = eng, fn, dma
        op.idx = len(self.ops)
        op.signal = False
        deps = {}
        for k in reads:
            w = self.last_w.get(k)
            if w is not None:
                deps[w] = True
        for k in writes:
            w = self.last_w.get(k)
            if w is not None and w not in deps:
                deps[w] = False
            for r in self.readers.get(k, ()):
                if r not in deps:
                    deps[r] = False
        if dma:
            slot = self.ndma % NDMASEM
            self.ndma += 1
            op.slot = slot
            prev = self.slot_last.get(slot)
            if prev is not None and prev not in deps:
                deps[prev] = True
            self.slot_last[slot] = op.idx
        op.deps = deps
        for k in writes:
            self.last_w[k] = op.idx
            self.readers[k] = []
        wset = set(writes)
        for k in reads:
            if k not in wset:
                lst = self.readers.setdefault(k, [])
                if not dma:
                    lst[:] = [r for r in lst if self.ops[r].dma or self.ops[r].eng != eng]
                lst.append(op.idx)
        self.ops.append(op)
        return op.idx

    def _resolve(self):
        ops = self.ops
        waited = {e: {e2: -1 for e2 in self.ENGS} for e in self.ENGS}
        dma_seen = {e: set() for e in self.ENGS}
        for x in ops:
            need = {}
            x.waits = []
            for p_idx, raw in x.deps.items():
                p = ops[p_idx]
                if p.dma:
                    if p_idx not in dma_seen[x.eng]:
                        dma_seen[x.eng].add(p_idx)
                        x.waits.append(p_idx)
                    continue
                if p.eng == x.eng and not raw and not x.dma and x.eng == "pe":
                    continue
                if waited[x.eng][p.eng] >= p_idx:
                    continue
                if p.eng not in need or need[p.eng] < p_idx:
                    need[p.eng] = p_idx
            for e2, p_idx in need.items():
                waited[x.eng][e2] = p_idx
                ops[p_idx].signal = True
                x.waits.append(p_idx)

    def emit(self, nc, ctx):
        self._resolve()
        ops = self.ops
        counts = {e: 0 for e in self.ENGS}
        eng_sems = {}
        ndma = 0
        dma_sems = [ctx.enter_context(nc.semaphore(f"dq{i}")) for i in range(NDMASEM)]
        dma_cnt = [0] * NDMASEM
        for x in ops:
            if x.dma:
                s = x.slot
                dma_cnt[s] += 16
                x.sem, x.val = dma_sems[s], dma_cnt[s]
            elif x.signal:
                c = counts[x.eng]
                key = (x.eng, c // EPOCH)
                if key not in eng_sems:
                    eng_sems[key] = ctx.enter_context(nc.semaphore(f"s_{x.eng}_{key[1]}"))
                x.sem, x.val = eng_sems[key], c % EPOCH + 1
                counts[x.eng] = c + 1
        self.streams = {e: [] for e in self.ENGS}
        for x in ops:
            self.streams[x.eng].append(x)

    def run_stream(self, eng_name, eng):
        ops = self.ops
        for x in self.streams[eng_name]:
            for p_idx in x.waits:
                p = ops[p_idx]
                eng.wait_ge(p.sem, p.val)
            m, kw = x.fn
            if m == "wait_only":
                continue
            ins = getattr(eng, m)(**kw)
            if x.dma:
                ins.then_inc(x.sem, 16)
            elif x.signal:
                ins.then_inc(x.sem, 1)


class Rot:
    def __init__(self, name, n):
        self.name, self.n, self.i = name, n, -1

    def next(self):
        self.i = (self.i + 1) % self.n
        return self.i, f"{self.name}{self.i}"


_STAGE = 99
NG1 = 8
NG2 = 3
N_ST = 4
BT = 2
NQ = BT * 128


def build_nc(NS, SS, SPO, SPX):
    NB = NS + 1
    nc = bass.Bass("TRN2", target_bir_lowering=False)
    dt_in = lambda name, shape: nc.dram_tensor(name, shape, F32, kind="ExternalInput").ap()
    xs_d = dt_in("xs", [NS * SS, D])
    xp_d = dt_in("xp", [SPO + SPX, D])
    tab_d = dt_in("tab", [SS + SPO + SPX, 384])
    cT_d = dt_in("cT", [128, KC * NB])
    wada_d = dt_in("w_ada", [D, 3 * D])
    bada_d = dt_in("b_ada", [1, 3 * D])
    win_d = dt_in("w_in", [D, INC])
    wout_d = dt_in("w_out", [D, D])
    ngT_d = dt_in("ngT", [128, KC])
    lrf_d = dt_in("lrf", [1, 4])
    lrb_d = dt_in("lrb", [1, 4])
    gng_d = dt_in("gng", [1, 512])
    qg_d = dt_in("qg", [1, 64])
    kg_d = dt_in("kg", [1, 64])
    NCST = 8 + 128 + 128 + 64 + 2
    cst_d = dt_in("cst", [128, NCST])
    ys_d = nc.dram_tensor("ys", [NS * SS, D], F32, kind="ExternalOutput").ap()
    yp_d = nc.dram_tensor("yp", [SPO, D], F32, kind="ExternalOutput").ap()
    mod_d = nc.dram_tensor("mod_scratch", [NB, 3 * D], F32, kind="Internal").ap()

    NT_OWN = max(SS, SPO) // 128
    NT_K = max(SS, SPO + SPX) // 128
    S = Sched()

    with contextlib.ExitStack() as ctx:
        sb_bytes = [0]

        def sb(name, shape, dt=F32):
            n = 1
            for d_ in shape[1:]:
                n *= d_
            sb_bytes[0] += ((n * (2 if dt == BF16 else 4) + 31) // 32) * 32
            return ctx.enter_context(nc.sbuf_tensor("s_" + name, shape, dt))

        def ps(name, shape, dt=F32):
            return ctx.enter_context(nc.psum_tensor("p_" + name, shape, dt))

        w_in = sb("w_in_sb", [128, KC, INC], BF16)
        w_out = sb("w_out_sb", [128, KC, D], BF16)
        akT = sb("akT", [128, NT_K * 128], BF16)
        av_ext = sb("av_ext", [128, NT_K, 192], BF16)
        Rfb = sb("Rfb", [128, NT_OWN, 4, 2, 128], BF16)
        ident = sb("ident", [128, 128], BF16)
        identf = sb("identf", [128, 16])
        cst = sb("cst", [128, NCST])
        posc = cst[:, 0:8]
        DPm = cst[:, 8:136]
        DNm = cst[:, 136:264]
        jtab = cst[:, 264:328]
        flags = cst[:, 328:330]
        eps_t = sb("eps_t", [128, 1])
        one_t = sb("one_t", [128, 1])
        lns_t = sb("lns_t", [128, 1])
        lg = sb("lg", [128, 8])
        dec = sb("dec", [128, 6, 4])
        wb = sb("wb", [128, 16, 4])
        mask = sb("mask", [128, 4, 128], BF16)
        gng = sb("gng", [128, 512])
        qg = sb("qg", [128, 64])
        kg = sb("kg", [128, 64])
        ngT = sb("ngT", [128, KC])
        AB = sb("AB", [128, 2, KC, NB])
        cT = sb("cT", [128, KC, NB])
        gate_rep = sb("gate_rep", [128, D])

        xbuf = [sb(f"xbuf{i}", [128, D]) for i in range(2)]
        xs_bf = [sb(f"xsbf{i}", [128, D], BF16) for i in range(1)]
        stat = [sb(f"stat{i}", [128, 16]) for i in range(4)]
        hT2 = [sb(f"hT{i}", [128, KC, NQ], BF16) for i in range(2)]
        hTc = [0]
        tabt = [sb(f"tabt{i}", [128, 384]) for i in range(2)]
        t1 = [sb(f"t1_{i}", [128, 512]) for i in range(2)]
        t2 = [sb(f"t2_{i}", [128, 512]) for i in range(2)]
        t3 = [sb(f"t3_{i}", [128, 512]) for i in range(2)]
        tok_bf = [sb(f"tokbf{i}", [128, 512], BF16) for i in range(4)]
        rqT2 = [sb(f"rqT{i}", [128, 4, 128], BF16) for i in range(2)]
        rkT2 = [sb(f"rkT_t{i}", [128, 4, 128], BF16) for i in range(2)]
        v_bf2 = [sb(f"v_bf{i}", [128, 4, 128], BF16) for i in range(2)]
        sgbuf = [sb(f"sgbuf{i}", [128, 512]) for i in range(2)]
        oabuf = [sb(f"oabuf{i}", [128, 512]) for i in range(2)]
        st_f = oabuf[0][:].rearrange("p (h e) -> p h e", h=4)
        vfb = oabuf[1][:].bitcast(BF16).rearrange("p (h f e) -> p h f e", h=4, f=2)
        st_b = [sgbuf[i][:].rearrange("p (h e) -> p h e", h=4) for i in range(2)]
        xo = [sb(f"xo{i}", [128, D]) for i in range(1)]
        aqT2 = [sb(f"aqT{i}", [128, 4, NQ], BF16) for i in range(2)]
        sgT2 = [sb(f"sgT{i}", [128, 4, NQ], BF16) for i in range(2)]
        pm2 = [sb(f"pm{i}", [128, 4, 128], BF16) for i in range(2)]
        mixT_r2 = [sb(f"mixT_r{i}", [128, 4, NQ], BF16) for i in range(2)]
        mixT_a = sb("mixT_a", [128, 4, NQ], BF16)
        pT = [sb(f"pT{i}", [128, 2, NQ], BF16) for i in range(4)]
        rec = sb("rec", [128, NQ])

        if _STAGE != 99:
            print("SBUF bytes/partition:", sb_bytes[0])
        sT = [ps(f"sT{i}", [128, 2, NQ]) for i in range(4)]
        oT = ps("oT", [128, 512])
        G = [ps(f"G{i}", [128, 512]) for i in range(3)]
        flat = lambda t: t[:].rearrange("p a q -> p (a q)")
        G = [g[:, :] for g in G] + [flat(sT[3]), flat(sT[0]), flat(sT[1]), flat(sT[2]), oT[:, :]]
        GKEYS = ["G0", "G1", "G2", "sT3", "sT0", "sT1", "sT2", "oT"]

        r_x = Rot("xbuf", 2)
        r_xs = Rot("xsbf", 1)
        r_stat = Rot("stat", 4)
        r_tab = Rot("tabt", 2)
        r_t1 = Rot("t1_", 2)
        r_t2 = Rot("t2_", 2)
        r_t3 = Rot("t3_", 2)
        r_tok = Rot("tokbf", 4)
        class RotG:
            def __init__(self):
                self.n, self.i = 3, -1

            def next(self):
                self.i = (self.i + 1) % self.n
                return self.i, GKEYS[self.i]

        r_G = RotG()
        r_sT = Rot("sT", N_ST)
        r_pT = Rot("pT", N_ST)
        store_keys = []

        PSK = {"G0", "G1", "G2", "sT0", "sT1", "sT2", "sT3", "oT"}
        evt = [0]
        REC = [None]
        gcount = [0]
        KALIAS = {"st_f": "oabuf_t0", "vfb": "oabuf_t1", "st_b0": "sgbuf_t0", "st_b1": "sgbuf_t1"}
        r_xo = Rot("xo", 1)

        def I(eng, method, r=(), w=(), dma=False, **kw):
            r = [KALIAS.get(k, k) for k in r]
            w = [KALIAS.get(k, k) for k in w]
            w = list(w) + [k for k in r if k in PSK and k not in w]
            if REC[0] is not None:
                REC[0][-1].append((eng, (method, kw), list(r), w, dma))
            else:
                S.add(eng, (method, kw), reads=list(r), writes=w, dma=dma)

        def bc(ap, shape, axis):
            return ap.unsqueeze(axis).to_broadcast(shape)

        def v3(ap, h):
            return ap.rearrange("p (h d) -> p h d", h=h)

        for _once in (0,):
            I("sp", "dma_start", w=["cst"], dma=True, out=cst[:], in_=cst_d)
            I("sp", "dma_start", w=["lg"], dma=True, out=lg[:, 0:4], in_=lrf_d.partition_broadcast(128))
            I("sp", "dma_start", w=["lg"], dma=True, out=lg[:, 4:8], in_=lrb_d.partition_broadcast(128))
            I("sp", "dma_start", w=["gng"], dma=True, out=gng[:], in_=gng_d.partition_broadcast(128))
            I("sp", "dma_start", w=["qg"], dma=True, out=qg[:], in_=qg_d.partition_broadcast(128))
            I("sp", "dma_start", w=["kg"], dma=True, out=kg[:], in_=kg_d.partition_broadcast(128))
            I("sp", "dma_start", w=["ngT"], dma=True, out=ngT[:], in_=ngT_d)
            I("sp", "dma_start", w=["cT"], dma=True, out=cT[:].rearrange("p k b -> p (k b)"), in_=cT_d)

            I("pool", "memset", w=["eps"], ap=eps_t[:], constant=EPS)
            I("pool", "memset", w=["one"], ap=one_t[:], constant=1.0)
            I("pool", "memset", w=["lns"], ap=lns_t[:], constant=math.log(128.0 ** -0.5))
            I("pool", "memset", w=["av_all"], ap=av_ext[:], constant=1.0)
            I("pool", "memset", w=["t1_0"], ap=t1[0][:, 0:128], constant=1.0)
            I("pool", "affine_select", r=["t1_0"], w=["t1_0"], out=t1[0][:, 0:128], in_=t1[0][:, 0:128], pattern=[[-1, 128]],
              compare_op=ALU.is_equal, fill=0.0, base=0, channel_multiplier=1)
            I("pool", "tensor_copy", r=["t1_0"], w=["ident"], out=ident[:], in_=t1[0][:, 0:128])
            I("pool", "tensor_copy", r=["t1_0"], w=["identf"], out=identf[0:16, :], in_=t1[0][0:16, 0:16])

            if _STAGE < 0.2 and _STAGE < 1:
                break
            I("act", "activation", r=["lg"], w=["lg"], out=lg[:], in_=lg[:], func=AF.Exp)
            I("dve", "tensor_scalar_mul", r=["lg"], w=["lg"], out=lg[:], in0=lg[:], scalar1=-1.0)
            for i, (half, col, use_s) in enumerate([(0, 0, True), (1, 1, True), (0, 2, False), (1, 3, False),
                                                    (0, 4, False), (1, 4, False)]):
                kw = dict(out=dec[:, i, :], in_=lg[:, half * 4:half * 4 + 4], func=AF.Exp, scale=posc[:, col:col + 1])
                if use_s:
                    kw["bias"] = lns_t[:]
                I("act", "activation", r=["lg", "cst", "lns"], w=["dec"], **kw)
            I("dve", "tensor_tensor", r=["cst", "lg"], w=["wb"], out=wb[:], in0=jtab.rearrange("p (j h) -> p j h", h=4),
              in1=bc(lg[:, 4:8], [128, 16, 4], 1), op=ALU.mult)
            I("act", "activation", r=["wb"], w=["wb"], out=wb[:], in_=wb[:], func=AF.Exp)
            for h in range(4):
                hsl = slice(h * 128, (h + 1) * 128)
                I("dve", "tensor_scalar_mul", r=["cst", "lg"], w=["t2_0"], out=t2[0][:, hsl], in0=DPm, scalar1=lg[:, h:h + 1])
                I("dve", "scalar_tensor_tensor", r=["cst", "lg", "t2_0"], w=["t2_0"], out=t2[0][:, hsl], in0=DNm,
                  scalar=lg[:, 4 + h:5 + h], in1=t2[0][:, hsl], op0=ALU.mult, op1=ALU.add)
            I("act", "activation", r=["t2_0", "lns"], w=["mask"], out=mask[:].rearrange("p h c -> p (h c)"), in_=t2[0][:],
              func=AF.Exp, bias=lns_t[:])

            if _STAGE < 0.3 and _STAGE < 1:
                break
            NSTG = NT_OWN * 512
            STG = Rfb[:].rearrange("p n h f e -> p (n h f e)").bitcast(F32)
            PW = min(INC, NSTG // 2)
            slot_i = [0]

            def stage_slot(width):
                i = slot_i[0] % 2
                slot_i[0] += 1
                return STG[:, i * (NSTG // 2):i * (NSTG // 2) + width], f"stg{i}"

            ckeys = []
            for k in range(KC):
                for c0 in range(0, INC, PW):
                    c1 = min(INC, c0 + PW)
                    st, stk = stage_slot(c1 - c0)
                    I("sp", "dma_start", w=[stk], dma=True, out=st, in_=win_d[k * 128:(k + 1) * 128, c0:c1])
                    wd = c1 - c0
                    cuts = [0, (wd // 3) // 64 * 64, (2 * wd // 3) // 64 * 64, wd]
                    for ei, eng in enumerate(("dve", "pool", "act")):
                        lo, hi = cuts[ei], cuts[ei + 1]
                        if hi <= lo:
                            continue
                        ck = f"w_in_c{len(ckeys)}"
                        ckeys.append(ck)
                        if eng == "act":
                            I("act", "activation", r=[stk], w=[ck], out=w_in[:, k, c0 + lo:c0 + hi], in_=st[:, lo:hi], func=AF.Copy)
                        else:
                            I(eng, "tensor_copy", r=[stk], w=[ck], out=w_in[:, k, c0 + lo:c0 + hi], in_=st[:, lo:hi])
            I("dve", "memset", r=ckeys, w=["w_in"], ap=stat[0][:, 15:16], constant=0.0)
            ckeys = []
            for k in range(KC):
                st, stk = stage_slot(D)
                I("sp", "dma_start", w=[stk], dma=True, out=st, in_=wout_d[k * 128:(k + 1) * 128, :])
                for ei, eng in enumerate(("dve", "pool")):
                    ck = f"w_out_c{len(ckeys)}"
                    ckeys.append(ck)
                    I(eng, "tensor_copy", r=[stk], w=[ck], out=w_out[:, k, ei * 512:(ei + 1) * 512], in_=st[:, ei * 512:(ei + 1) * 512])
            I("dve", "memset", r=ckeys, w=["w_out"], ap=stat[0][:, 14:15], constant=0.0)

            if _STAGE < 0.4 and _STAGE < 1:
                break
            cTf = cT[:].rearrange("p k b -> p (k b)")
            nkb = KC * NB
            I("act", "activation", r=["cT"], w=["rec0", "rec1"], out=rec[:, 0:nkb], in_=cTf, func=AF.Exp, scale=-1.0)
            I("dve", "tensor_scalar_add", r=["rec0", "rec1"], w=["rec0", "rec1"], out=rec[:, 0:nkb], in0=rec[:, 0:nkb], scalar1=1.0)
            I("dve", "reciprocal", r=["rec0", "rec1"], w=["rec0", "rec1"], out=rec[:, 0:nkb], in_=rec[:, 0:nkb])
            I("dve", "tensor_tensor", r=["rec0", "rec1", "cT"], w=["cT"], out=cTf, in0=cTf, in1=rec[:, 0:nkb], op=ALU.mult)
            segs = [(t1[0], "t1_0"), (t1[1], "t1_1"), (t2[0], "t2_0"), (t2[1], "t2_1"), (t3[0], "t3_0"), (t3[1], "t3_1")]
            for cc in range(6):
                seg, segk = segs[cc]
                I("sp", "dma_start", r=["ident", "mask"], w=[segk], dma=True, out=seg[0:NB, :],
                  in_=bada_d[:, cc * 512:(cc + 1) * 512].partition_broadcast(NB))
                gi, gk = r_G.next()
                kper = min(KC, (NSTG // 2) // 512)
                for k0 in range(0, KC, kper):
                    st, stk = stage_slot(kper * 512)
                    st3 = st.rearrange("p (k c) -> p k c", k=kper)
                    I("sp", "dma_start", w=[stk], dma=True, out=st3,
                      in_=wada_d[k0 * 128:(k0 + kper) * 128, cc * 512:(cc + 1) * 512].rearrange("(k p) c -> p k c", p=128))
                    for kk in range(kper):
                        k = k0 + kk
                        I("pe", "matmul", r=[stk, "cT"], w=[gk], out=G[gi][0:NB, :], lhsT=cT[:, k, :], rhs=st3[:, kk, :],
                          start=(k == 0), stop=(k == KC - 1))
                I("dve", "tensor_tensor", r=[gk, segk], w=[segk], out=seg[0:NB, :], in0=G[gi][0:NB, :], in1=seg[0:NB, :], op=ALU.add)
                I("sp", "dma_start", r=[segk], w=["mod_d"], dma=True, out=mod_d[:, cc * 512:(cc + 1) * 512], in_=seg[0:NB, :])
                if cc < 4:
                    gi2, gk2 = r_G.next()
                    for j in range(4):
                        I("pe", "transpose", r=[segk, "identf"], w=[gk2], out=G[gi2][:, j * NB:(j + 1) * NB],
                          in_=seg[0:NB, j * 128:(j + 1) * 128], identity=identf[0:NB, 0:NB])
                    for j in range(4):
                        kk = cc * 4 + j
                        I("dve", "tensor_copy", r=[gk2], w=["AB"], out=AB[:, kk // 8, kk % 8, :], in_=G[gi2][:, j * NB:(j + 1) * NB])
            if _STAGE < 0.5 and _STAGE < 1:
                break
            I("dve", "tensor_scalar_add", r=["AB"], w=["AB"], out=AB[:, 1], in0=AB[:, 1], scalar1=1.0)
            I("dve", "tensor_tensor", r=["AB", "ngT"], w=["AB"], out=AB[:, 1], in0=AB[:, 1], in1=bc(ngT[:], [128, KC, NB], 2),
              op=ALU.mult)

        xseq = []
        xpos = [0]

        def next_x_load(t):
            k = xpos[0] + (1 if t == BT - 1 else 0)
            nxt = xpos[0] + 1
            if t == BT - 1:
                xpos[0] += 1
            if nxt < len(xseq):
                src, r0 = xseq[nxt]
                I("sp", "dma_start", w=[f"xbuf{t}"], dma=True, out=xbuf[t][:], in_=src[r0 + t * 128:r0 + (t + 1) * 128, :])

        def front(x_src, row0, ntile, b):
            for t in range(ntile):
                xi, xk = t, f"xbuf{t}"
                si, sk = r_xs.next()
                ti, tk = r_stat.next()
                I("act", "activation", r=[xk], w=[sk, tk], out=xs_bf[si][:], in_=xbuf[xi][:], func=AF.Square,
                  accum_out=stat[ti][:, 0:1])
                I("act", "activation", r=[tk, "eps"], w=[tk], out=stat[ti][:, 1:2], in_=stat[ti][:, 0:1], func=AF.Ln,
                  scale=1.0 / D, bias=eps_t[:])
                I("act", "activation", r=[tk], w=[tk], out=stat[ti][:, 2:3], in_=stat[ti][:, 1:2], func=AF.Exp, scale=-0.5)
                I("act", "activation", r=[xk, tk], w=[sk], out=xs_bf[si][:], in_=xbuf[xi][:], func=AF.Copy, scale=stat[ti][:, 2:3])
                next_x_load(t)
                banks = [r_G.next(), r_G.next()]
                for k in range(KC):
                    gi, gk = banks[k // 4]
                    gbf = G[gi][:].bitcast(BF16)
                    I("pe", "transpose", r=[sk, "ident"], w=[gk], out=gbf[:, (k % 4) * 128:(k % 4 + 1) * 128],
                      in_=xs_bf[si][:, k * 128:(k + 1) * 128], identity=ident[:])
                for k in range(KC):
                    gi, gk = banks[k // 4]
                    gbf = G[gi][:].bitcast(BF16)
                    if k // 4 == 0:
                        I("act", "activation", r=[gk, "AB"], w=[f"hT{hTc[0]}"], out=hT2[hTc[0]][:, k, t * 128:(t + 1) * 128],
                          in_=gbf[:, (k % 4) * 128:(k % 4 + 1) * 128], func=AF.Identity, scale=AB[:, 1, k, b:b + 1], bias=AB[:, 0, k, b:b + 1])
                    else:
                        I("dve", "tensor_scalar", r=[gk, "AB"], w=[f"hT{hTc[0]}"], out=hT2[hTc[0]][:, k, t * 128:(t + 1) * 128],
                          in0=gbf[:, (k % 4) * 128:(k % 4 + 1) * 128], scalar1=AB[:, 1, k, b:b + 1], scalar2=AB[:, 0, k, b:b + 1],
                          op0=ALU.mult, op1=ALU.add)

        def front_g(x_src, row0, ntile, b):
            for t in range(ntile):
                xi, xk = t, f"xbuf{t}"
                si, sk = r_xs.next()
                ti, tk = r_stat.next()
                I("act", "activation", r=[xk], w=[sk, tk], out=xs_bf[si][:], in_=xbuf[xi][:], func=AF.Square,
                  accum_out=stat[ti][:, 0:1])
                I("act", "activation", r=[tk, "eps"], w=[tk], out=stat[ti][:, 1:2], in_=stat[ti][:, 0:1], func=AF.Ln,
                  scale=1.0 / D, bias=eps_t[:])
                I("act", "activation", r=[tk], w=[tk], out=stat[ti][:, 2:3], in_=stat[ti][:, 1:2], func=AF.Exp, scale=-0.5)
                I("act", "activation", r=[xk, tk], w=[sk], out=xs_bf[si][:], in_=xbuf[xi][:], func=AF.Copy, scale=stat[ti][:, 2:3])
                next_x_load(t)
                banks = [r_G.next(), r_G.next()]
                for k in range(KC):
                    gi, gk = banks[k // 4]
                    gbf = G[gi][:].bitcast(BF16)
                    I("pe", "transpose", r=[sk, "ident"], w=[gk], out=gbf[:, (k % 4) * 128:(k % 4 + 1) * 128],
                      in_=xs_bf[si][:, k * 128:(k + 1) * 128], identity=ident[:])
                for k in range(KC):
                    gi, gk = banks[k // 4]
                    gbf = G[gi][:].bitcast(BF16)
                    if k // 4 == 0:
                        I("act", "activation", r=[gk, "AB"], w=[f"hT{hTc[0]}"], out=hT2[hTc[0]][:, k, t * 128:(t + 1) * 128],
                          in_=gbf[:, (k % 4) * 128:(k % 4 + 1) * 128], func=AF.Identity, scale=AB[:, 1, k, b:b + 1], bias=AB[:, 0, k, b:b + 1])
                    else:
                        I("dve", "tensor_scalar", r=[gk, "AB"], w=[f"hT{hTc[0]}"], out=hT2[hTc[0]][:, k, t * 128:(t + 1) * 128],
                          in0=gbf[:, (k % 4) * 128:(k % 4 + 1) * 128], scalar1=AB[:, 1, k, b:b + 1], scalar2=AB[:, 0, k, b:b + 1],
                          op0=ALU.mult, op1=ALU.add)
                yield

        def proj_tok(t, c0, ncol):
            gi, gk = r_G.next()
            for k in range(KC):
                I("pe", "matmul", r=[f"hT{hTc[0]}", "w_in"], w=[gk], out=G[gi][:, 0:ncol], lhsT=hT2[hTc[0]][:, k, t * 128:(t + 1) * 128],
                  rhs=w_in[:, k, c0:c0 + ncol], start=(k == 0), stop=(k == KC - 1))
            return gi, gk

        def load_tab(row):
            ti, tk = r_tab.next()
            I("sp", "dma_start", w=[tk], dma=True, out=tabt[ti][:], in_=tab_d[row:row + 128, :])
            return ti, tk

        def rope(src, src_keys, src_psum, nh, hd, cosv, sinv, tkey, out_ap, out_key):
            q = hd // 4
            a, ak_ = r_t1.next()
            bq, bk_ = r_t2.next()
            n = nh * hd
            I("dve", "tensor_tensor", r=list(src_keys) + [tkey], w=[ak_],
              out=v3(t1[a][:, 0:n], nh), in0=v3(src, nh), in1=bc(cosv, [128, nh, hd], 1), op=ALU.mult)
            s5 = src.rearrange("p (h f j i) -> p h f j i", h=nh, f=2, j=2, i=q)
            o5 = t2[bq][:, 0:n].rearrange("p (h f j i) -> p h f j i", h=nh, f=2, j=2, i=q)
            sn4 = sinv.rearrange("p (f j i) -> p f j i", f=2, j=2, i=q)
            for j in range(2):
                I("dve", "tensor_tensor", r=list(src_keys) + [tkey], w=[bk_],
                  out=o5[:, :, :, j, :], in0=s5[:, :, :, 1 - j, :], in1=bc(sn4[:, :, j, :], [128, nh, 2, q], 1), op=ALU.mult)
            I("dve", "tensor_tensor", r=[ak_, bk_], w=[out_key], out=out_ap, in0=t1[a][:, 0:n], in1=t2[bq][:, 0:n], op=ALU.add)

        def head_rstd(src_ap, src_key, nh, hd, src_psum=True):
            a, ak_ = r_t3.next()
            n = nh * hd
            ti, tk = r_stat.next()
            I("act", "activation", r=[src_key], w=[ak_], out=t3[a][:, 0:n], in_=src_ap, func=AF.Square)
            I("dve", "tensor_reduce", r=[ak_], w=[tk], out=stat[ti][:, 4:4 + nh], in_=v3(t3[a][:, 0:n], nh), axis=AX.X, op=ALU.add)
            I("act", "activation", r=[tk, "eps"], w=[tk], out=stat[ti][:, 4:4 + nh], in_=stat[ti][:, 4:4 + nh], func=AF.Ln,
              scale=1.0 / hd, bias=eps_t[:])
            I("act", "activation", r=[tk], w=[tk], out=stat[ti][:, 4:4 + nh], in_=stat[ti][:, 4:4 + nh], func=AF.Exp, scale=-0.5)
            return stat[ti][:, 4:4 + nh], tk

        def qk_norm_rope(gi, gk, nh, gain, gain_key, ti, tk, out_ap, out_key):
            n = nh * 64
            src = G[gi][:, 0:n]
            rs, rsk = head_rstd(src, gk, nh, 64)
            a, ak_ = r_t3.next()
            I("dve", "tensor_tensor", r=[gk, rsk], w=[ak_], out=v3(t3[a][:, 0:n], nh), in0=v3(src, nh),
              in1=bc(rs, [128, nh, 64], 2), op=ALU.mult)
            I("dve", "tensor_tensor", r=[ak_, gain_key], w=[ak_], out=v3(t3[a][:, 0:n], nh), in0=v3(t3[a][:, 0:n], nh),
              in1=bc(gain, [128, nh, 64], 1), op=ALU.mult)
            rope(t3[a][:, 0:n], [ak_], False, nh, 64, tabt[ti][:, 256:320], tabt[ti][:, 320:384], tk, out_ap, out_key)

        def transpose_to(src_bf, src_key, nblk, dst_fn, dst_key, evac_engs=("act", "dve")):
            gi, gk = r_G.next()
            gbf = G[gi][:].bitcast(BF16)
            for i in range(nblk):
                I("pe", "transpose", r=[src_key, "ident"], w=[gk], out=gbf[:, i * 128:(i + 1) * 128],
                  in_=src_bf[:, i * 128:(i + 1) * 128], identity=ident[:])
            evt[0] += 1
            for i in range(nblk):
                eng = evac_engs[evt[0] % len(evac_engs)]
                if eng == "act":
                    I("act", "activation", r=[gk], w=[dst_key], out=dst_fn(i), in_=gbf[:, i * 128:(i + 1) * 128], func=AF.Copy)
                else:
                    I("dve", "tensor_copy", r=[gk], w=[dst_key], out=dst_fn(i), in_=gbf[:, i * 128:(i + 1) * 128])

        def ret_k(t, ti, tk):
            gi, gk = proj_tok(t, O_RK, 512)
            ri, rkk = r_tok.next()
            rope(G[gi][:, :], [gk], True, 4, 128, tabt[ti][:, 0:128], tabt[ti][:, 128:256], tk, tok_bf[ri][:], rkk)
            return ri, rkk

        def g_ktile(t, tp, tab_row, kslot, own_n, other_j):
            ti, tk = tp, f"tabt{tp}"
            I("sp", "dma_start", w=[tk], dma=True, out=tabt[ti][:], in_=tab_d[tab_row:tab_row + 128, :])
            gi, gk = proj_tok(t, O_RK, 512)
            ri, rkk = tp * 2, f"tokbf{tp * 2}"
            rope(G[gi][:, :], [gk], True, 4, 128, tabt[ti][:, 0:128], tabt[ti][:, 128:256], tk, tok_bf[ri][:], rkk)
            yield
            gi, gk = proj_tok(t, O_RV, 512)
            g4 = v3(G[gi][:, :], 4)
            for d_ in range(2):
                I("dve", "tensor_tensor", r=[gk, "dec"], w=["vfb"], out=vfb[:, :, d_, :], in0=g4,
                  in1=bc(dec[:, d_, :], [128, 4, 128], 2), op=ALU.mult)
            sg_ = []
            for hp in range(2):
                g2, g2k = r_G.next()
                for hh in range(2):
                    h = hp * 2 + hh
                    I("pe", "matmul", r=[rkk, "vfb"], w=[g2k], out=G[g2][:, hh * 256:(hh + 1) * 256],
                      lhsT=tok_bf[ri][:, h * 128:(h + 1) * 128], rhs=vfb[:, h].rearrange("p f e -> p (f e)"), start=True, stop=True)
                sg_.append((g2, g2k))
            for hp, (g2, g2k) in enumerate(sg_):
                S4 = G[g2][:, :].rearrange("p (h f e) -> p h f e", h=2, f=2)
                hs = slice(hp * 2, hp * 2 + 2)
                if own_n is not None:
                    I("dve", "tensor_copy", r=["st_f"], w=[f"Rf{own_n}"], out=Rfb[:, own_n, hs, 0, :], in_=st_f[:, hs, :])
                    I("act", "activation", r=[g2k], w=[f"Rb{own_n}"], out=Rfb[:, own_n, hs, 1, :], in_=S4[:, :, 1, :], func=AF.Copy)
                else:
                    for hh in range(2):
                        h = hp * 2 + hh
                        I("dve", "scalar_tensor_tensor", r=[g2k, "wb", "st_b0"], w=["st_b0"], out=st_b[0][:, h, :], in0=S4[:, hh, 1, :],
                          scalar=wb[:, other_j, h:h + 1], in1=st_b[0][:, h, :], op0=ALU.mult, op1=ALU.add)
                for hh in range(2):
                    h = hp * 2 + hh
                    I("dve", "scalar_tensor_tensor", r=[g2k, "st_f", "dec"], w=["st_f"], out=st_f[:, h, :], in0=st_f[:, h, :],
                      scalar=dec[:, 4, h:h + 1], in1=S4[:, hh, 0, :], op0=ALU.mult, op1=ALU.add)
            yield
            gi, gk = proj_tok(t, O_AK, 256)
            ai, akk = tp * 2 + 1, f"tokbf{tp * 2 + 1}"
            qk_norm_rope(gi, gk, 2, kg[:], "kg", ti, tk, tok_bf[ai][:, 0:128], akk)
            I("act", "activation", r=[gk, "av_all"], w=[f"av{kslot}"], out=av_ext[:, kslot, 0:64], in_=G[gi][:, 128:192], func=AF.Copy)
            I("act", "activation", r=[gk, "av_all"], w=[f"av{kslot}"], out=av_ext[:, kslot, 128:192], in_=G[gi][:, 192:256], func=AF.Copy)
            yield
            transpose_to(tok_bf[ai], akk, 1, lambda i: akT[:, kslot * 128:(kslot + 1) * 128], f"akT{kslot}", evac_engs=("dve",))
            yield

        def g_FPhead(x_src, row0, ntile, b, par):
            nq = ntile * 128
            sgT = sgT2[par]
            P = f"_{par}"
            for _ in front_g(x_src, row0, ntile, b):
                yield
            for i in range(4):
                gi, gk = r_G.next()
                for k in range(KC):
                    I("pe", "matmul", r=[f"hT{hTc[0]}", "w_in"], w=[gk], out=G[gi][:, 0:nq], lhsT=w_in[:, k, O_AG + i * 128:O_AG + (i + 1) * 128],
                      rhs=hT2[hTc[0]][:, k, 0:nq], start=(k == 0), stop=(k == KC - 1))
                a, ak_ = r_t3.next()
                I("act", "activation", r=[gk], w=[ak_], out=t3[a][:, 0:nq], in_=G[gi][:, 0:nq], func=AF.Exp, scale=-1.0)
                I("act", "activation", r=[ak_, "one"], w=[ak_], out=t3[a][:, 0:nq], in_=t3[a][:, 0:nq], func=AF.Ln, bias=one_t[:])
                I("act", "activation", r=[ak_], w=[ak_], out=t3[a][:, 0:nq], in_=t3[a][:, 0:nq], func=AF.Exp, scale=-1.0)
                I("dve", "tensor_tensor", r=[gk, ak_], w=[f"sgT{i}" + P], out=sgT[:, i, 0:nq], in0=G[gi][:, 0:nq], in1=t3[a][:, 0:nq],
                  op=ALU.mult)
                yield

        def g_FPtile(t, tp, tab_row, n, par):
            aqT, mixT_r = aqT2[par], mixT_r2[par]
            P = f"_{par}"
            T = f"_t{tp}"
            rqT, rkT, v_bf, pm = rqT2[tp], rkT2[tp], v_bf2[tp], pm2[tp]
            sgb, oab = sgbuf[tp], oabuf[tp]
            ts_ = slice(t * 128, (t + 1) * 128)
            ti, tk = tp, f"tabt{tp}"
            I("sp", "dma_start", w=[tk], dma=True, out=tabt[ti][:], in_=tab_d[tab_row:tab_row + 128, :])
            tk0, tk1 = f"tokbf{tp * 2}", f"tokbf{tp * 2 + 1}"
            tb0, tb1 = tok_bf[tp * 2], tok_bf[tp * 2 + 1]
            gi, gk = proj_tok(t, O_RQ, 512)
            rope(G[gi][:, :], [gk], True, 4, 128, tabt[ti][:, 0:128], tabt[ti][:, 128:256], tk, tb0[:], tk0)
            yield
            transpose_to(tb0, tk0, 4, lambda i: rqT[:, i, :], "rqT" + T)
            gi, gk = proj_tok(t, O_RK, 512)
            rope(G[gi][:, :], [gk], True, 4, 128, tabt[ti][:, 0:128], tabt[ti][:, 128:256], tk, tb1[:], tk1)
            yield
            transpose_to(tb1, tk1, 4, lambda i: rkT[:, i, :], "rkT" + T)
            gi, gk = proj_tok(t, O_RV, 512)
            I("act", "activation", r=[gk], w=["v_bf" + T], out=v_bf[:].rearrange("p h e -> p (h e)"), in_=G[gi][:, :], func=AF.Copy)
            gi, gk = proj_tok(t, O_AQ, 512)
            qk_norm_rope(gi, gk, 8, qg[:], "qg", ti, tk, tb0[:], tk0)
            yield
            transpose_to(tb0, tk0, 4, lambda i: aqT[:, i, ts_], "aqT" + P)
            gi, gk = proj_tok(t, O_RG, 512)
            a, ak_ = r_t3.next()
            sgk = "sgbuf" + T
            I("act", "activation", r=[gk], w=[ak_], out=t3[a][:], in_=G[gi][:, :], func=AF.Exp, scale=-1.0)
            I("act", "activation", r=[ak_, "one"], w=[ak_], out=t3[a][:], in_=t3[a][:], func=AF.Ln, bias=one_t[:])
            I("act", "activation", r=[ak_], w=[ak_], out=t3[a][:], in_=t3[a][:], func=AF.Exp, scale=-1.0)
            I("dve", "tensor_tensor", r=[gk, ak_], w=[sgk], out=sgb[:], in0=G[gi][:, :], in1=t3[a][:], op=ALU.mult)
            I("dve", "tensor_tensor", r=[sgk, "gng"], w=[sgk], out=sgb[:], in0=sgb[:], in1=gng[:], op=ALU.mult)
            gi, gk = r_G.next()
            for h in range(4):
                I("pe", "matmul", r=["rkT" + T, "rqT" + T], w=[gk], out=G[gi][:, h * 128:(h + 1) * 128], lhsT=rkT[:, h, :], rhs=rqT[:, h, :],
                  start=True, stop=True)
            I("dve", "tensor_tensor", r=[gk, "mask"], w=["pm" + T], out=pm[:].rearrange("p h c -> p (h c)"), in0=G[gi][:, :],
              in1=mask[:].rearrange("p h c -> p (h c)"), op=ALU.mult)
            yield
            oak = "oabuf" + T
            crs = []
            for hp in range(2):
                g2, g2k = r_G.next()
                for hh in range(2):
                    h = hp * 2 + hh
                    I("pe", "matmul", r=["rqT" + T, f"Rf{n}", f"Rb{n}"], w=[g2k], out=G[g2][:, hh * 256:(hh + 1) * 256],
                      lhsT=rqT[:, h, :], rhs=Rfb[:, n, h].rearrange("p f e -> p (f e)"), start=True, stop=True)
                crs.append((g2, g2k))
            gi, gk = r_G.next()
            for h in range(4):
                I("pe", "matmul", r=["pm" + T, "v_bf" + T], w=[gk], out=G[gi][:, h * 128:(h + 1) * 128], lhsT=pm[:, h, :], rhs=v_bf[:, h, :],
                  start=True, stop=True)
            u, uk = r_t3.next()
            for hp, (g2, g2k) in enumerate(crs):
                C4 = G[g2][:, :].rearrange("p (h f e) -> p h f e", h=2, f=2)
                hs = slice(hp * 2, hp * 2 + 2)
                o3 = v3(oab[:, hp * 256:(hp + 1) * 256], 2)
                u3 = v3(t3[u][:, hp * 256:(hp + 1) * 256], 2)
                I("dve", "tensor_tensor", r=[g2k, "dec"], w=[oak], out=o3, in0=C4[:, :, 0, :],
                  in1=bc(dec[:, 2, hs], [128, 2, 128], 2), op=ALU.mult)
                I("dve", "tensor_tensor", r=[g2k, "dec"], w=[uk], out=u3, in0=C4[:, :, 1, :],
                  in1=bc(dec[:, 3, hs], [128, 2, 128], 2), op=ALU.mult)
            I("dve", "tensor_tensor", r=[oak, uk], w=[oak], out=oab[:], in0=oab[:], in1=t3[u][:], op=ALU.add)
            I("dve", "tensor_tensor", r=[gk, oak], w=[oak], out=oab[:], in0=G[gi][:, :], in1=oab[:], op=ALU.add)
            sq, sqk = r_t3.next()
            st_i, stk = r_stat.next()
            I("dve", "tensor_tensor", r=[oak], w=[sqk], out=t3[sq][:], in0=oab[:], in1=oab[:], op=ALU.mult)
            I("dve", "tensor_reduce", r=[sqk], w=[stk], out=stat[st_i][:, 4:8], in_=v3(t3[sq][:], 4), axis=AX.X, op=ALU.add)
            I("act", "activation", r=[stk, "eps"], w=[stk], out=stat[st_i][:, 4:8], in_=stat[st_i][:, 4:8], func=AF.Ln,
              scale=1.0 / 128, bias=eps_t[:])
            I("act", "activation", r=[stk], w=[stk], out=stat[st_i][:, 4:8], in_=stat[st_i][:, 4:8], func=AF.Exp, scale=-0.5)
            I("dve", "tensor_tensor", r=[oak, stk], w=[oak], out=v3(oab[:], 4), in0=v3(oab[:], 4),
              in1=bc(stat[st_i][:, 4:8], [128, 4, 128], 2), op=ALU.mult)
            I("dve", "tensor_tensor", r=[oak, sgk], w=[tk1], out=tb1[:], in0=oab[:], in1=sgb[:], op=ALU.mult)
            yield
            transpose_to(tb1, tk1, 4, lambda i: mixT_r[:, i, ts_], "mixT_r" + P)
            yield

        def g_AT(ntile, nkt, par):
            LA = N_ST - 1
            nq = ntile * 128
            aqT, sgT = aqT2[par], sgT2[par]
            P = f"_{par}"
            its = [(g, ip, j) for g in range(2) for ip in range(2) for j in range(nkt)]

            def qk(it):
                g, ip, j = it
                rows = slice(g * 64, (g + 1) * 64)
                si, sk = r_sT.next()
                I("pe", "matmul", r=[f"akT{j}", "aqT" + P], w=[sk], out=sT[si][:, :, 0:nq], lhsT=akT[rows, j * 128:(j + 1) * 128],
                  rhs=aqT[rows, 2 * ip:2 * ip + 2, 0:nq], start=True, stop=True)
                pi, pk = r_pT.next()
                I("act", "activation", r=[sk], w=[pk], out=pT[pi][:, :, 0:nq], in_=sT[si][:, :, 0:nq], func=AF.Exp, scale=0.125)
                return pi, pk

            pend = [qk(its[k]) for k in range(min(LA, len(its)))]
            for n_it, (g, ip, j) in enumerate(its):
                pi, pk = pend.pop(0)
                if n_it + LA < len(its):
                    pend.append(qk(its[n_it + LA]))
                rows = slice(g * 64, (g + 1) * 64)
                orow = slice((1 - g) * 64, (2 - g) * 64)
                ext = slice(g * 64, g * 64 + 128)
                o3 = oT[:, :].rearrange("p (a q) -> p a q", a=2)
                I("pe", "matmul", r=[pk, f"av{j}"], w=["oT"], out=o3[:, :, 0:nq], lhsT=av_ext[:, j, ext], rhs=pT[pi][:, :, 0:nq],
                  start=(j == 0), stop=(j == nkt - 1))
                yield
                if j == nkt - 1:
                    a, ak_ = r_t1.next()
                    r3 = t1[a][:, :].rearrange("p (a q) -> p a q", a=2)
                    I("dve", "reciprocal", r=["oT"], w=[ak_], out=r3[orow, :, 0:nq], in_=o3[orow, :, 0:nq])
                    I("dve", "tensor_tensor", r=["oT", ak_], w=[ak_], out=r3[rows, :, 0:nq], in0=o3[rows, :, 0:nq],
                      in1=r3[orow, :, 0:nq], op=ALU.mult)
                    I("dve", "tensor_tensor", r=[ak_, f"sgT{2 * ip}" + P, f"sgT{2 * ip + 1}" + P], w=["mixT_a"],
                      out=mixT_a[rows, 2 * ip:2 * ip + 2, 0:nq], in0=r3[rows, :, 0:nq], in1=sgT[rows, 2 * ip:2 * ip + 2, 0:nq], op=ALU.mult)
                    yield

        def g_O(x_src, row0, ntile, y_dst, par):
            mixT_r = mixT_r2[par]
            P = f"_{par}"
            for t in range(ntile):
                ts_ = slice(t * 128, (t + 1) * 128)
                xi, xk = r_xo.next()
                I("sp", "dma_start", w=[xk], dma=True, out=xo[xi][:], in_=x_src[row0 + t * 128:row0 + (t + 1) * 128, :])
                for half in range(2):
                    gi, gk = r_G.next()
                    cs = slice(half * 512, (half + 1) * 512)
                    for k in range(KC):
                        lhs = mixT_r[:, k, ts_] if k < 4 else mixT_a[:, k - 4, ts_]
                        I("pe", "matmul", r=["mixT_r" + P, "mixT_a", "w_out"], w=[gk], out=G[gi][:, :], lhsT=lhs, rhs=w_out[:, k, cs],
                          start=(k == 0), stop=(k == KC - 1))
                    a, ak_ = r_t1.next()
                    I("dve", "tensor_tensor", r=[gk, "gate_rep"], w=[ak_], out=t1[a][:], in0=G[gi][:, :], in1=gate_rep[:, cs],
                      op=ALU.mult)
                    I("dve", "tensor_tensor", r=[ak_, xk], w=[xk], out=xo[xi][:, cs], in0=xo[xi][:, cs], in1=t1[a][:],
                      op=ALU.add)
                skey = f"YOUT{len(store_keys)}"
                store_keys.append(skey)
                I("sp", "dma_start", r=[xk], w=[skey], dma=True, out=y_dst[row0 + t * 128:row0 + (t + 1) * 128, :], in_=xo[xi][:])
                yield

        def record(gen):
            REC[0] = [[]]
            for _ in gen:
                REC[0].append([])
            chunks = [c for c in REC[0] if c]
            REC[0] = None
            return chunks

        def replay(chunks):
            for c in chunks:
                for (eng, fn, r, w, dma) in c:
                    S.add(eng, fn, reads=r, writes=w, dma=dma)

        def split_pe(chunks):
            out = []
            for c in chunks:
                cur, cur_pe = [], None
                for op in c:
                    is_pe = op[0] == "pe"
                    if cur and is_pe and not cur_pe:
                        out.append(cur)
                        cur = []
                    if cur and (not is_pe) and cur_pe:
                        out.append(cur)
                        cur = []
                    cur.append(op)
                    cur_pe = is_pe
                if cur:
                    out.append(cur)
            return out

        def zip_chunks(a, b):
            out = []
            for i in range(max(len(a), len(b))):
                if i < len(a):
                    out.append(a[i])
                if i < len(b):
                    out.append(b[i])
            return out

        def chunk_cost(c):
            t = 0.0
            prev = None
            for (eng, (m, kw), r, w, dma) in c:
                if eng == "pe":
                    t += 0.05
                    continue
                n = 1
                o = kw.get("out", kw.get("ap"))
                if o is not None:
                    for d_ in o.shape[1:]:
                        n *= d_
                if dma:
                    t += 2.0
                elif eng == "act":
                    t += 0.25 + n / 1300.0
                elif eng == "dve":
                    t += 0.12 + n / 900.0
                else:
                    t += 0.15 + n / 480.0
                if prev is not None and prev != eng:
                    t += 0.3
                prev = eng
            return t

        def merge_weighted(main, other):
            if not other:
                return list(main)
            wts = [chunk_cost(c) for c in other]
            tot = sum(wts) or 1.0
            out = []
            mi = 0
            cum = 0.0
            for c, wv in zip(other, wts):
                out.append(c)
                cum += wv
                tgt = int(round(cum / tot * len(main)))
                while mi < tgt and mi < len(main):
                    out.append(main[mi])
                    mi += 1
            out.extend(main[mi:])
            return out

        def merge_even(main, other):
            if not other:
                return list(main)
            out = []
            acc = 0.0
            oi = 0
            ratio = len(other) / float(len(main))
            for c in main:
                out.append(c)
                acc += ratio
                while acc >= 1.0 and oi < len(other):
                    out.append(other[oi])
                    oi += 1
                    acc -= 1.0
            out.extend(other[oi:])
            return out

        jobs = []
        for j in range(NS):
            jobs.append(dict(x=xs_d, row0=j * SS, tab0=0, n_own=SS // 128, n_oth=0, b=j, y=ys_d))
        jobs.append(dict(x=xp_d, row0=0, tab0=SS, n_own=SPO // 128, n_oth=SPX // 128, b=NS, y=yp_d))
        for jb in jobs:
            n_own, n_oth = jb["n_own"], jb["n_oth"]
            for b0 in range(0, n_oth, BT):
                xseq.append((jb["x"], jb["row0"] + (n_own + b0) * 128))
            for b0 in range(0, n_own, BT):
                xseq.append((jb["x"], jb["row0"] + b0 * 128))
            for b0 in range(0, n_own, BT):
                xseq.append((jb["x"], jb["row0"] + b0 * 128))
        for t in range(BT):
            I("sp", "dma_start", w=[f"xbuf{t}"], dma=True, out=xbuf[t][:], in_=xseq[0][0][xseq[0][1] + t * 128:xseq[0][1] + (t + 1) * 128, :])
        for jb in (jobs if _STAGE >= 1 else []):
            b = jb["b"]
            n_own, n_oth = jb["n_own"], jb["n_oth"]
            nkt = n_own + n_oth
            assert nkt % 4 == 0 and n_own % BT == 0 and n_oth % BT == 0 and BT == 2
            I("sp", "dma_start", r=["mod_d"], w=["gate_rep"], dma=True, out=gate_rep[:],
              in_=mod_d[b:b + 1, 2 * D:3 * D].partition_broadcast(128))
            I("pool", "memset", w=["st_f"], ap=st_f[:], constant=0.0)
            I("pool", "memset", w=["st_b0"], ap=st_b[0][:], constant=0.0)
            order = [("oth", i) for i in range(n_oth)] + [("own", i) for i in range(n_own)]
            p1 = []
            for b0 in range(0, len(order), BT):
                blk = order[b0:b0 + BT]
                kind, first = blk[0]
                off = (n_own * 128 if kind == "oth" else 0) + first * 128
                p1.append((blk, kind, first, off))
            r_G.n = NG1
            hTc[0] = gcount[0] % 2
            gcount[0] += 1
            front(jb["x"], jb["row0"] + p1[0][3], BT, b)
            for k, (blk, kind, first, off) in enumerate(p1):
                if kind == "own" and first == 0 and n_oth > 0:
                    I("pool", "tensor_scalar_mul", r=["st_f", "cst"], w=["st_f"], out=st_f[:], in0=st_f[:], scalar1=flags[:, 0:1])
                    I("pool", "tensor_scalar_mul", r=["st_b0", "cst"], w=["st_b0"], out=st_b[0][:], in0=st_b[0][:], scalar1=flags[:, 1:2])
                tiles = []
                for t, (kd, i) in enumerate(blk):
                    if kd == "own":
                        tiles.append(record(g_ktile(t, t, jb["tab0"] + off + t * 128, i, i, None)))
                    else:
                        tiles.append(record(g_ktile(t, t, jb["tab0"] + off + t * 128, n_own + i, None, i)))
                nxt = []
                if k + 1 < len(p1):
                    hTc[0] = gcount[0] % 2
                    gcount[0] += 1
                    nxt = record(front_g(jb["x"], jb["row0"] + p1[k + 1][3], BT, b))
                kz = zip_chunks(tiles[0], tiles[1])
                mg = []
                for ci, c in enumerate(kz):
                    mg.append(c)
                    if ci < len(nxt):
                        mg.append(nxt[ci])
                mg.extend(nxt[len(kz):])
                replay(mg)
            if _STAGE < 2:
                break
            def g_scan():
                cur = 0
                for n in range(n_own - 1, -1, -1):
                    nxt = 1 - cur
                    for h in range(4):
                        I("dve", "scalar_tensor_tensor", r=[f"st_b{cur}", "dec", f"Rb{n}"], w=[f"st_b{nxt}"], out=st_b[nxt][:, h, :],
                          in0=st_b[cur][:, h, :], scalar=dec[:, 5, h:h + 1], in1=Rfb[:, n, h, 1, :], op0=ALU.mult, op1=ALU.add)
                    I("dve", "tensor_copy", r=[f"st_b{cur}"], w=[f"Rb{n}"], out=Rfb[:, n, :, 1, :], in_=st_b[cur][:])
                    cur = nxt
                    yield

            scan_chunks = record(g_scan())
            blocks = list(range(0, n_own, BT))
            r_G.n = NG2
            r_G.i = -1

            def rec_fp(bi):
                b0 = blocks[bi]
                par = bi % 2
                hTc[0] = gcount[0] % 2
                gcount[0] += 1
                head = record(g_FPhead(jb["x"], jb["row0"] + b0 * 128, BT, b, par))
                tl = [record(g_FPtile(t, t, jb["tab0"] + (b0 + t) * 128, b0 + t, par)) for t in range(BT)]
                return head + zip_chunks(tl[0], tl[1])

            fp0 = rec_fp(0)
            mg0 = []
            si_ = 0
            for ci, c in enumerate(fp0):
                mg0.append(c)
                if ci < 8:
                    take = -(-len(scan_chunks) // 8)
                    mg0.extend(scan_chunks[si_:si_ + take])
                    si_ += take
            assert si_ >= len(scan_chunks) and len(fp0) >= 12
            replay(mg0)
            o_prev = []
            for bi, b0 in enumerate(blocks):
                at = record(g_AT(BT, nkt, bi % 2))
                fp = rec_fp(bi + 1) if bi + 1 < len(blocks) else []
                nhead = min(len(o_prev), nkt - 1)
                headc = []
                for k in range(nhead):
                    headc.append(at[k])
                    headc.append(o_prev[k])
                headc.extend(o_prev[nhead:])
                replay(headc)
                replay(merge_weighted(at[nhead:], split_pe(fp)))
                o_prev = record(g_O(jb["x"], jb["row0"] + b0 * 128, BT, jb["y"], bi % 2))
            replay(o_prev)
        I("sp", "wait_only", r=store_keys)
        S.emit(nc, ctx)
        with nc.Block() as block:
            @block.sync
            def _(e):
                S.run_stream("sp", e)

            @block.scalar
            def _(e):
                S.run_stream("act", e)

            @block.vector
            def _(e):
                S.run_stream("dve", e)

            @block.gpsimd
            def _(e):
                S.run_stream("pool", e)

            @block.tensor
            def _(e):
                S.run_stream("pe", e)
    return nc


def _rope_tab(pos):
    pos = np.asarray(pos)
    row = (pos // 64).astype(np.float32)
    col = (pos % 64).astype(np.float32)
    out = np.zeros((len(pos), 384), np.float32)

    def fill(hd, c_off, s_off):
        half = hd // 2
        freqs = (np.float32(10000.0) ** (-np.arange(0, half, 2, dtype=np.float32) / np.float32(half))).astype(np.float32)
        ar = row[:, None] * freqs[None, :]
        ac = col[:, None] * freqs[None, :]
        q = half // 2
        out[:, c_off:c_off + hd] = np.concatenate([np.cos(ar), np.cos(ar), np.cos(ac), np.cos(ac)], 1)
        out[:, s_off:s_off + hd] = np.concatenate([-np.sin(ar), np.sin(ar), -np.sin(ac), np.sin(ac)], 1)
    fill(128, 0, 128)
    fill(64, 256, 320)
    return out


def _consts(flag_f):
    p = np.arange(128, dtype=np.float32)
    cst = np.zeros((128, 8 + 128 + 128 + 64 + 2), np.float32)
    cst[:, 0] = 127 - p
    cst[:, 1] = p
    cst[:, 2] = p + 1
    cst[:, 3] = 128 - p
    cst[:, 4] = 128
    c = p[None, :]
    m = p[:, None]
    cst[:, 8:136] = np.maximum(c - m, 0)
    cst[:, 136:264] = np.maximum(m - c, 0)
    cst[:, 264:328] = np.repeat(128.0 * np.arange(16, dtype=np.float32), 4)[None, :]
    cst[:, 328] = flag_f
    cst[:, 329] = 1.0 - flag_f
    return cst


_PAIR = np.array([0, 4, 1, 5, 2, 6, 3, 7])


def _perm_cols():
    idx = np.arange(INC)
    for off in (O_AQ, O_AG):
        blk = idx[off:off + 512].reshape(8, 64)[_PAIR].reshape(-1)
        idx[off:off + 512] = blk
    return idx


def run(inputs, n_cores, NS, SS, SP, trace=False):
    f32 = lambda a: np.ascontiguousarray(np.asarray(a, dtype=np.float32))
    x_prompt, x_sample = f32(inputs["x_prompt"]), f32(inputs["x_sample"])
    c_prompt, c_sample = f32(inputs["c_prompt"]), f32(inputs["c_sample"])
    half = SP // 2
    cols = _perm_cols()
    w_in_p = f32(inputs["w_in"][0][:, cols])
    rows = np.arange(D)
    rows[512:] = 512 + np.arange(512).reshape(8, 64)[_PAIR].reshape(-1)
    w_out_p = f32(inputs["w_out"][0][rows, :])
    ngT = f32(inputs["norm_g"][0].reshape(KC, 128).T)
    common = {
        "w_ada": f32(inputs["w_ada"][0]), "b_ada": f32(inputs["b_ada"][0][None, :]),
        "w_in": w_in_p, "w_out": w_out_p, "ngT": ngT,
        "lrf": f32(inputs["ret_log_rate_fwd"][0][None, :]), "lrb": f32(inputs["ret_log_rate_bwd"][0][None, :]),
        "gng": f32(inputs["ret_gn_g"][0].reshape(1, 512)), "qg": f32(inputs["q_norm_g"][0][None, :]),
        "kg": f32(inputs["k_norm_g"][0][None, :]),
    }
    in_maps = []
    for c in range(n_cores):
        pj, hf = c // 2, c % 2
        own = np.arange(hf * half, (hf + 1) * half)
        oth = np.arange((1 - hf) * half, (2 - hf) * half)
        xp = np.concatenate([x_prompt[pj, own], x_prompt[pj, oth]], 0)
        tab = np.concatenate([_rope_tab(np.arange(SS)), _rope_tab(own), _rope_tab(oth)], 0)
        cb = np.concatenate([c_sample[c * NS:(c + 1) * NS], c_prompt[pj:pj + 1]], 0)
        cT = cb.T.reshape(KC, 128, NS + 1).transpose(1, 0, 2).reshape(128, KC * (NS + 1))
        m = dict(common)
        m.update({"xs": f32(x_sample[c * NS:(c + 1) * NS].reshape(NS * SS, D)), "xp": f32(xp), "tab": f32(tab),
                  "cT": f32(cT), "cst": _consts(float(hf))})
        in_maps.append(m)
    nc = build_nc(NS, SS, half, half)
    res = run_bass_kernel_spmd(nc, in_maps, core_ids=list(range(n_cores)), trace=trace)
    y_s = np.concatenate([r["ys"].reshape(NS, SS, D) for r in res.results], 0)
    y_p = np.zeros((n_cores // 2, SP, D), np.float32)
    for c in range(n_cores):
        y_p[c // 2, (c % 2) * half:(c % 2 + 1) * half] = res.results[c]["yp"]
    return (y_p, y_s), res


def kernel(x_prompt, x_sample, c_prompt, c_sample, norm_g, w_ada, b_ada, w_in, ret_log_rate_fwd, ret_log_rate_bwd,
           ret_gn_g, q_norm_g, k_norm_g, w_out):
    inputs = dict(x_prompt=x_prompt, x_sample=x_sample, c_prompt=c_prompt, c_sample=c_sample, norm_g=norm_g, w_ada=w_ada,
                  b_ada=b_ada, w_in=w_in, ret_log_rate_fwd=ret_log_rate_fwd, ret_log_rate_bwd=ret_log_rate_bwd,
                  ret_gn_g=ret_gn_g, q_norm_g=q_norm_g, k_norm_g=k_norm_g, w_out=w_out)
    inputs = {k: np.asarray(v) for k, v in inputs.items()}
    (y_p, y_s), _ = run(inputs, 8, 4, 2048, 4096)
    return (y_p.astype(np.float32), y_s.astype(np.float32))
```

```python
import contextlib
import math
import numpy as np
import concourse.bass as bass
import concourse.mybir as mybir
from concourse.bass_utils import run_bass_kernel_spmd

F32 = mybir.dt.float32
BF16 = mybir.dt.bfloat16
AF = mybir.ActivationFunctionType
ALU = mybir.AluOpType
AX = mybir.AxisListType

D = 1024
KC = 8
INC = 3328
EPS = 1e-6
O_RQ, O_RK, O_RV, O_RG, O_AQ, O_AK, O_AV, O_AG = 0, 512, 1024, 1536, 2048, 2560, 2688, 2816

EPOCH = 24000
NDMASEM = 32


class _Op:
    __slots__ = ("eng", "fn", "deps", "idx", "dma", "signal", "sem", "val", "waits", "slot")


class Sched:
    ENGS = ("pe", "act", "dve", "pool", "sp")

    def __init__(self):
        self.ops = []
        self.last_w = {}
        self.readers = {}
        self.ndma = 0
        self.slot_last = {}

    def add(self, eng, fn, reads=(), writes=(), dma=False):
        op = _Op()
        op.eng, op.fn, op.dma = eng, fn, dma
        op.idx = len(self.ops)
        op.signal = False
        deps = {}
        for k in reads:
            w = self.last_w.get(k)
            if w is not None:
                deps[w] = True
        for k in writes:
            w = self.last_w.get(k)
            if w is not None and w not in deps:
                deps[w] = False
            for r in self.readers.get(k, ()):
                if r not in deps:
                    deps[r] = False
        if dma:
            slot = self.ndma % NDMASEM
            self.ndma += 1
            op.slot = slot
            prev = self.slot_last.get(slot)
            if prev is not None and prev not in deps:
                deps[prev] = True
            self.slot_last[slot] = op.idx
        op.deps = deps
        for k in writes:
            self.last_w[k] = op.idx
            self.readers[k] = []
        wset = set(writes)
        for k in reads:
            if k not in wset:
                lst = self.readers.setdefault(k, [])
                if not dma:
                    lst[:] = [r for r in lst if self.ops[r].dma or self.ops[r].eng != eng]
                lst.append(op.idx)
        self.ops.append(op)
        return op.idx

    def _resolve(self):
        ops = self.ops
        waited = {e: {e2: -1 for e2 in self.ENGS} for e in self.ENGS}
        dma_seen = {e: set() for e in self.ENGS}
        for x in ops:
            need = {}
            x.waits = []
            for p_idx, raw in x.deps.items():
                p = ops[p_idx]
                if p.dma:
                    if p_idx not in dma_seen[x.eng]:
                        dma_seen[x.eng].add(p_idx)
                        x.waits.append(p_idx)
                    continue
                if p.eng == x.eng and not raw and not x.dma and x.eng == "pe":
                    continue
                if waited[x.eng][p.eng] >= p_idx:
                    continue
                if p.eng not in need or need[p.eng] < p_idx:
                    need[p.eng] = p_idx
            for e2, p_idx in need.items():
                waited[x.eng][e2] = p_idx
                ops[p_idx].signal = True
                x.waits.append(p_idx)

    def emit(self, nc, ctx):
        self._resolve()
        ops = self.ops
        counts = {e: 0 for e in self.ENGS}
        eng_sems = {}
        ndma = 0
        dma_sems = [ctx.enter_context(nc.semaphore(f"dq{i}")) for i in range(NDMASEM)]
        dma_cnt = [0] * NDMASEM
        for x in ops:
            if x.dma:
                s = x.slot
                dma_cnt[s] += 16
                x.sem, x.val = dma_sems[s], dma_cnt[s]
            elif x.signal:
                c = counts[x.eng]
                key = (x.eng, c // EPOCH)
                if key not in eng_sems:
                    eng_sems[key] = ctx.enter_context(nc.semaphore(f"s_{x.eng}_{key[1]}"))
                x.sem, x.val = eng_sems[key], c % EPOCH + 1
                counts[x.eng] = c + 1
        self.streams = {e: [] for e in self.ENGS}
        for x in ops:
            self.streams[x.eng].append(x)

    def run_stream(self, eng_name, eng):
        ops = self.ops
        for x in self.streams[eng_name]:
            for p_idx in x.waits:
                p = ops[p_idx]
                eng.wait_ge(p.sem, p.val)
            m, kw = x.fn
            if m == "wait_only":
                continue
            ins = getattr(eng, m)(**kw)
            if x.dma:
                ins.then_inc(x.sem, 16)
            elif x.signal:
                ins.then_inc(x.sem, 1)


class Rot:
    def __init__(self, name, n):
        self.name, self.n, self.i = name, n, -1

    def next(self):
        self.i = (self.i + 1) % self.n
        return self.i, f"{self.name}{self.i}"


_STAGE = 99
NG1 = 8
NG2 = 3
N_ST = 4
BT = 2
NQ = BT * 128


def build_nc(NS, SS, SPO, SPX):
    NB = NS + 1
    nc = bass.Bass("TRN2", target_bir_lowering=False)
    dt_in = lambda name, shape: nc.dram_tensor(name, shape, F32, kind="ExternalInput").ap()
    xs_d = dt_in("xs", [NS * SS, D])
    xp_d = dt_in("xp", [SPO + SPX, D])
    tab_d = dt_in("tab", [SS + SPO + SPX, 384])
    cT_d = dt_in("cT", [128, KC * NB])
    wada_d = dt_in("w_ada", [D, 3 * D])
    bada_d = dt_in("b_ada", [1, 3 * D])
    win_d = dt_in("w_in", [D, INC])
    wout_d = dt_in("w_out", [D, D])
    ngT_d = dt_in("ngT", [128, KC])
    lrf_d = dt_in("lrf", [1, 4])
    lrb_d = dt_in("lrb", [1, 4])
    gng_d = dt_in("gng", [1, 512])
    qg_d = dt_in("qg", [1, 64])
    kg_d = dt_in("kg", [1, 64])
    NCST = 8 + 128 + 128 + 64 + 2
    cst_d = dt_in("cst", [128, NCST])
    ys_d = nc.dram_tensor("ys", [NS * SS, D], F32, kind="ExternalOutput").ap()
    yp_d = nc.dram_tensor("yp", [SPO, D], F32, kind="ExternalOutput").ap()
    mod_d = nc.dram_tensor("mod_scratch", [NB, 3 * D], F32, kind="Internal").ap()

    NT_OWN = max(SS, SPO) // 128
    NT_K = max(SS, SPO + SPX) // 128
    S = Sched()

    with contextlib.ExitStack() as ctx:
        sb_bytes = [0]

        def sb(name, shape, dt=F32):
            n = 1
            for d_ in shape[1:]:
                n *= d_
            sb_bytes[0] += ((n * (2 if dt == BF16 else 4) + 31) // 32) * 32
            return ctx.enter_context(nc.sbuf_tensor("s_" + name, shape, dt))

        def ps(name, shape, dt=F32):
            return ctx.enter_context(nc.psum_tensor("p_" + name, shape, dt))

        w_in = sb("w_in_sb", [128, KC, INC], BF16)
        w_out = sb("w_out_sb", [128, KC, D], BF16)
        akT = sb("akT", [128, NT_K * 128], BF16)
        av_ext = sb("av_ext", [128, NT_K, 192], BF16)
        Rfb = sb("Rfb", [128, NT_OWN, 4, 2, 128], BF16)
        ident = sb("ident", [128, 128], BF16)
        identf = sb("identf", [128, 16])
        cst = sb("cst", [128, NCST])
        posc = cst[:, 0:8]
        DPm = cst[:, 8:136]
        DNm = cst[:, 136:264]
        jtab = cst[:, 264:328]
        flags = cst[:, 328:330]
        eps_t = sb("eps_t", [128, 1])
        one_t = sb("one_t", [128, 1])
        lns_t = sb("lns_t", [128, 1])
        lg = sb("lg", [128, 8])
        dec = sb("dec", [128, 6, 4])
        wb = sb("wb", [128, 16, 4])
        mask = sb("mask", [128, 4, 128], BF16)
        gng = sb("gng", [128, 512])
        qg = sb("qg", [128, 64])
        kg = sb("kg", [128, 64])
        ngT = sb("ngT", [128, KC])
        AB = sb("AB", [128, 2, KC, NB])
        cT = sb("cT", [128, KC, NB])
        gate_rep = sb("gate_rep", [128, D])

        xbuf = [sb(f"xbuf{i}", [128, D]) for i in range(2)]
        xs_bf = [sb(f"xsbf{i}", [128, D], BF16) for i in range(1)]
        stat = [sb(f"stat{i}", [128, 16]) for i in range(4)]
        hT2 = [sb(f"hT{i}", [128, KC, NQ], BF16) for i in range(2)]
        hTc = [0]
        tabt = [sb(f"tabt{i}", [128, 384]) for i in range(2)]
        t1 = [sb(f"t1_{i}", [128, 512]) for i in range(2)]
        t2 = [sb(f"t2_{i}", [128, 512]) for i in range(2)]
        t3 = [sb(f"t3_{i}", [128, 512]) for i in range(2)]
        tok_bf = [sb(f"tokbf{i}", [128, 512], BF16) for i in range(4)]
        rqT2 = [sb(f"rqT{i}", [128, 4, 128], BF16) for i in range(2)]
        rkT2 = [sb(f"rkT_t{i}", [128, 4, 128], BF16) for i in range(2)]
        v_bf2 = [sb(f"v_bf{i}", [128, 4, 128], BF16) for i in range(2)]
        sgbuf = [sb(f"sgbuf{i}", [128, 512]) for i in range(2)]
        oabuf = [sb(f"oabuf{i}", [128, 512]) for i in range(2)]
        st_f = oabuf[0][:].rearrange("p (h e) -> p h e", h=4)
        vfb = oabuf[1][:].bitcast(BF16).rearrange("p (h f e) -> p h f e", h=4, f=2)
        st_b = [sgbuf[i][:].rearrange("p (h e) -> p h e", h=4) for i in range(2)]
        xo = [sb(f"xo{i}", [128, D]) for i in range(1)]
        aqT2 = [sb(f"aqT{i}", [128, 4, NQ], BF16) for i in range(2)]
        sgT2 = [sb(f"sgT{i}", [128, 4, NQ], BF16) for i in range(2)]
        pm2 = [sb(f"pm{i}", [128, 4, 128], BF16) for i in range(2)]
        mixT_r2 = [sb(f"mixT_r{i}", [128, 4, NQ], BF16) for i in range(2)]
        mixT_a = sb("mixT_a", [128, 4, NQ], BF16)
        pT = [sb(f"pT{i}", [128, 2, NQ], BF16) for i in range(4)]
        rec = sb("rec", [128, NQ])

        if _STAGE != 99:
            print("SBUF bytes/partition:", sb_bytes[0])
        sT = [ps(f"sT{i}", [128, 2, NQ]) for i in range(4)]
        oT = ps("oT", [128, 512])
        G = [ps(f"G{i}", [128, 512]) for i in range(3)]
        flat = lambda t: t[:].rearrange("p a q -> p (a q)")
        G = [g[:, :] for g in G] + [flat(sT[3]), flat(sT[0]), flat(sT[1]), flat(sT[2]), oT[:, :]]
        GKEYS = ["G0", "G1", "G2", "sT3", "sT0", "sT1", "sT2", "oT"]

        r_x = Rot("xbuf", 2)
        r_xs = Rot("xsbf", 1)
        r_stat = Rot("stat", 4)
        r_tab = Rot("tabt", 2)
        r_t1 = Rot("t1_", 2)
        r_t2 = Rot("t2_", 2)
        r_t3 = Rot("t3_", 2)
        r_tok = Rot("tokbf", 4)
        class RotG:
            def __init__(self):
                self.n, self.i = 3, -1

            def next(self):
                self.i = (self.i + 1) % self.n
                return self.i, GKEYS[self.i]

        r_G = RotG()
        r_sT = Rot("sT", N_ST)
        r_pT = Rot("pT", N_ST)
        store_keys = []

        PSK = {"G0", "G1", "G2", "sT0", "sT1", "sT2", "sT3", "oT"}
        evt = [0]
        REC = [None]
        gcount = [0]
        KALIAS = {"st_f": "oabuf_t0", "vfb": "oabuf_t1", "st_b0": "sgbuf_t0", "st_b1": "sgbuf_t1"}
        r_xo = Rot("xo", 1)

        def I(eng, method, r=(), w=(), dma=False, **kw):
            r = [KALIAS.get(k, k) for k in r]
            w = [KALIAS.get(k, k) for k in w]
            w = list(w) + [k for k in r if k in PSK and k not in w]
            if REC[0] is not None:
                REC[0][-1].append((eng, (method, kw), list(r), w, dma))
            else:
                S.add(eng, (method, kw), reads=list(r), writes=w, dma=dma)

        def bc(ap, shape, axis):
            return ap.unsqueeze(axis).to_broadcast(shape)

        def v3(ap, h):
            return ap.rearrange("p (h d) -> p h d", h=h)

        for _once in (0,):
            I("sp", "dma_start", w=["cst"], dma=True, out=cst[:], in_=cst_d)
            I("sp", "dma_start", w=["lg"], dma=True, out=lg[:, 0:4], in_=lrf_d.partition_broadcast(128))
            I("sp", "dma_start", w=["lg"], dma=True, out=lg[:, 4:8], in_=lrb_d.partition_broadcast(128))
            I("sp", "dma_start", w=["gng"], dma=True, out=gng[:], in_=gng_d.partition_broadcast(128))
            I("sp", "dma_start", w=["qg"], dma=True, out=qg[:], in_=qg_d.partition_broadcast(128))
            I("sp", "dma_start", w=["kg"], dma=True, out=kg[:], in_=kg_d.partition_broadcast(128))
            I("sp", "dma_start", w=["ngT"], dma=True, out=ngT[:], in_=ngT_d)
            I("sp", "dma_start", w=["cT"], dma=True, out=cT[:].rearrange("p k b -> p (k b)"), in_=cT_d)

            I("pool", "memset", w=["eps"], ap=eps_t[:], constant=EPS)
            I("pool", "memset", w=["one"], ap=one_t[:], constant=1.0)
            I("pool", "memset", w=["lns"], ap=lns_t[:], constant=math.log(128.0 ** -0.5))
            I("pool", "memset", w=["av_all"], ap=av_ext[:], constant=1.0)
            I("pool", "memset", w=["t1_0"], ap=t1[0][:, 0:128], constant=1.0)
            I("pool", "affine_select", r=["t1_0"], w=["t1_0"], out=t1[0][:, 0:128], in_=t1[0][:, 0:128], pattern=[[-1, 128]],
              compare_op=ALU.is_equal, fill=0.0, base=0, channel_multiplier=1)
            I("pool", "tensor_copy", r=["t1_0"], w=["ident"], out=ident[:], in_=t1[0][:, 0:128])
            I("pool", "tensor_copy", r=["t1_0"], w=["identf"], out=identf[0:16, :], in_=t1[0][0:16, 0:16])

            if _STAGE < 0.2 and _STAGE < 1:
                break
            I("act", "activation", r=["lg"], w=["lg"], out=lg[:], in_=lg[:], func=AF.Exp)
            I("dve", "tensor_scalar_mul", r=["lg"], w=["lg"], out=lg[:], in0=lg[:], scalar1=-1.0)
            for i, (half, col, use_s) in enumerate([(0, 0, True), (1, 1, True), (0, 2, False), (1, 3, False),
                                                    (0, 4, False), (1, 4, False)]):
                kw = dict(out=dec[:, i, :], in_=lg[:, half * 4:half * 4 + 4], func=AF.Exp, scale=posc[:, col:col + 1])
                if use_s:
                    kw["bias"] = lns_t[:]
                I("act", "activation", r=["lg", "cst", "lns"], w=["dec"], **kw)
            I("dve", "tensor_tensor", r=["cst", "lg"], w=["wb"], out=wb[:], in0=jtab.rearrange("p (j h) -> p j h", h=4),
              in1=bc(lg[:, 4:8], [128, 16, 4], 1), op=ALU.mult)
            I("act", "activation", r=["wb"], w=["wb"], out=wb[:], in_=wb[:], func=AF.Exp)
            for h in range(4):
                hsl = slice(h * 128, (h + 1) * 128)
                I("dve", "tensor_scalar_mul", r=["cst", "lg"], w=["t2_0"], out=t2[0][:, hsl], in0=DPm, scalar1=lg[:, h:h + 1])
                I("dve", "scalar_tensor_tensor", r=["cst", "lg", "t2_0"], w=["t2_0"], out=t2[0][:, hsl], in0=DNm,
                  scalar=lg[:, 4 + h:5 + h], in1=t2[0][:, hsl], op0=ALU.mult, op1=ALU.add)
            I("act", "activation", r=["t2_0", "lns"], w=["mask"], out=mask[:].rearrange("p h c -> p (h c)"), in_=t2[0][:],
              func=AF.Exp, bias=lns_t[:])

            if _STAGE < 0.3 and _STAGE < 1:
                break
            NSTG = NT_OWN * 512
            STG = Rfb[:].rearrange("p n h f e -> p (n h f e)").bitcast(F32)
            PW = min(INC, NSTG // 2)
            slot_i = [0]

            def stage_slot(width):
                i = slot_i[0] % 2
                slot_i[0] += 1
                return STG[:, i * (NSTG // 2):i * (NSTG // 2) + width], f"stg{i}"

            ckeys = []
            for k in range(KC):
                for c0 in range(0, INC, PW):
                    c1 = min(INC, c0 + PW)
                    st, stk = stage_slot(c1 - c0)
                    I("sp", "dma_start", w=[stk], dma=True, out=st, in_=win_d[k * 128:(k + 1) * 128, c0:c1])
                    wd = c1 - c0
                    cuts = [0, (wd // 3) // 64 * 64, (2 * wd // 3) // 64 * 64, wd]
                    for ei, eng in enumerate(("dve", "pool", "act")):
                        lo, hi = cuts[ei], cuts[ei + 1]
                        if hi <= lo:
                            continue
                        ck = f"w_in_c{len(ckeys)}"
                        ckeys.append(ck)
                        if eng == "act":
                            I("act", "activation", r=[stk], w=[ck], out=w_in[:, k, c0 + lo:c0 + hi], in_=st[:, lo:hi], func=AF.Copy)
                        else:
                            I(eng, "tensor_copy", r=[stk], w=[ck], out=w_in[:, k, c0 + lo:c0 + hi], in_=st[:, lo:hi])
            I("dve", "memset", r=ckeys, w=["w_in"], ap=stat[0][:, 15:16], constant=0.0)
            ckeys = []
            for k in range(KC):
                st, stk = stage_slot(D)
                I("sp", "dma_start", w=[stk], dma=True, out=st, in_=wout_d[k * 128:(k + 1) * 128, :])
                for ei, eng in enumerate(("dve", "pool")):
                    ck = f"w_out_c{len(ckeys)}"
                    ckeys.append(ck)
                    I(eng, "tensor_copy", r=[stk], w=[ck], out=w_out[:, k, ei * 512:(ei + 1) * 512], in_=st[:, ei * 512:(ei + 1) * 512])
            I("dve", "memset", r=ckeys, w=["w_out"], ap=stat[0][:, 14:15], constant=0.0)

            if _STAGE < 0.4 and _STAGE < 1:
                break
            cTf = cT[:].rearrange("p k b -> p (k b)")
            nkb = KC * NB
            I("act", "activation", r=["cT"], w=["rec0", "rec1"], out=rec[:, 0:nkb], in_=cTf, func=AF.Exp, scale=-1.0)
            I("dve", "tensor_scalar_add", r=["rec0", "rec1"], w=["rec0", "rec1"], out=rec[:, 0:nkb], in0=rec[:, 0:nkb], scalar1=1.0)
            I("dve", "reciprocal", r=["rec0", "rec1"], w=["rec0", "rec1"], out=rec[:, 0:nkb], in_=rec[:, 0:nkb])
            I("dve", "tensor_tensor", r=["rec0", "rec1", "cT"], w=["cT"], out=cTf, in0=cTf, in1=rec[:, 0:nkb], op=ALU.mult)
            segs = [(t1[0], "t1_0"), (t1[1], "t1_1"), (t2[0], "t2_0"), (t2[1], "t2_1"), (t3[0], "t3_0"), (t3[1], "t3_1")]
            for cc in range(6):
                seg, segk = segs[cc]
                I("sp", "dma_start", r=["ident", "mask"], w=[segk], dma=True, out=seg[0:NB, :],
                  in_=bada_d[:, cc * 512:(cc + 1) * 512].partition_broadcast(NB))
                gi, gk = r_G.next()
                kper = min(KC, (NSTG // 2) // 512)
                for k0 in range(0, KC, kper):
                    st, stk = stage_slot(kper * 512)
                    st3 = st.rearrange("p (k c) -> p k c", k=kper)
                    I("sp", "dma_start", w=[stk], dma=True, out=st3,
                      in_=wada_d[k0 * 128:(k0 + kper) * 128, cc * 512:(cc + 1) * 512].rearrange("(k p) c -> p k c", p=128))
                    for kk in range(kper):
                        k = k0 + kk
                        I("pe", "matmul", r=[stk, "cT"], w=[gk], out=G[gi][0:NB, :], lhsT=cT[:, k, :], rhs=st3[:, kk, :],
                          start=(k == 0), stop=(k == KC - 1))
                I("dve", "tensor_tensor", r=[gk, segk], w=[segk], out=seg[0:NB, :], in0=G[gi][0:NB, :], in1=seg[0:NB, :], op=ALU.add)
                I("sp", "dma_start", r=[segk], w=["mod_d"], dma=True, out=mod_d[:, cc * 512:(cc + 1) * 512], in_=seg[0:NB, :])
                if cc < 4:
                    gi2, gk2 = r_G.next()
                    for j in range(4):
                        I("pe", "transpose", r=[segk, "identf"], w=[gk2], out=G[gi2][:, j * NB:(j + 1) * NB],
                          in_=seg[0:NB, j * 128:(j + 1) * 128], identity=identf[0:NB, 0:NB])
                    for j in range(4):
                        kk = cc * 4 + j
                        I("dve", "tensor_copy", r=[gk2], w=["AB"], out=AB[:, kk // 8, kk % 8, :], in_=G[gi2][:, j * NB:(j + 1) * NB])
            if _STAGE < 0.5 and _STAGE < 1:
                break
            I("dve", "tensor_scalar_add", r=["AB"], w=["AB"], out=AB[:, 1], in0=AB[:, 1], scalar1=1.0)
            I("dve", "tensor_tensor", r=["AB", "ngT"], w=["AB"], out=AB[:, 1], in0=AB[:, 1], in1=bc(ngT[:], [128, KC, NB], 2),
              op=ALU.mult)

        xseq = []
        xpos = [0]

        def next_x_load(t):
            k = xpos[0] + (1 if t == BT - 1 else 0)
            nxt = xpos[0] + 1
            if t == BT - 1:
                xpos[0] += 1
            if nxt < len(xseq):
                src, r0 = xseq[nxt]
                I("sp", "dma_start", w=[f"xbuf{t}"], dma=True, out=xbuf[t][:], in_=src[r0 + t * 128:r0 + (t + 1) * 128, :])

        def front(x_src, row0, ntile, b):
            for t in range(ntile):
                xi, xk = t, f"xbuf{t}"
                si, sk = r_xs.next()
                ti, tk = r_stat.next()
                I("act", "activation", r=[xk], w=[sk, tk], out=xs_bf[si][:], in_=xbuf[xi][:], func=AF.Square,
                  accum_out=stat[ti][:, 0:1])
                I("act", "activation", r=[tk, "eps"], w=[tk], out=stat[ti][:, 1:2], in_=stat[ti][:, 0:1], func=AF.Ln,
                  scale=1.0 / D, bias=eps_t[:])
                I("act", "activation", r=[tk], w=[tk], out=stat[ti][:, 2:3], in_=stat[ti][:, 1:2], func=AF.Exp, scale=-0.5)
                I("act", "activation", r=[xk, tk], w=[sk], out=xs_bf[si][:], in_=xbuf[xi][:], func=AF.Copy, scale=stat[ti][:, 2:3])
                next_x_load(t)
                banks = [r_G.next(), r_G.next()]
                for k in range(KC):
                    gi, gk = banks[k // 4]
                    gbf = G[gi][:].bitcast(BF16)
                    I("pe", "transpose", r=[sk, "ident"], w=[gk], out=gbf[:, (k % 4) * 128:(k % 4 + 1) * 128],
                      in_=xs_bf[si][:, k * 128:(k + 1) * 128], identity=ident[:])
                for k in range(KC):
                    gi, gk = banks[k // 4]
                    gbf = G[gi][:].bitcast(BF16)
                    if k // 4 == 0:
                        I("act", "activation", r=[gk, "AB"], w=[f"hT{hTc[0]}"], out=hT2[hTc[0]][:, k, t * 128:(t + 1) * 128],
                          in_=gbf[:, (k % 4) * 128:(k % 4 + 1) * 128], func=AF.Identity, scale=AB[:, 1, k, b:b + 1], bias=AB[:, 0, k, b:b + 1])
                    else:
                        I("dve", "tensor_scalar", r=[gk, "AB"], w=[f"hT{hTc[0]}"], out=hT2[hTc[0]][:, k, t * 128:(t + 1) * 128],
                          in0=gbf[:, (k % 4) * 128:(k % 4 + 1) * 128], scalar1=AB[:, 1, k, b:b + 1], scalar2=AB[:, 0, k, b:b + 1],
                          op0=ALU.mult, op1=ALU.add)

        def front_g(x_src, row0, ntile, b):
            for t in range(ntile):
                xi, xk = t, f"xbuf{t}"
                si, sk = r_xs.next()
                ti, tk = r_stat.next()
                I("act", "activation", r=[xk], w=[sk, tk], out=xs_bf[si][:], in_=xbuf[xi][:], func=AF.Square,
                  accum_out=stat[ti][:, 0:1])
                I("act", "activation", r=[tk, "eps"], w=[tk], out=stat[ti][:, 1:2], in_=stat[ti][:, 0:1], func=AF.Ln,
                  scale=1.0 / D, bias=eps_t[:])
                I("act", "activation", r=[tk], w=[tk], out=stat[ti][:, 2:3], in_=stat[ti][:, 1:2], func=AF.Exp, scale=-0.5)
                I("act", "activation", r=[xk, tk], w=[sk], out=xs_bf[si][:], in_=xbuf[xi][:], func=AF.Copy, scale=stat[ti][:, 2:3])
                next_x_load(t)
                banks = [r_G.next(), r_G.next()]
                for k in range(KC):
                    gi, gk = banks[k // 4]
                    gbf = G[gi][:].bitcast(BF16)
                    I("pe", "transpose", r=[sk, "ident"], w=[gk], out=gbf[:, (k % 4) * 128:(k % 4 + 1) * 128],
                      in_=xs_bf[si][:, k * 128:(k + 1) * 128], identity=ident[:])
                for k in range(KC):
                    gi, gk = banks[k // 4]
                    gbf = G[gi][:].bitcast(BF16)
                    if k // 4 == 0:
                        I("act", "activation", r=[gk, "AB"], w=[f"hT{hTc[0]}"], out=hT2[hTc[0]][:, k, t * 128:(t + 1) * 128],
                          in_=gbf[:, (k % 4) * 128:(k % 4 + 1) * 128], func=AF.Identity, scale=AB[:, 1, k, b:b + 1], bias=AB[:, 0, k, b:b + 1])
                    else:
                        I("dve", "tensor_scalar", r=[gk, "AB"], w=[f"hT{hTc[0]}"], out=hT2[hTc[0]][:, k, t * 128:(t + 1) * 128],
                          in0=gbf[:, (k % 4) * 128:(k % 4 + 1) * 128], scalar1=AB[:, 1, k, b:b + 1], scalar2=AB[:, 0, k, b:b + 1],
                          op0=ALU.mult, op1=ALU.add)
                yield

        def proj_tok(t, c0, ncol):
            gi, gk = r_G.next()
            for k in range(KC):
                I("pe", "matmul", r=[f"hT{hTc[0]}", "w_in"], w=[gk], out=G[gi][:, 0:ncol], lhsT=hT2[hTc[0]][:, k, t * 128:(t + 1) * 128],
                  rhs=w_in[:, k, c0:c0 + ncol], start=(k == 0), stop=(k == KC - 1))
            return gi, gk

        def load_tab(row):
            ti, tk = r_tab.next()
            I("sp", "dma_start", w=[tk], dma=True, out=tabt[ti][:], in_=tab_d[row:row + 128, :])
            return ti, tk

        def rope(src, src_keys, src_psum, nh, hd, cosv, sinv, tkey, out_ap, out_key):
            q = hd // 4
            a, ak_ = r_t1.next()
            bq, bk_ = r_t2.next()
            n = nh * hd
            I("dve", "tensor_tensor", r=list(src_keys) + [tkey], w=[ak_],
              out=v3(t1[a][:, 0:n], nh), in0=v3(src, nh), in1=bc(cosv, [128, nh, hd], 1), op=ALU.mult)
            s5 = src.rearrange("p (h f j i) -> p h f j i", h=nh, f=2, j=2, i=q)
            o5 = t2[bq][:, 0:n].rearrange("p (h f j i) -> p h f j i", h=nh, f=2, j=2, i=q)
            sn4 = sinv.rearrange("p (f j i) -> p f j i", f=2, j=2, i=q)
            for j in range(2):
                I("dve", "tensor_tensor", r=list(src_keys) + [tkey], w=[bk_],
                  out=o5[:, :, :, j, :], in0=s5[:, :, :, 1 - j, :], in1=bc(sn4[:, :, j, :], [128, nh, 2, q], 1), op=ALU.mult)
            I("dve", "tensor_tensor", r=[ak_, bk_], w=[out_key], out=out_ap, in0=t1[a][:, 0:n], in1=t2[bq][:, 0:n], op=ALU.add)

        def head_rstd(src_ap, src_key, nh, hd, src_psum=True):
            a, ak_ = r_t3.next()
            n = nh * hd
            ti, tk = r_stat.next()
            I("act", "activation", r=[src_key], w=[ak_], out=t3[a][:, 0:n], in_=src_ap, func=AF.Square)
            I("dve", "tensor_reduce", r=[ak_], w=[tk], out=stat[ti][:, 4:4 + nh], in_=v3(t3[a][:, 0:n], nh), axis=AX.X, op=ALU.add)
            I("act", "activation", r=[tk, "eps"], w=[tk], out=stat[ti][:, 4:4 + nh], in_=stat[ti][:, 4:4 + nh], func=AF.Ln,
              scale=1.0 / hd, bias=eps_t[:])
            I("act", "activation", r=[tk], w=[tk], out=stat[ti][:, 4:4 + nh], in_=stat[ti][:, 4:4 + nh], func=AF.Exp, scale=-0.5)
            return stat[ti][:, 4:4 + nh], tk

        def qk_norm_rope(gi, gk, nh, gain, gain_key, ti, tk, out_ap, out_key):
            n = nh * 64
            src = G[gi][:, 0:n]
            rs, rsk = head_rstd(src, gk, nh, 64)
            a, ak_ = r_t3.next()
            I("dve", "tensor_tensor", r=[gk, rsk], w=[ak_], out=v3(t3[a][:, 0:n], nh), in0=v3(src, nh),
              in1=bc(rs, [128, nh, 64], 2), op=ALU.mult)
            I("dve", "tensor_tensor", r=[ak_, gain_key], w=[ak_], out=v3(t3[a][:, 0:n], nh), in0=v3(t3[a][:, 0:n], nh),
              in1=bc(gain, [128, nh, 64], 1), op=ALU.mult)
            rope(t3[a][:, 0:n], [ak_], False, nh, 64, tabt[ti][:, 256:320], tabt[ti][:, 320:384], tk, out_ap, out_key)

        def transpose_to(src_bf, src_key, nblk, dst_fn, dst_key, evac_engs=("act", "dve")):
            gi, gk = r_G.next()
            gbf = G[gi][:].bitcast(BF16)
            for i in range(nblk):
                I("pe", "transpose", r=[src_key, "ident"], w=[gk], out=gbf[:, i * 128:(i + 1) * 128],
                  in_=src_bf[:, i * 128:(i + 1) * 128], identity=ident[:])
            evt[0] += 1
            for i in range(nblk):
                eng = evac_engs[evt[0] % len(evac_engs)]
                if eng == "act":
                    I("act", "activation", r=[gk], w=[dst_key], out=dst_fn(i), in_=gbf[:, i * 128:(i + 1) * 128], func=AF.Copy)
                else:
                    I("dve", "tensor_copy", r=[gk], w=[dst_key], out=dst_fn(i), in_=gbf[:, i * 128:(i + 1) * 128])

        def ret_k(t, ti, tk):
            gi, gk = proj_tok(t, O_RK, 512)
            ri, rkk = r_tok.next()
            rope(G[gi][:, :], [gk], True, 4, 128, tabt[ti][:, 0:128], tabt[ti][:, 128:256], tk, tok_bf[ri][:], rkk)
            return ri, rkk

        def g_ktile(t, tp, tab_row, kslot, own_n, other_j):
            ti, tk = tp, f"tabt{tp}"
            I("sp", "dma_start", w=[tk], dma=True, out=tabt[ti][:], in_=tab_d[tab_row:tab_row + 128, :])
            gi, gk = proj_tok(t, O_RK, 512)
            ri, rkk = tp * 2, f"tokbf{tp * 2}"
            rope(G[gi][:, :], [gk], True, 4, 128, tabt[ti][:, 0:128], tabt[ti][:, 128:256], tk, tok_bf[ri][:], rkk)
            yield
            gi, gk = proj_tok(t, O_RV, 512)
            g4 = v3(G[gi][:, :], 4)
            for d_ in range(2):
                I("dve", "tensor_tensor", r=[gk, "dec"], w=["vfb"], out=vfb[:, :, d_, :], in0=g4,
                  in1=bc(dec[:, d_, :], [128, 4, 128], 2), op=ALU.mult)
            sg_ = []
            for hp in range(2):
                g2, g2k = r_G.next()
                for hh in range(2):
                    h = hp * 2 + hh
                    I("pe", "matmul", r=[rkk, "vfb"], w=[g2k], out=G[g2][:, hh * 256:(hh + 1) * 256],
                      lhsT=tok_bf[ri][:, h * 128:(h + 1) * 128], rhs=vfb[:, h].rearrange("p f e -> p (f e)"), start=True, stop=True)
                sg_.append((g2, g2k))
            for hp, (g2, g2k) in enumerate(sg_):
                S4 = G[g2][:, :].rearrange("p (h f e) -> p h f e", h=2, f=2)
                hs = slice(hp * 2, hp * 2 + 2)
                if own_n is not None:
                    I("dve", "tensor_copy", r=["st_f"], w=[f"Rf{own_n}"], out=Rfb[:, own_n, hs, 0, :], in_=st_f[:, hs, :])
                    I("act", "activation", r=[g2k], w=[f"Rb{own_n}"], out=Rfb[:, own_n, hs, 1, :], in_=S4[:, :, 1, :], func=AF.Copy)
                else:
                    for hh in range(2):
                        h = hp * 2 + hh
                        I("dve", "scalar_tensor_tensor", r=[g2k, "wb", "st_b0"], w=["st_b0"], out=st_b[0][:, h, :], in0=S4[:, hh, 1, :],
                          scalar=wb[:, other_j, h:h + 1], in1=st_b[0][:, h, :], op0=ALU.mult, op1=ALU.add)
                for hh in range(2):
                    h = hp * 2 + hh
                    I("dve", "scalar_tensor_tensor", r=[g2k, "st_f", "dec"], w=["st_f"], out=st_f[:, h, :], in0=st_f[:, h, :],
                      scalar=dec[:, 4, h:h + 1], in1=S4[:, hh, 0, :], op0=ALU.mult, op1=ALU.add)
            yield
            gi, gk = proj_tok(t, O_AK, 256)
            ai, akk = tp * 2 + 1, f"tokbf{tp * 2 + 1}"
            qk_norm_rope(gi, gk, 2, kg[:], "kg", ti, tk, tok_bf[ai][:, 0:128], akk)
            I("act", "activation", r=[gk, "av_all"], w=[f"av{kslot}"], out=av_ext[:, kslot, 0:64], in_=G[gi][:, 128:192], func=AF.Copy)
            I("act", "activation", r=[gk, "av_all"], w=[f"av{kslot}"], out=av_ext[:, kslot, 128:192], in_=G[gi][:, 192:256], func=AF.Copy)
            yield
            transpose_to(tok_bf[ai], akk, 1, lambda i: akT[:, kslot * 128:(kslot + 1) * 128], f"akT{kslot}", evac_engs=("dve",))
            yield

        def g_FPhead(x_src, row0, ntile, b, par):
            nq = ntile * 128
            sgT = sgT2[par]
            P = f"_{par}"
            for _ in front_g(x_src, row0, ntile, b):
                yield
            for i in range(4):
                gi, gk = r_G.next()
                for k in range(KC):
                    I("pe", "matmul", r=[f"hT{hTc[0]}", "w_in"], w=[gk], out=G[gi][:, 0:nq], lhsT=w_in[:, k, O_AG + i * 128:O_AG + (i + 1) * 128],
                      rhs=hT2[hTc[0]][:, k, 0:nq], start=(k == 0), stop=(k == KC - 1))
                a, ak_ = r_t3.next()
                I("act", "activation", r=[gk], w=[ak_], out=t3[a][:, 0:nq], in_=G[gi][:, 0:nq], func=AF.Exp, scale=-1.0)
                I("act", "activation", r=[ak_, "one"], w=[ak_], out=t3[a][:, 0:nq], in_=t3[a][:, 0:nq], func=AF.Ln, bias=one_t[:])
                I("act", "activation", r=[ak_], w=[ak_], out=t3[a][:, 0:nq], in_=t3[a][:, 0:nq], func=AF.Exp, scale=-1.0)
                I("dve", "tensor_tensor", r=[gk, ak_], w=[f"sgT{i}" + P], out=sgT[:, i, 0:nq], in0=G[gi][:, 0:nq], in1=t3[a][:, 0:nq],
                  op=ALU.mult)
                yield

        def g_FPtile(t, tp, tab_row, n, par):
            aqT, mixT_r = aqT2[par], mixT_r2[par]
            P = f"_{par}"
            T = f"_t{tp}"
            rqT, rkT, v_bf, pm = rqT2[tp], rkT2[tp], v_bf2[tp], pm2[tp]
            sgb, oab = sgbuf[tp], oabuf[tp]
            ts_ = slice(t * 128, (t + 1) * 128)
            ti, tk = tp, f"tabt{tp}"
            I("sp", "dma_start", w=[tk], dma=True, out=tabt[ti][:], in_=tab_d[tab_row:tab_row + 128, :])
            tk0, tk1 = f"tokbf{tp * 2}", f"tokbf{tp * 2 + 1}"
            tb0, tb1 = tok_bf[tp * 2], tok_bf[tp * 2 + 1]
            gi, gk = proj_tok(t, O_RQ, 512)
            rope(G[gi][:, :], [gk], True, 4, 128, tabt[ti][:, 0:128], tabt[ti][:, 128:256], tk, tb0[:], tk0)
            yield
            transpose_to(tb0, tk0, 4, lambda i: rqT[:, i, :], "rqT" + T)
            gi, gk = proj_tok(t, O_RK, 512)
            rope(G[gi][:, :], [gk], True, 4, 128, tabt[ti][:, 0:128], tabt[ti][:, 128:256], tk, tb1[:], tk1)
            yield
            transpose_to(tb1, tk1, 4, lambda i: rkT[:, i, :], "rkT" + T)
            gi, gk = proj_tok(t, O_RV, 512)
            I("act", "activation", r=[gk], w=["v_bf" + T], out=v_bf[:].rearrange("p h e -> p (h e)"), in_=G[gi][:, :], func=AF.Copy)
            gi, gk = proj_tok(t, O_AQ, 512)
            qk_norm_rope(gi, gk, 8, qg[:], "qg", ti, tk, tb0[:], tk0)
            yield
            transpose_to(tb0, tk0, 4, lambda i: aqT[:, i, ts_], "aqT" + P)
            gi, gk = proj_tok(t, O_RG, 512)
            a, ak_ = r_t3.next()
            sgk = "sgbuf" + T
            I("act", "activation", r=[gk], w=[ak_], out=t3[a][:], in_=G[gi][:, :], func=AF.Exp, scale=-1.0)
            I("act", "activation", r=[ak_, "one"], w=[ak_], out=t3[a][:], in_=t3[a][:], func=AF.Ln, bias=one_t[:])
            I("act", "activation", r=[ak_], w=[ak_], out=t3[a][:], in_=t3[a][:], func=AF.Exp, scale=-1.0)
            I("dve", "tensor_tensor", r=[gk, ak_], w=[sgk], out=sgb[:], in0=G[gi][:, :], in1=t3[a][:], op=ALU.mult)
            I("dve", "tensor_tensor", r=[sgk, "gng"], w=[sgk], out=sgb[:], in0=sgb[:], in1=gng[:], op=ALU.mult)
            gi, gk = r_G.next()
            for h in range(4):
                I("pe", "matmul", r=["rkT" + T, "rqT" + T], w=[gk], out=G[gi][:, h * 128:(h + 1) * 128], lhsT=rkT[:, h, :], rhs=rqT[:, h, :],
                  start=True, stop=True)
            I("dve", "tensor_tensor", r=[gk, "mask"], w=["pm" + T], out=pm[:].rearrange("p h c -> p (h c)"), in0=G[gi][:, :],
              in1=mask[:].rearrange("p h c -> p (h c)"), op=ALU.mult)
            yield
            oak = "oabuf" + T
            crs = []
            for hp in range(2):
                g2, g2k = r_G.next()
                for hh in range(2):
                    h = hp * 2 + hh
                    I("pe", "matmul", r=["rqT" + T, f"Rf{n}", f"Rb{n}"], w=[g2k], out=G[g2][:, hh * 256:(hh + 1) * 256],
                      lhsT=rqT[:, h, :], rhs=Rfb[:, n, h].rearrange("p f e -> p (f e)"), start=True, stop=True)
                crs.append((g2, g2k))
            gi, gk = r_G.next()
            for h in range(4):
                I("pe", "matmul", r=["pm" + T, "v_bf" + T], w=[gk], out=G[gi][:, h * 128:(h + 1) * 128], lhsT=pm[:, h, :], rhs=v_bf[:, h, :],
                  start=True, stop=True)
            u, uk = r_t3.next()
            for hp, (g2, g2k) in enumerate(crs):
                C4 = G[g2][:, :].rearrange("p (h f e) -> p h f e", h=2, f=2)
                hs = slice(hp * 2, hp * 2 + 2)
                o3 = v3(oab[:, hp * 256:(hp + 1) * 256], 2)
                u3 = v3(t3[u][:, hp * 256:(hp + 1) * 256], 2)
                I("dve", "tensor_tensor", r=[g2k, "dec"], w=[oak], out=o3, in0=C4[:, :, 0, :],
                  in1=bc(dec[:, 2, hs], [128, 2, 128], 2), op=ALU.mult)
                I("dve", "tensor_tensor", r=[g2k, "dec"], w=[uk], out=u3, in0=C4[:, :, 1, :],
                  in1=bc(dec[:, 3, hs], [128, 2, 128], 2), op=ALU.mult)
            I("dve", "tensor_tensor", r=[oak, uk], w=[oak], out=oab[:], in0=oab[:], in1=t3[u][:], op=ALU.add)
            I("dve", "tensor_tensor", r=[gk, oak], w=[oak], out=oab[:], in0=G[gi][:, :], in1=oab[:], op=ALU.add)
            sq, sqk = r_t3.next()
            st_i, stk = r_stat.next()
            I("dve", "tensor_tensor", r=[oak], w=[sqk], out=t3[sq][:], in0=oab[:], in1=oab[:], op=ALU.mult)
            I("dve", "tensor_reduce", r=[sqk], w=[stk], out=stat[st_i][:, 4:8], in_=v3(t3[sq][:], 4), axis=AX.X, op=ALU.add)
            I("act", "activation", r=[stk, "eps"], w=[stk], out=stat[st_i][:, 4:8], in_=stat[st_i][:, 4:8], func=AF.Ln,
              scale=1.0 / 128, bias=eps_t[:])
            I("act", "activation", r=[stk], w=[stk], out=stat[st_i][:, 4:8], in_=stat[st_i][:, 4:8], func=AF.Exp, scale=-0.5)
            I("dve", "tensor_tensor", r=[oak, stk], w=[oak], out=v3(oab[:], 4), in0=v3(oab[:], 4),
              in1=bc(stat[st_i][:, 4:8], [128, 4, 128], 2), op=ALU.mult)
            I("dve", "tensor_tensor", r=[oak, sgk], w=[tk1], out=tb1[:], in0=oab[:], in1=sgb[:], op=ALU.mult)
            yield
            transpose_to(tb1, tk1, 4, lambda i: mixT_r[:, i, ts_], "mixT_r" + P)
            yield

        def g_AT(ntile, nkt, par):
            LA = N_ST - 1
            nq = ntile * 128
            aqT, sgT = aqT2[par], sgT2[par]
            P = f"_{par}"
            its = [(g, ip, j) for g in range(2) for ip in range(2) for j in range(nkt)]

            def qk(it):
                g, ip, j = it
                rows = slice(g * 64, (g + 1) * 64)
                si, sk = r_sT.next()
                I("pe", "matmul", r=[f"akT{j}", "aqT" + P], w=[sk], out=sT[si][:, :, 0:nq], lhsT=akT[rows, j * 128:(j + 1) * 128],
                  rhs=aqT[rows, 2 * ip:2 * ip + 2, 0:nq], start=True, stop=True)
                pi, pk = r_pT.next()
                I("act", "activation", r=[sk], w=[pk], out=pT[pi][:, :, 0:nq], in_=sT[si][:, :, 0:nq], func=AF.Exp, scale=0.125)
                return pi, pk

            pend = [qk(its[k]) for k in range(min(LA, len(its)))]
            for n_it, (g, ip, j) in enumerate(its):
                pi, pk = pend.pop(0)
                if n_it + LA < len(its):
                    pend.append(qk(its[n_it + LA]))
                rows = slice(g * 64, (g + 1) * 64)
                orow = slice((1 - g) * 64, (2 - g) * 64)
                ext = slice(g * 64, g * 64 + 128)
                o3 = oT[:, :].rearrange("p (a q) -> p a q", a=2)
                I("pe", "matmul", r=[pk, f"av{j}"], w=["oT"], out=o3[:, :, 0:nq], lhsT=av_ext[:, j, ext], rhs=pT[pi][:, :, 0:nq],
                  start=(j == 0), stop=(j == nkt - 1))
                yield
                if j == nkt - 1:
                    a, ak_ = r_t1.next()
                    r3 = t1[a][:, :].rearrange("p (a q) -> p a q", a=2)
                    I("dve", "reciprocal", r=["oT"], w=[ak_], out=r3[orow, :, 0:nq], in_=o3[orow, :, 0:nq])
                    I("dve", "tensor_tensor", r=["oT", ak_], w=[ak_], out=r3[rows, :, 0:nq], in0=o3[rows, :, 0:nq],
                      in1=r3[orow, :, 0:nq], op=ALU.mult)
                    I("dve", "tensor_tensor", r=[ak_, f"sgT{2 * ip}" + P, f"sgT{2 * ip + 1}" + P], w=["mixT_a"],
                      out=mixT_a[rows, 2 * ip:2 * ip + 2, 0:nq], in0=r3[rows, :, 0:nq], in1=sgT[rows, 2 * ip:2 * ip + 2, 0:nq], op=ALU.mult)
                    yield

        def g_O(x_src, row0, ntile, y_dst, par):
            mixT_r = mixT_r2[par]
            P = f"_{par}"
            for t in range(ntile):
                ts_ = slice(t * 128, (t + 1) * 128)
                xi, xk = r_xo.next()
                I("sp", "dma_start", w=[xk], dma=True, out=xo[xi][:], in_=x_src[row0 + t * 128:row0 + (t + 1) * 128, :])
                for half in range(2):
                    gi, gk = r_G.next()
                    cs = slice(half * 512, (half + 1) * 512)
                    for k in range(KC):
                        lhs = mixT_r[:, k, ts_] if k < 4 else mixT_a[:, k - 4, ts_]
                        I("pe", "matmul", r=["mixT_r" + P, "mixT_a", "w_out"], w=[gk], out=G[gi][:, :], lhsT=lhs, rhs=w_out[:, k, cs],
                          start=(k == 0), stop=(k == KC - 1))
                    a, ak_ = r_t1.next()
                    I("dve", "tensor_tensor", r=[gk, "gate_rep"], w=[ak_], out=t1[a][:], in0=G[gi][:, :], in1=gate_rep[:, cs],
                      op=ALU.mult)
                    I("dve", "tensor_tensor", r=[ak_, xk], w=[xk], out=xo[xi][:, cs], in0=xo[xi][:, cs], in1=t1[a][:],
                      op=ALU.add)
                skey = f"YOUT{len(store_keys)}"
                store_keys.append(skey)
                I("sp", "dma_start", r=[xk], w=[skey], dma=True, out=y_dst[row0 + t * 128:row0 + (t + 1) * 128, :], in_=xo[xi][:])
                yield

        def record(gen):
            REC[0] = [[]]
            for _ in gen:
                REC[0].append([])
            chunks = [c for c in REC[0] if c]
            REC[0] = None
            return chunks

        def replay(chunks):
            for c in chunks:
                for (eng, fn, r, w, dma) in c:
                    S.add(eng, fn, reads=r, writes=w, dma=dma)

        def split_pe(chunks):
            out = []
            for c in chunks:
                cur, cur_pe = [], None
                for op in c:
                    is_pe = op[0] == "pe"
                    if cur and is_pe and not cur_pe:
                        out.append(cur)
                        cur = []
                    if cur and (not is_pe) and cur_pe:
                        out.append(cur)
                        cur = []
                    cur.append(op)
                    cur_pe = is_pe
                if cur:
                    out.append(cur)
            return out

        def zip_chunks(a, b):
            out = []
            for i in range(max(len(a), len(b))):
                if i < len(a):
                    out.append(a[i])
                if i < len(b):
                    out.append(b[i])
            return out

        def chunk_cost(c):
            t = 0.0
            prev = None
            for (eng, (m, kw), r, w, dma) in c:
                n = 1
                o = kw.get("out", kw.get("ap"))
                if o is not None:
                    for d_ in o.shape[1:]:
                        n *= d_
                if eng == "pe":
                    t += 0.08 + n / 1800.0
                    continue
                if dma:
                    t += 2.0
                elif eng == "act":
                    t += 0.25 + n / 1300.0
                elif eng == "dve":
                    t += 0.12 + n / 900.0
                else:
                    t += 0.15 + n / 480.0
                if prev is not None and prev != eng:
                    t += 0.3
                prev = eng
            return t

        def merge_weighted(main, other):
            if not other:
                return list(main)
            wts = [chunk_cost(c) for c in other]
            tot = sum(wts) or 1.0
            out = []
            mi = 0
            cum = 0.0
            for c, wv in zip(other, wts):
                out.append(c)
                cum += wv
                tgt = int(round(cum / tot * len(main)))
                while mi < tgt and mi < len(main):
                    out.append(main[mi])
                    mi += 1
            out.extend(main[mi:])
            return out

        def merge_even(main, other):
            if not other:
                return list(main)
            out = []
            acc = 0.0
            oi = 0
            ratio = len(other) / float(len(main))
            for c in main:
                out.append(c)
                acc += ratio
                while acc >= 1.0 and oi < len(other):
                    out.append(other[oi])
                    oi += 1
                    acc -= 1.0
            out.extend(other[oi:])
            return out

        jobs = []
        for j in range(NS):
            jobs.append(dict(x=xs_d, row0=j * SS, tab0=0, n_own=SS // 128, n_oth=0, b=j, y=ys_d))
        jobs.append(dict(x=xp_d, row0=0, tab0=SS, n_own=SPO // 128, n_oth=SPX // 128, b=NS, y=yp_d))
        for jb in jobs:
            n_own, n_oth = jb["n_own"], jb["n_oth"]
            for b0 in range(0, n_oth, BT):
                xseq.append((jb["x"], jb["row0"] + (n_own + b0) * 128))
            for b0 in range(0, n_own, BT):
                xseq.append((jb["x"], jb["row0"] + b0 * 128))
            for b0 in range(0, n_own, BT):
                xseq.append((jb["x"], jb["row0"] + b0 * 128))
        for t in range(BT):
            I("sp", "dma_start", w=[f"xbuf{t}"], dma=True, out=xbuf[t][:], in_=xseq[0][0][xseq[0][1] + t * 128:xseq[0][1] + (t + 1) * 128, :])
        for jb in (jobs if _STAGE >= 1 else []):
            b = jb["b"]
            n_own, n_oth = jb["n_own"], jb["n_oth"]
            nkt = n_own + n_oth
            assert nkt % 4 == 0 and n_own % BT == 0 and n_oth % BT == 0 and BT == 2
            I("sp", "dma_start", r=["mod_d"], w=["gate_rep"], dma=True, out=gate_rep[:],
              in_=mod_d[b:b + 1, 2 * D:3 * D].partition_broadcast(128))
            I("pool", "memset", w=["st_f"], ap=st_f[:], constant=0.0)
            I("pool", "memset", w=["st_b0"], ap=st_b[0][:], constant=0.0)
            order = [("oth", i) for i in range(n_oth)] + [("own", i) for i in range(n_own)]
            p1 = []
            for b0 in range(0, len(order), BT):
                blk = order[b0:b0 + BT]
                kind, first = blk[0]
                off = (n_own * 128 if kind == "oth" else 0) + first * 128
                p1.append((blk, kind, first, off))
            r_G.n = NG1
            hTc[0] = gcount[0] % 2
            gcount[0] += 1
            front(jb["x"], jb["row0"] + p1[0][3], BT, b)
            for k, (blk, kind, first, off) in enumerate(p1):
                if kind == "own" and first == 0 and n_oth > 0:
                    I("pool", "tensor_scalar_mul", r=["st_f", "cst"], w=["st_f"], out=st_f[:], in0=st_f[:], scalar1=flags[:, 0:1])
                    I("pool", "tensor_scalar_mul", r=["st_b0", "cst"], w=["st_b0"], out=st_b[0][:], in0=st_b[0][:], scalar1=flags[:, 1:2])
                tiles = []
                for t, (kd, i) in enumerate(blk):
                    if kd == "own":
                        tiles.append(record(g_ktile(t, t, jb["tab0"] + off + t * 128, i, i, None)))
                    else:
                        tiles.append(record(g_ktile(t, t, jb["tab0"] + off + t * 128, n_own + i, None, i)))
                nxt = []
                if k + 1 < len(p1):
                    hTc[0] = gcount[0] % 2
                    gcount[0] += 1
                    nxt = record(front_g(jb["x"], jb["row0"] + p1[k + 1][3], BT, b))
                kz = zip_chunks(tiles[0], tiles[1])
                mg = []
                for ci, c in enumerate(kz):
                    mg.append(c)
                    if ci < len(nxt):
                        mg.append(nxt[ci])
                mg.extend(nxt[len(kz):])
                replay(mg)
            if _STAGE < 2:
                break
            def g_scan():
                cur = 0
                for n in range(n_own - 1, -1, -1):
                    nxt = 1 - cur
                    for h in range(4):
                        I("dve", "scalar_tensor_tensor", r=[f"st_b{cur}", "dec", f"Rb{n}"], w=[f"st_b{nxt}"], out=st_b[nxt][:, h, :],
                          in0=st_b[cur][:, h, :], scalar=dec[:, 5, h:h + 1], in1=Rfb[:, n, h, 1, :], op0=ALU.mult, op1=ALU.add)
                    I("dve", "tensor_copy", r=[f"st_b{cur}"], w=[f"Rb{n}"], out=Rfb[:, n, :, 1, :], in_=st_b[cur][:])
                    cur = nxt
                    yield

            scan_chunks = record(g_scan())
            blocks = list(range(0, n_own, BT))
            r_G.n = NG2
            r_G.i = -1

            def rec_fp(bi):
                b0 = blocks[bi]
                par = bi % 2
                hTc[0] = gcount[0] % 2
                gcount[0] += 1
                head = record(g_FPhead(jb["x"], jb["row0"] + b0 * 128, BT, b, par))
                tl = [record(g_FPtile(t, t, jb["tab0"] + (b0 + t) * 128, b0 + t, par)) for t in range(BT)]
                return head + zip_chunks(tl[0], tl[1])

            fp0 = rec_fp(0)
            mg0 = []
            si_ = 0
            for ci, c in enumerate(fp0):
                mg0.append(c)
                if ci < 8:
                    take = -(-len(scan_chunks) // 8)
                    mg0.extend(scan_chunks[si_:si_ + take])
                    si_ += take
            assert si_ >= len(scan_chunks) and len(fp0) >= 12
            replay(mg0)
            o_prev = []
            for bi, b0 in enumerate(blocks):
                at = record(g_AT(BT, nkt, bi % 2))
                fp = rec_fp(bi + 1) if bi + 1 < len(blocks) else []
                nhead = min(len(o_prev), nkt - 1)
                headc = []
                for k in range(nhead):
                    headc.append(at[k])
                    headc.append(o_prev[k])
                headc.extend(o_prev[nhead:])
                replay(headc)
                replay(merge_weighted(at[nhead:], split_pe(fp)))
                o_prev = record(g_O(jb["x"], jb["row0"] + b0 * 128, BT, jb["y"], bi % 2))
            replay(o_prev)
        I("sp", "wait_only", r=store_keys)
        S.emit(nc, ctx)
        with nc.Block() as block:
            @block.sync
            def _(e):
                S.run_stream("sp", e)

            @block.scalar
            def _(e):
                S.run_stream("act", e)

            @block.vector
            def _(e):
                S.run_stream("dve", e)

            @block.gpsimd
            def _(e):
                S.run_stream("pool", e)

            @block.tensor
            def _(e):
                S.run_stream("pe", e)
    return nc


def _rope_tab(pos):
    pos = np.asarray(pos)
    row = (pos // 64).astype(np.float32)
    col = (pos % 64).astype(np.float32)
    out = np.zeros((len(pos), 384), np.float32)

    def fill(hd, c_off, s_off):
        half = hd // 2
        freqs = (np.float32(10000.0) ** (-np.arange(0, half, 2, dtype=np.float32) / np.float32(half))).astype(np.float32)
        ar = row[:, None] * freqs[None, :]
        ac = col[:, None] * freqs[None, :]
        q = half // 2
        out[:, c_off:c_off + hd] = np.concatenate([np.cos(ar), np.cos(ar), np.cos(ac), np.cos(ac)], 1)
        out[:, s_off:s_off + hd] = np.concatenate([-np.sin(ar), np.sin(ar), -np.sin(ac), np.sin(ac)], 1)
    fill(128, 0, 128)
    fill(64, 256, 320)
    return out


def _consts(flag_f):
    p = np.arange(128, dtype=np.float32)
    cst = np.zeros((128, 8 + 128 + 128 + 64 + 2), np.float32)
    cst[:, 0] = 127 - p
    cst[:, 1] = p
    cst[:, 2] = p + 1
    cst[:, 3] = 128 - p
    cst[:, 4] = 128
    c = p[None, :]
    m = p[:, None]
    cst[:, 8:136] = np.maximum(c - m, 0)
    cst[:, 136:264] = np.maximum(m - c, 0)
    cst[:, 264:328] = np.repeat(128.0 * np.arange(16, dtype=np.float32), 4)[None, :]
    cst[:, 328] = flag_f
    cst[:, 329] = 1.0 - flag_f
    return cst


_PAIR = np.array([0, 4, 1, 5, 2, 6, 3, 7])


def _perm_cols():
    idx = np.arange(INC)
    for off in (O_AQ, O_AG):
        blk = idx[off:off + 512].reshape(8, 64)[_PAIR].reshape(-1)
        idx[off:off + 512] = blk
    return idx


def run(inputs, n_cores, NS, SS, SP, trace=False):
    f32 = lambda a: np.ascontiguousarray(np.asarray(a, dtype=np.float32))
    x_prompt, x_sample = f32(inputs["x_prompt"]), f32(inputs["x_sample"])
    c_prompt, c_sample = f32(inputs["c_prompt"]), f32(inputs["c_sample"])
    half = SP // 2
    cols = _perm_cols()
    w_in_p = f32(inputs["w_in"][0][:, cols])
    rows = np.arange(D)
    rows[512:] = 512 + np.arange(512).reshape(8, 64)[_PAIR].reshape(-1)
    w_out_p = f32(inputs["w_out"][0][rows, :])
    ngT = f32(inputs["norm_g"][0].reshape(KC, 128).T)
    common = {
        "w_ada": f32(inputs["w_ada"][0]), "b_ada": f32(inputs["b_ada"][0][None, :]),
        "w_in": w_in_p, "w_out": w_out_p, "ngT": ngT,
        "lrf": f32(inputs["ret_log_rate_fwd"][0][None, :]), "lrb": f32(inputs["ret_log_rate_bwd"][0][None, :]),
        "gng": f32(inputs["ret_gn_g"][0].reshape(1, 512)), "qg": f32(inputs["q_norm_g"][0][None, :]),
        "kg": f32(inputs["k_norm_g"][0][None, :]),
    }
    in_maps = []
    for c in range(n_cores):
        pj, hf = c // 2, c % 2
        own = np.arange(hf * half, (hf + 1) * half)
        oth = np.arange((1 - hf) * half, (2 - hf) * half)
        xp = np.concatenate([x_prompt[pj, own], x_prompt[pj, oth]], 0)
        tab = np.concatenate([_rope_tab(np.arange(SS)), _rope_tab(own), _rope_tab(oth)], 0)
        cb = np.concatenate([c_sample[c * NS:(c + 1) * NS], c_prompt[pj:pj + 1]], 0)
        cT = cb.T.reshape(KC, 128, NS + 1).transpose(1, 0, 2).reshape(128, KC * (NS + 1))
        m = dict(common)
        m.update({"xs": f32(x_sample[c * NS:(c + 1) * NS].reshape(NS * SS, D)), "xp": f32(xp), "tab": f32(tab),
                  "cT": f32(cT), "cst": _consts(float(hf))})
        in_maps.append(m)
    nc = build_nc(NS, SS, half, half)
    res = run_bass_kernel_spmd(nc, in_maps, core_ids=list(range(n_cores)), trace=trace)
    y_s = np.concatenate([r["ys"].reshape(NS, SS, D) for r in res.results], 0)
    y_p = np.zeros((n_cores // 2, SP, D), np.float32)
    for c in range(n_cores):
        y_p[c // 2, (c % 2) * half:(c % 2 + 1) * half] = res.results[c]["yp"]
    return (y_p, y_s), res


def kernel(x_prompt, x_sample, c_prompt, c_sample, norm_g, w_ada, b_ada, w_in, ret_log_rate_fwd, ret_log_rate_bwd,
           ret_gn_g, q_norm_g, k_norm_g, w_out):
    inputs = dict(x_prompt=x_prompt, x_sample=x_sample, c_prompt=c_prompt, c_sample=c_sample, norm_g=norm_g, w_ada=w_ada,
                  b_ada=b_ada, w_in=w_in, ret_log_rate_fwd=ret_log_rate_fwd, ret_log_rate_bwd=ret_log_rate_bwd,
                  ret_gn_g=ret_gn_g, q_norm_g=q_norm_g, k_norm_g=k_norm_g, w_out=w_out)
    inputs = {k: np.asarray(v) for k, v in inputs.items()}
    (y_p, y_s), _ = run(inputs, 8, 4, 2048, 4096)
    return (y_p.astype(np.float32), y_s.astype(np.float32))
```

```python
import contextlib
import math
import numpy as np
import concourse.bass as bass
import concourse.mybir as mybir
from concourse.bass_utils import run_bass_kernel_spmd

F32 = mybir.dt.float32
BF16 = mybir.dt.bfloat16
AF = mybir.ActivationFunctionType
ALU = mybir.AluOpType
AX = mybir.AxisListType

D = 1024
KC = 8
INC = 3328
EPS = 1e-6
O_RQ, O_RK, O_RV, O_RG, O_AQ, O_AK, O_AV, O_AG = 0, 512, 1024, 1536, 2048, 2560, 2688, 2816

EPOCH = 24000
NDMASEM = 32


class _Op:
    __slots__ = ("eng", "fn", "deps", "idx", "dma", "signal", "sem", "val", "waits", "slot")


class Sched:
    ENGS = ("pe", "act", "dve", "pool", "sp")

    def __init__(self):
        self.ops = []
        self.last_w = {}
        self.readers = {}
        self.ndma = 0
        self.slot_last = {}

    def add(self, eng, fn, reads=(), writes=(), dma=False):
        op = _Op()
        op.eng, op.fn, op.dma = eng, fn, dma
        op.idx = len(self.ops)
        op.signal = False
        deps = {}
        for k in reads:
            w = self.last_w.get(k)
            if w is not None:
                deps[w] = True
        for k in writes:
            w = self.last_w.get(k)
            if w is not None and w not in deps:
                deps[w] = False
            for r in self.readers.get(k, ()):
                if r not in deps:
                    deps[r] = False
        if dma:
            slot = self.ndma % NDMASEM
            self.ndma += 1
            op.slot = slot
            prev = self.slot_last.get(slot)
            if prev is not None and prev not in deps:
                deps[prev] = True
            self.slot_last[slot] = op.idx
        op.deps = deps
        for k in writes:
            self.last_w[k] = op.idx
            self.readers[k] = []
        wset = set(writes)
        for k in reads:
            if k not in wset:
                lst = self.readers.setdefault(k, [])
                if not dma:
                    lst[:] = [r for r in lst if self.ops[r].dma or self.ops[r].eng != eng]
                lst.append(op.idx)
        self.ops.append(op)
        return op.idx

    def _resolve(self):
        ops = self.ops
        waited = {e: {e2: -1 for e2 in self.ENGS} for e in self.ENGS}
        dma_seen = {e: set() for e in self.ENGS}
        for x in ops:
            need = {}
            x.waits = []
            for p_idx, raw in x.deps.items():
                p = ops[p_idx]
                if p.dma:
                    if p_idx not in dma_seen[x.eng]:
                        dma_seen[x.eng].add(p_idx)
                        x.waits.append(p_idx)
                    continue
                if p.eng == x.eng and not raw and not x.dma and x.eng == "pe":
                    continue
                if waited[x.eng][p.eng] >= p_idx:
                    continue
                if p.eng not in need or need[p.eng] < p_idx:
                    need[p.eng] = p_idx
            for e2, p_idx in need.items():
                waited[x.eng][e2] = p_idx
                ops[p_idx].signal = True
                x.waits.append(p_idx)

    def emit(self, nc, ctx):
        self._resolve()
        ops = self.ops
        counts = {e: 0 for e in self.ENGS}
        eng_sems = {}
        ndma = 0
        dma_sems = [ctx.enter_context(nc.semaphore(f"dq{i}")) for i in range(NDMASEM)]
        dma_cnt = [0] * NDMASEM
        for x in ops:
            if x.dma:
                s = x.slot
                dma_cnt[s] += 16
                x.sem, x.val = dma_sems[s], dma_cnt[s]
            elif x.signal:
                c = counts[x.eng]
                key = (x.eng, c // EPOCH)
                if key not in eng_sems:
                    eng_sems[key] = ctx.enter_context(nc.semaphore(f"s_{x.eng}_{key[1]}"))
                x.sem, x.val = eng_sems[key], c % EPOCH + 1
                counts[x.eng] = c + 1
        self.streams = {e: [] for e in self.ENGS}
        for x in ops:
            self.streams[x.eng].append(x)

    def run_stream(self, eng_name, eng):
        ops = self.ops
        for x in self.streams[eng_name]:
            for p_idx in x.waits:
                p = ops[p_idx]
                eng.wait_ge(p.sem, p.val)
            m, kw = x.fn
            if m == "wait_only":
                continue
            ins = getattr(eng, m)(**kw)
            if x.dma:
                ins.then_inc(x.sem, 16)
            elif x.signal:
                ins.then_inc(x.sem, 1)


class Rot:
    def __init__(self, name, n):
        self.name, self.n, self.i = name, n, -1

    def next(self):
        self.i = (self.i + 1) % self.n
        return self.i, f"{self.name}{self.i}"


_STAGE = 99
NG1 = 8
NG2 = 3
N_ST = 4
BT = 2
NQ = BT * 128


def build_nc(NS, SS, SPO, SPX):
    NB = NS + 1
    nc = bass.Bass("TRN2", target_bir_lowering=False)
    dt_in = lambda name, shape: nc.dram_tensor(name, shape, F32, kind="ExternalInput").ap()
    xs_d = dt_in("xs", [NS * SS, D])
    xp_d = dt_in("xp", [SPO + SPX, D])
    tab_d = dt_in("tab", [SS + SPO + SPX, 384])
    cT_d = dt_in("cT", [128, KC * NB])
    wada_d = dt_in("w_ada", [D, 3 * D])
    bada_d = dt_in("b_ada", [1, 3 * D])
    win_d = dt_in("w_in", [D, INC])
    wout_d = dt_in("w_out", [D, D])
    ngT_d = dt_in("ngT", [128, KC])
    lrf_d = dt_in("lrf", [1, 4])
    lrb_d = dt_in("lrb", [1, 4])
    gng_d = dt_in("gng", [1, 512])
    qg_d = dt_in("qg", [1, 64])
    kg_d = dt_in("kg", [1, 64])
    NCST = 8 + 128 + 128 + 64 + 2
    cst_d = dt_in("cst", [128, NCST])
    ys_d = nc.dram_tensor("ys", [NS * SS, D], F32, kind="ExternalOutput").ap()
    yp_d = nc.dram_tensor("yp", [SPO, D], F32, kind="ExternalOutput").ap()
    mod_d = nc.dram_tensor("mod_scratch", [NB, 3 * D], F32, kind="Internal").ap()

    NT_OWN = max(SS, SPO) // 128
    NT_K = max(SS, SPO + SPX) // 128
    S = Sched()

    with contextlib.ExitStack() as ctx:
        sb_bytes = [0]

        def sb(name, shape, dt=F32):
            n = 1
            for d_ in shape[1:]:
                n *= d_
            sb_bytes[0] += ((n * (2 if dt == BF16 else 4) + 31) // 32) * 32
            return ctx.enter_context(nc.sbuf_tensor("s_" + name, shape, dt))

        def ps(name, shape, dt=F32):
            return ctx.enter_context(nc.psum_tensor("p_" + name, shape, dt))

        w_in = sb("w_in_sb", [128, KC, INC], BF16)
        w_out = sb("w_out_sb", [128, KC, D], BF16)
        akT = sb("akT", [128, NT_K * 128], BF16)
        av_ext = sb("av_ext", [128, NT_K, 192], BF16)
        Rfb = sb("Rfb", [128, NT_OWN, 4, 2, 128], BF16)
        ident = sb("ident", [128, 128], BF16)
        identf = sb("identf", [128, 16])
        cst = sb("cst", [128, NCST])
        posc = cst[:, 0:8]
        DPm = cst[:, 8:136]
        DNm = cst[:, 136:264]
        jtab = cst[:, 264:328]
        flags = cst[:, 328:330]
        eps_t = sb("eps_t", [128, 1])
        one_t = sb("one_t", [128, 1])
        lns_t = sb("lns_t", [128, 1])
        lg = sb("lg", [128, 8])
        dec = sb("dec", [128, 6, 4])
        wb = sb("wb", [128, 16, 4])
        mask = sb("mask", [128, 4, 128], BF16)
        gng = sb("gng", [128, 512])
        qg = sb("qg", [128, 64])
        kg = sb("kg", [128, 64])
        ngT = sb("ngT", [128, KC])
        AB = sb("AB", [128, 2, KC, NB])
        cT = sb("cT", [128, KC, NB])
        gate_rep = sb("gate_rep", [128, D])

        xbuf = [sb(f"xbuf{i}", [128, D]) for i in range(2)]
        xs_bf = [sb(f"xsbf{i}", [128, D], BF16) for i in range(1)]
        stat = [sb(f"stat{i}", [128, 16]) for i in range(4)]
        hT2 = [sb(f"hT{i}", [128, KC, NQ], BF16) for i in range(2)]
        hTc = [0]
        tabt = [sb(f"tabt{i}", [128, 384]) for i in range(2)]
        t1 = [sb(f"t1_{i}", [128, 512]) for i in range(2)]
        t2 = [sb(f"t2_{i}", [128, 512]) for i in range(2)]
        t3 = [sb(f"t3_{i}", [128, 512]) for i in range(2)]
        tok_bf = [sb(f"tokbf{i}", [128, 512], BF16) for i in range(4)]
        rqT2 = [sb(f"rqT{i}", [128, 4, 128], BF16) for i in range(2)]
        rkT2 = [sb(f"rkT_t{i}", [128, 4, 128], BF16) for i in range(2)]
        v_bf2 = [sb(f"v_bf{i}", [128, 4, 128], BF16) for i in range(2)]
        sgbuf = [sb(f"sgbuf{i}", [128, 512]) for i in range(2)]
        oabuf = [sb(f"oabuf{i}", [128, 512]) for i in range(2)]
        st_f = oabuf[0][:].rearrange("p (h e) -> p h e", h=4)
        vfb = oabuf[1][:].bitcast(BF16).rearrange("p (h f e) -> p h f e", h=4, f=2)
        st_b = [sgbuf[i][:].rearrange("p (h e) -> p h e", h=4) for i in range(2)]
        xo = [sb(f"xo{i}", [128, D]) for i in range(1)]
        aqT2 = [sb(f"aqT{i}", [128, 4, NQ], BF16) for i in range(2)]
        sgT2 = [sb(f"sgT{i}", [128, 4, NQ], BF16) for i in range(2)]
        pm2 = [sb(f"pm{i}", [128, 4, 128], BF16) for i in range(2)]
        mixT_r2 = [sb(f"mixT_r{i}", [128, 4, NQ], BF16) for i in range(2)]
        mixT_a = sb("mixT_a", [128, 4, NQ], BF16)
        pT = [sb(f"pT{i}", [128, 2, NQ], BF16) for i in range(4)]
        rec = sb("rec", [128, NQ])

        if _STAGE != 99:
            print("SBUF bytes/partition:", sb_bytes[0])
        sT = [ps(f"sT{i}", [128, 2, NQ]) for i in range(4)]
        oT = ps("oT", [128, 512])
        G = [ps(f"G{i}", [128, 512]) for i in range(3)]
        flat = lambda t: t[:].rearrange("p a q -> p (a q)")
        G = [g[:, :] for g in G] + [flat(sT[3]), flat(sT[0]), flat(sT[1]), flat(sT[2]), oT[:, :]]
        GKEYS = ["G0", "G1", "G2", "sT3", "sT0", "sT1", "sT2", "oT"]

        r_x = Rot("xbuf", 2)
        r_xs = Rot("xsbf", 1)
        r_stat = Rot("stat", 4)
        r_tab = Rot("tabt", 2)
        r_t1 = Rot("t1_", 2)
        r_t2 = Rot("t2_", 2)
        r_t3 = Rot("t3_", 2)
        r_tok = Rot("tokbf", 4)
        class RotG:
            def __init__(self):
                self.n, self.i = 3, -1

            def next(self):
                self.i = (self.i + 1) % self.n
                return self.i, GKEYS[self.i]

        r_G = RotG()
        r_sT = Rot("sT", N_ST)
        r_pT = Rot("pT", N_ST)
        store_keys = []

        PSK = {"G0", "G1", "G2", "sT0", "sT1", "sT2", "sT3", "oT"}
        evt = [0]
        REC = [None]
        gcount = [0]
        KALIAS = {"st_f": "oabuf_t0", "vfb": "oabuf_t1", "st_b0": "sgbuf_t0", "st_b1": "sgbuf_t1"}
        r_xo = Rot("xo", 1)

        def I(eng, method, r=(), w=(), dma=False, **kw):
            r = [KALIAS.get(k, k) for k in r]
            w = [KALIAS.get(k, k) for k in w]
            w = list(w) + [k for k in r if k in PSK and k not in w]
            if REC[0] is not None:
                REC[0][-1].append((eng, (method, kw), list(r), w, dma))
            else:
                S.add(eng, (method, kw), reads=list(r), writes=w, dma=dma)

        def bc(ap, shape, axis):
            return ap.unsqueeze(axis).to_broadcast(shape)

        def v3(ap, h):
            return ap.rearrange("p (h d) -> p h d", h=h)

        for _once in (0,):
            I("sp", "dma_start", w=["cst"], dma=True, out=cst[:], in_=cst_d)
            I("sp", "dma_start", w=["lg"], dma=True, out=lg[:, 0:4], in_=lrf_d.partition_broadcast(128))
            I("sp", "dma_start", w=["lg"], dma=True, out=lg[:, 4:8], in_=lrb_d.partition_broadcast(128))
            I("sp", "dma_start", w=["gng"], dma=True, out=gng[:], in_=gng_d.partition_broadcast(128))
            I("sp", "dma_start", w=["qg"], dma=True, out=qg[:], in_=qg_d.partition_broadcast(128))
            I("sp", "dma_start", w=["kg"], dma=True, out=kg[:], in_=kg_d.partition_broadcast(128))
            I("sp", "dma_start", w=["ngT"], dma=True, out=ngT[:], in_=ngT_d)
            I("sp", "dma_start", w=["cT"], dma=True, out=cT[:].rearrange("p k b -> p (k b)"), in_=cT_d)

            I("pool", "memset", w=["eps"], ap=eps_t[:], constant=EPS)
            I("pool", "memset", w=["one"], ap=one_t[:], constant=1.0)
            I("pool", "memset", w=["lns"], ap=lns_t[:], constant=math.log(128.0 ** -0.5))
            I("pool", "memset", w=["av_all"], ap=av_ext[:], constant=1.0)
            I("pool", "memset", w=["t1_0"], ap=t1[0][:, 0:128], constant=1.0)
            I("pool", "affine_select", r=["t1_0"], w=["t1_0"], out=t1[0][:, 0:128], in_=t1[0][:, 0:128], pattern=[[-1, 128]],
              compare_op=ALU.is_equal, fill=0.0, base=0, channel_multiplier=1)
            I("pool", "tensor_copy", r=["t1_0"], w=["ident"], out=ident[:], in_=t1[0][:, 0:128])
            I("pool", "tensor_copy", r=["t1_0"], w=["identf"], out=identf[0:16, :], in_=t1[0][0:16, 0:16])

            if _STAGE < 0.2 and _STAGE < 1:
                break
            I("act", "activation", r=["lg"], w=["lg"], out=lg[:], in_=lg[:], func=AF.Exp)
            I("dve", "tensor_scalar_mul", r=["lg"], w=["lg"], out=lg[:], in0=lg[:], scalar1=-1.0)
            for i, (half, col, use_s) in enumerate([(0, 0, True), (1, 1, True), (0, 2, False), (1, 3, False),
                                                    (0, 4, False), (1, 4, False)]):
                kw = dict(out=dec[:, i, :], in_=lg[:, half * 4:half * 4 + 4], func=AF.Exp, scale=posc[:, col:col + 1])
                if use_s:
                    kw["bias"] = lns_t[:]
                I("act", "activation", r=["lg", "cst", "lns"], w=["dec"], **kw)
            I("dve", "tensor_tensor", r=["cst", "lg"], w=["wb"], out=wb[:], in0=jtab.rearrange("p (j h) -> p j h", h=4),
              in1=bc(lg[:, 4:8], [128, 16, 4], 1), op=ALU.mult)
            I("act", "activation", r=["wb"], w=["wb"], out=wb[:], in_=wb[:], func=AF.Exp)
            for h in range(4):
                hsl = slice(h * 128, (h + 1) * 128)
                I("dve", "tensor_scalar_mul", r=["cst", "lg"], w=["t2_0"], out=t2[0][:, hsl], in0=DPm, scalar1=lg[:, h:h + 1])
                I("dve", "scalar_tensor_tensor", r=["cst", "lg", "t2_0"], w=["t2_0"], out=t2[0][:, hsl], in0=DNm,
                  scalar=lg[:, 4 + h:5 + h], in1=t2[0][:, hsl], op0=ALU.mult, op1=ALU.add)
            I("act", "activation", r=["t2_0", "lns"], w=["mask"], out=mask[:].rearrange("p h c -> p (h c)"), in_=t2[0][:],
              func=AF.Exp, bias=lns_t[:])

            if _STAGE < 0.3 and _STAGE < 1:
                break
            NSTG = NT_OWN * 512
            STG = Rfb[:].rearrange("p n h f e -> p (n h f e)").bitcast(F32)
            PW = min(INC, NSTG // 2)
            slot_i = [0]

            def stage_slot(width):
                i = slot_i[0] % 2
                slot_i[0] += 1
                return STG[:, i * (NSTG // 2):i * (NSTG // 2) + width], f"stg{i}"

            ckeys = []
            for k in range(KC):
                for c0 in range(0, INC, PW):
                    c1 = min(INC, c0 + PW)
                    st, stk = stage_slot(c1 - c0)
                    I("sp", "dma_start", w=[stk], dma=True, out=st, in_=win_d[k * 128:(k + 1) * 128, c0:c1])
                    wd = c1 - c0
                    cuts = [0, (wd // 3) // 64 * 64, (2 * wd // 3) // 64 * 64, wd]
                    for ei, eng in enumerate(("dve", "pool", "act")):
                        lo, hi = cuts[ei], cuts[ei + 1]
                        if hi <= lo:
                            continue
                        ck = f"w_in_c{len(ckeys)}"
                        ckeys.append(ck)
                        if eng == "act":
                            I("act", "activation", r=[stk], w=[ck], out=w_in[:, k, c0 + lo:c0 + hi], in_=st[:, lo:hi], func=AF.Copy)
                        else:
                            I(eng, "tensor_copy", r=[stk], w=[ck], out=w_in[:, k, c0 + lo:c0 + hi], in_=st[:, lo:hi])
            I("dve", "memset", r=ckeys, w=["w_in"], ap=stat[0][:, 15:16], constant=0.0)
            ckeys = []
            for k in range(KC):
                st, stk = stage_slot(D)
                I("sp", "dma_start", w=[stk], dma=True, out=st, in_=wout_d[k * 128:(k + 1) * 128, :])
                for ei, eng in enumerate(("dve", "pool")):
                    ck = f"w_out_c{len(ckeys)}"
                    ckeys.append(ck)
                    I(eng, "tensor_copy", r=[stk], w=[ck], out=w_out[:, k, ei * 512:(ei + 1) * 512], in_=st[:, ei * 512:(ei + 1) * 512])
            I("dve", "memset", r=ckeys, w=["w_out"], ap=stat[0][:, 14:15], constant=0.0)

            if _STAGE < 0.4 and _STAGE < 1:
                break
            cTf = cT[:].rearrange("p k b -> p (k b)")
            nkb = KC * NB
            I("act", "activation", r=["cT"], w=["rec0", "rec1"], out=rec[:, 0:nkb], in_=cTf, func=AF.Exp, scale=-1.0)
            I("dve", "tensor_scalar_add", r=["rec0", "rec1"], w=["rec0", "rec1"], out=rec[:, 0:nkb], in0=rec[:, 0:nkb], scalar1=1.0)
            I("dve", "reciprocal", r=["rec0", "rec1"], w=["rec0", "rec1"], out=rec[:, 0:nkb], in_=rec[:, 0:nkb])
            I("dve", "tensor_tensor", r=["rec0", "rec1", "cT"], w=["cT"], out=cTf, in0=cTf, in1=rec[:, 0:nkb], op=ALU.mult)
            segs = [(t1[0], "t1_0"), (t1[1], "t1_1"), (t2[0], "t2_0"), (t2[1], "t2_1"), (t3[0], "t3_0"), (t3[1], "t3_1")]
            for cc in range(6):
                seg, segk = segs[cc]
                I("sp", "dma_start", r=["ident", "mask"], w=[segk], dma=True, out=seg[0:NB, :],
                  in_=bada_d[:, cc * 512:(cc + 1) * 512].partition_broadcast(NB))
                gi, gk = r_G.next()
                kper = min(KC, (NSTG // 2) // 512)
                for k0 in range(0, KC, kper):
                    st, stk = stage_slot(kper * 512)
                    st3 = st.rearrange("p (k c) -> p k c", k=kper)
                    I("sp", "dma_start", w=[stk], dma=True, out=st3,
                      in_=wada_d[k0 * 128:(k0 + kper) * 128, cc * 512:(cc + 1) * 512].rearrange("(k p) c -> p k c", p=128))
                    for kk in range(kper):
                        k = k0 + kk
                        I("pe", "matmul", r=[stk, "cT"], w=[gk], out=G[gi][0:NB, :], lhsT=cT[:, k, :], rhs=st3[:, kk, :],
                          start=(k == 0), stop=(k == KC - 1))
                I("dve", "tensor_tensor", r=[gk, segk], w=[segk], out=seg[0:NB, :], in0=G[gi][0:NB, :], in1=seg[0:NB, :], op=ALU.add)
                I("sp", "dma_start", r=[segk], w=["mod_d"], dma=True, out=mod_d[:, cc * 512:(cc + 1) * 512], in_=seg[0:NB, :])
                if cc < 4:
                    gi2, gk2 = r_G.next()
                    for j in range(4):
                        I("pe", "transpose", r=[segk, "identf"], w=[gk2], out=G[gi2][:, j * NB:(j + 1) * NB],
                          in_=seg[0:NB, j * 128:(j + 1) * 128], identity=identf[0:NB, 0:NB])
                    for j in range(4):
                        kk = cc * 4 + j
                        I("dve", "tensor_copy", r=[gk2], w=["AB"], out=AB[:, kk // 8, kk % 8, :], in_=G[gi2][:, j * NB:(j + 1) * NB])
            if _STAGE < 0.5 and _STAGE < 1:
                break
            I("dve", "tensor_scalar_add", r=["AB"], w=["AB"], out=AB[:, 1], in0=AB[:, 1], scalar1=1.0)
            I("dve", "tensor_tensor", r=["AB", "ngT"], w=["AB"], out=AB[:, 1], in0=AB[:, 1], in1=bc(ngT[:], [128, KC, NB], 2),
              op=ALU.mult)

        xseq = []
        xpos = [0]

        def next_x_load(t):
            k = xpos[0] + (1 if t == BT - 1 else 0)
            nxt = xpos[0] + 1
            if t == BT - 1:
                xpos[0] += 1
            if nxt < len(xseq):
                src, r0 = xseq[nxt]
                I("sp", "dma_start", w=[f"xbuf{t}"], dma=True, out=xbuf[t][:], in_=src[r0 + t * 128:r0 + (t + 1) * 128, :])

        def front(x_src, row0, ntile, b):
            for t in range(ntile):
                xi, xk = t, f"xbuf{t}"
                si, sk = r_xs.next()
                ti, tk = r_stat.next()
                I("act", "activation", r=[xk], w=[sk, tk], out=xs_bf[si][:], in_=xbuf[xi][:], func=AF.Square,
                  accum_out=stat[ti][:, 0:1])
                I("act", "activation", r=[tk, "eps"], w=[tk], out=stat[ti][:, 1:2], in_=stat[ti][:, 0:1], func=AF.Ln,
                  scale=1.0 / D, bias=eps_t[:])
                I("act", "activation", r=[tk], w=[tk], out=stat[ti][:, 2:3], in_=stat[ti][:, 1:2], func=AF.Exp, scale=-0.5)
                I("act", "activation", r=[xk, tk], w=[sk], out=xs_bf[si][:], in_=xbuf[xi][:], func=AF.Copy, scale=stat[ti][:, 2:3])
                next_x_load(t)
                banks = [r_G.next(), r_G.next()]
                for k in range(KC):
                    gi, gk = banks[k // 4]
                    gbf = G[gi][:].bitcast(BF16)
                    I("pe", "transpose", r=[sk, "ident"], w=[gk], out=gbf[:, (k % 4) * 128:(k % 4 + 1) * 128],
                      in_=xs_bf[si][:, k * 128:(k + 1) * 128], identity=ident[:])
                for k in range(KC):
                    gi, gk = banks[k // 4]
                    gbf = G[gi][:].bitcast(BF16)
                    if k // 4 == 0:
                        I("act", "activation", r=[gk, "AB"], w=[f"hT{hTc[0]}"], out=hT2[hTc[0]][:, k, t * 128:(t + 1) * 128],
                          in_=gbf[:, (k % 4) * 128:(k % 4 + 1) * 128], func=AF.Identity, scale=AB[:, 1, k, b:b + 1], bias=AB[:, 0, k, b:b + 1])
                    else:
                        I("dve", "tensor_scalar", r=[gk, "AB"], w=[f"hT{hTc[0]}"], out=hT2[hTc[0]][:, k, t * 128:(t + 1) * 128],
                          in0=gbf[:, (k % 4) * 128:(k % 4 + 1) * 128], scalar1=AB[:, 1, k, b:b + 1], scalar2=AB[:, 0, k, b:b + 1],
                          op0=ALU.mult, op1=ALU.add)

        def front_g(x_src, row0, ntile, b):
            for t in range(ntile):
                xi, xk = t, f"xbuf{t}"
                si, sk = r_xs.next()
                ti, tk = r_stat.next()
                I("act", "activation", r=[xk], w=[sk, tk], out=xs_bf[si][:], in_=xbuf[xi][:], func=AF.Square,
                  accum_out=stat[ti][:, 0:1])
                I("act", "activation", r=[tk, "eps"], w=[tk], out=stat[ti][:, 1:2], in_=stat[ti][:, 0:1], func=AF.Ln,
                  scale=1.0 / D, bias=eps_t[:])
                I("act", "activation", r=[tk], w=[tk], out=stat[ti][:, 2:3], in_=stat[ti][:, 1:2], func=AF.Exp, scale=-0.5)
                I("act", "activation", r=[xk, tk], w=[sk], out=xs_bf[si][:], in_=xbuf[xi][:], func=AF.Copy, scale=stat[ti][:, 2:3])
                next_x_load(t)
                banks = [r_G.next(), r_G.next()]
                for k in range(KC):
                    gi, gk = banks[k // 4]
                    gbf = G[gi][:].bitcast(BF16)
                    I("pe", "transpose", r=[sk, "ident"], w=[gk], out=gbf[:, (k % 4) * 128:(k % 4 + 1) * 128],
                      in_=xs_bf[si][:, k * 128:(k + 1) * 128], identity=ident[:])
                for k in range(KC):
                    gi, gk = banks[k // 4]
                    gbf = G[gi][:].bitcast(BF16)
                    if k // 4 == 0:
                        I("act", "activation", r=[gk, "AB"], w=[f"hT{hTc[0]}"], out=hT2[hTc[0]][:, k, t * 128:(t + 1) * 128],
                          in_=gbf[:, (k % 4) * 128:(k % 4 + 1) * 128], func=AF.Identity, scale=AB[:, 1, k, b:b + 1], bias=AB[:, 0, k, b:b + 1])
                    else:
                        I("dve", "tensor_scalar", r=[gk, "AB"], w=[f"hT{hTc[0]}"], out=hT2[hTc[0]][:, k, t * 128:(t + 1) * 128],
                          in0=gbf[:, (k % 4) * 128:(k % 4 + 1) * 128], scalar1=AB[:, 1, k, b:b + 1], scalar2=AB[:, 0, k, b:b + 1],
                          op0=ALU.mult, op1=ALU.add)
                yield

        def proj_tok(t, c0, ncol):
            gi, gk = r_G.next()
            for k in range(KC):
                I("pe", "matmul", r=[f"hT{hTc[0]}", "w_in"], w=[gk], out=G[gi][:, 0:ncol], lhsT=hT2[hTc[0]][:, k, t * 128:(t + 1) * 128],
                  rhs=w_in[:, k, c0:c0 + ncol], start=(k == 0), stop=(k == KC - 1))
            return gi, gk

        def load_tab(row):
            ti, tk = r_tab.next()
            I("sp", "dma_start", w=[tk], dma=True, out=tabt[ti][:], in_=tab_d[row:row + 128, :])
            return ti, tk

        def rope(src, src_keys, src_psum, nh, hd, cosv, sinv, tkey, out_ap, out_key):
            q = hd // 4
            a, ak_ = r_t1.next()
            bq, bk_ = r_t2.next()
            n = nh * hd
            I("dve", "tensor_tensor", r=list(src_keys) + [tkey], w=[ak_],
              out=v3(t1[a][:, 0:n], nh), in0=v3(src, nh), in1=bc(cosv, [128, nh, hd], 1), op=ALU.mult)
            s5 = src.rearrange("p (h f j i) -> p h f j i", h=nh, f=2, j=2, i=q)
            o5 = t2[bq][:, 0:n].rearrange("p (h f j i) -> p h f j i", h=nh, f=2, j=2, i=q)
            sn4 = sinv.rearrange("p (f j i) -> p f j i", f=2, j=2, i=q)
            for j in range(2):
                I("dve", "tensor_tensor", r=list(src_keys) + [tkey], w=[bk_],
                  out=o5[:, :, :, j, :], in0=s5[:, :, :, 1 - j, :], in1=bc(sn4[:, :, j, :], [128, nh, 2, q], 1), op=ALU.mult)
            I("dve", "tensor_tensor", r=[ak_, bk_], w=[out_key], out=out_ap, in0=t1[a][:, 0:n], in1=t2[bq][:, 0:n], op=ALU.add)

        def head_rstd(src_ap, src_key, nh, hd, src_psum=True):
            a, ak_ = r_t3.next()
            n = nh * hd
            ti, tk = r_stat.next()
            I("act", "activation", r=[src_key], w=[ak_], out=t3[a][:, 0:n], in_=src_ap, func=AF.Square)
            I("dve", "tensor_reduce", r=[ak_], w=[tk], out=stat[ti][:, 4:4 + nh], in_=v3(t3[a][:, 0:n], nh), axis=AX.X, op=ALU.add)
            I("act", "activation", r=[tk, "eps"], w=[tk], out=stat[ti][:, 4:4 + nh], in_=stat[ti][:, 4:4 + nh], func=AF.Ln,
              scale=1.0 / hd, bias=eps_t[:])
            I("act", "activation", r=[tk], w=[tk], out=stat[ti][:, 4:4 + nh], in_=stat[ti][:, 4:4 + nh], func=AF.Exp, scale=-0.5)
            return stat[ti][:, 4:4 + nh], tk

        def qk_norm_rope(gi, gk, nh, gain, gain_key, ti, tk, out_ap, out_key):
            n = nh * 64
            src = G[gi][:, 0:n]
            rs, rsk = head_rstd(src, gk, nh, 64)
            a, ak_ = r_t3.next()
            I("dve", "tensor_tensor", r=[gk, rsk], w=[ak_], out=v3(t3[a][:, 0:n], nh), in0=v3(src, nh),
              in1=bc(rs, [128, nh, 64], 2), op=ALU.mult)
            I("dve", "tensor_tensor", r=[ak_, gain_key], w=[ak_], out=v3(t3[a][:, 0:n], nh), in0=v3(t3[a][:, 0:n], nh),
              in1=bc(gain, [128, nh, 64], 1), op=ALU.mult)
            rope(t3[a][:, 0:n], [ak_], False, nh, 64, tabt[ti][:, 256:320], tabt[ti][:, 320:384], tk, out_ap, out_key)

        def transpose_to(src_bf, src_key, nblk, dst_fn, dst_key, evac_engs=("act", "dve")):
            gi, gk = r_G.next()
            gbf = G[gi][:].bitcast(BF16)
            for i in range(nblk):
                I("pe", "transpose", r=[src_key, "ident"], w=[gk], out=gbf[:, i * 128:(i + 1) * 128],
                  in_=src_bf[:, i * 128:(i + 1) * 128], identity=ident[:])
            evt[0] += 1
            for i in range(nblk):
                eng = evac_engs[evt[0] % len(evac_engs)]
                if eng == "act":
                    I("act", "activation", r=[gk], w=[dst_key], out=dst_fn(i), in_=gbf[:, i * 128:(i + 1) * 128], func=AF.Copy)
                else:
                    I("dve", "tensor_copy", r=[gk], w=[dst_key], out=dst_fn(i), in_=gbf[:, i * 128:(i + 1) * 128])

        def ret_k(t, ti, tk):
            gi, gk = proj_tok(t, O_RK, 512)
            ri, rkk = r_tok.next()
            rope(G[gi][:, :], [gk], True, 4, 128, tabt[ti][:, 0:128], tabt[ti][:, 128:256], tk, tok_bf[ri][:], rkk)
            return ri, rkk

        def g_ktile(t, tp, tab_row, kslot, own_n, other_j):
            ti, tk = tp, f"tabt{tp}"
            I("sp", "dma_start", w=[tk], dma=True, out=tabt[ti][:], in_=tab_d[tab_row:tab_row + 128, :])
            gi, gk = proj_tok(t, O_RK, 512)
            ri, rkk = tp * 2, f"tokbf{tp * 2}"
            rope(G[gi][:, :], [gk], True, 4, 128, tabt[ti][:, 0:128], tabt[ti][:, 128:256], tk, tok_bf[ri][:], rkk)
            yield
            gi, gk = proj_tok(t, O_RV, 512)
            g4 = v3(G[gi][:, :], 4)
            for d_ in range(2):
                I("dve", "tensor_tensor", r=[gk, "dec"], w=["vfb"], out=vfb[:, :, d_, :], in0=g4,
                  in1=bc(dec[:, d_, :], [128, 4, 128], 2), op=ALU.mult)
            sg_ = []
            for hp in range(2):
                g2, g2k = r_G.next()
                for hh in range(2):
                    h = hp * 2 + hh
                    I("pe", "matmul", r=[rkk, "vfb"], w=[g2k], out=G[g2][:, hh * 256:(hh + 1) * 256],
                      lhsT=tok_bf[ri][:, h * 128:(h + 1) * 128], rhs=vfb[:, h].rearrange("p f e -> p (f e)"), start=True, stop=True)
                sg_.append((g2, g2k))
            for hp, (g2, g2k) in enumerate(sg_):
                S4 = G[g2][:, :].rearrange("p (h f e) -> p h f e", h=2, f=2)
                hs = slice(hp * 2, hp * 2 + 2)
                if own_n is not None:
                    I("dve", "tensor_copy", r=["st_f"], w=[f"Rf{own_n}"], out=Rfb[:, own_n, hs, 0, :], in_=st_f[:, hs, :])
                    I("act", "activation", r=[g2k], w=[f"Rb{own_n}"], out=Rfb[:, own_n, hs, 1, :], in_=S4[:, :, 1, :], func=AF.Copy)
                else:
                    for hh in range(2):
                        h = hp * 2 + hh
                        I("dve", "scalar_tensor_tensor", r=[g2k, "wb", "st_b0"], w=["st_b0"], out=st_b[0][:, h, :], in0=S4[:, hh, 1, :],
                          scalar=wb[:, other_j, h:h + 1], in1=st_b[0][:, h, :], op0=ALU.mult, op1=ALU.add)
                for hh in range(2):
                    h = hp * 2 + hh
                    I("dve", "scalar_tensor_tensor", r=[g2k, "st_f", "dec"], w=["st_f"], out=st_f[:, h, :], in0=st_f[:, h, :],
                      scalar=dec[:, 4, h:h + 1], in1=S4[:, hh, 0, :], op0=ALU.mult, op1=ALU.add)
            yield
            gi, gk = proj_tok(t, O_AK, 256)
            ai, akk = tp * 2 + 1, f"tokbf{tp * 2 + 1}"
            qk_norm_rope(gi, gk, 2, kg[:], "kg", ti, tk, tok_bf[ai][:, 0:128], akk)
            I("act", "activation", r=[gk, "av_all"], w=[f"av{kslot}"], out=av_ext[:, kslot, 0:64], in_=G[gi][:, 128:192], func=AF.Copy)
            I("act", "activation", r=[gk, "av_all"], w=[f"av{kslot}"], out=av_ext[:, kslot, 128:192], in_=G[gi][:, 192:256], func=AF.Copy)
            yield
            transpose_to(tok_bf[ai], akk, 1, lambda i: akT[:, kslot * 128:(kslot + 1) * 128], f"akT{kslot}", evac_engs=("dve",))
            yield

        def g_FPhead(x_src, row0, ntile, b, par):
            nq = ntile * 128
            sgT = sgT2[par]
            P = f"_{par}"
            for _ in front_g(x_src, row0, ntile, b):
                yield
            for i in range(4):
                gi, gk = r_G.next()
                for k in range(KC):
                    I("pe", "matmul", r=[f"hT{hTc[0]}", "w_in"], w=[gk], out=G[gi][:, 0:nq], lhsT=w_in[:, k, O_AG + i * 128:O_AG + (i + 1) * 128],
                      rhs=hT2[hTc[0]][:, k, 0:nq], start=(k == 0), stop=(k == KC - 1))
                a, ak_ = r_t3.next()
                I("act", "activation", r=[gk], w=[ak_], out=t3[a][:, 0:nq], in_=G[gi][:, 0:nq], func=AF.Exp, scale=-1.0)
                I("act", "activation", r=[ak_, "one"], w=[ak_], out=t3[a][:, 0:nq], in_=t3[a][:, 0:nq], func=AF.Ln, bias=one_t[:])
                I("act", "activation", r=[ak_], w=[ak_], out=t3[a][:, 0:nq], in_=t3[a][:, 0:nq], func=AF.Exp, scale=-1.0)
                I("dve", "tensor_tensor", r=[gk, ak_], w=[f"sgT{i}" + P], out=sgT[:, i, 0:nq], in0=G[gi][:, 0:nq], in1=t3[a][:, 0:nq],
                  op=ALU.mult)
                yield

        def g_FPtile(t, tp, tab_row, n, par):
            aqT, mixT_r = aqT2[par], mixT_r2[par]
            P = f"_{par}"
            T = f"_t{tp}"
            rqT, rkT, v_bf, pm = rqT2[tp], rkT2[tp], v_bf2[tp], pm2[tp]
            sgb, oab = sgbuf[tp], oabuf[tp]
            ts_ = slice(t * 128, (t + 1) * 128)
            ti, tk = tp, f"tabt{tp}"
            I("sp", "dma_start", w=[tk], dma=True, out=tabt[ti][:], in_=tab_d[tab_row:tab_row + 128, :])
            tk0, tk1 = f"tokbf{tp * 2}", f"tokbf{tp * 2 + 1}"
            tb0, tb1 = tok_bf[tp * 2], tok_bf[tp * 2 + 1]
            gi, gk = proj_tok(t, O_RQ, 512)
            rope(G[gi][:, :], [gk], True, 4, 128, tabt[ti][:, 0:128], tabt[ti][:, 128:256], tk, tb0[:], tk0)
            yield
            transpose_to(tb0, tk0, 4, lambda i: rqT[:, i, :], "rqT" + T)
            gi, gk = proj_tok(t, O_RK, 512)
            rope(G[gi][:, :], [gk], True, 4, 128, tabt[ti][:, 0:128], tabt[ti][:, 128:256], tk, tb1[:], tk1)
            yield
            transpose_to(tb1, tk1, 4, lambda i: rkT[:, i, :], "rkT" + T)
            gi, gk = proj_tok(t, O_RV, 512)
            I("act", "activation", r=[gk], w=["v_bf" + T], out=v_bf[:].rearrange("p h e -> p (h e)"), in_=G[gi][:, :], func=AF.Copy)
            gi, gk = proj_tok(t, O_AQ, 512)
            qk_norm_rope(gi, gk, 8, qg[:], "qg", ti, tk, tb0[:], tk0)
            yield
            transpose_to(tb0, tk0, 4, lambda i: aqT[:, i, ts_], "aqT" + P)
            gi, gk = proj_tok(t, O_RG, 512)
            a, ak_ = r_t3.next()
            sgk = "sgbuf" + T
            I("act", "activation", r=[gk], w=[ak_], out=t3[a][:], in_=G[gi][:, :], func=AF.Exp, scale=-1.0)
            I("act", "activation", r=[ak_, "one"], w=[ak_], out=t3[a][:], in_=t3[a][:], func=AF.Ln, bias=one_t[:])
            I("act", "activation", r=[ak_], w=[ak_], out=t3[a][:], in_=t3[a][:], func=AF.Exp, scale=-1.0)
            I("dve", "tensor_tensor", r=[gk, ak_], w=[sgk], out=sgb[:], in0=G[gi][:, :], in1=t3[a][:], op=ALU.mult)
            I("dve", "tensor_tensor", r=[sgk, "gng"], w=[sgk], out=sgb[:], in0=sgb[:], in1=gng[:], op=ALU.mult)
            gi, gk = r_G.next()
            for h in range(4):
                I("pe", "matmul", r=["rkT" + T, "rqT" + T], w=[gk], out=G[gi][:, h * 128:(h + 1) * 128], lhsT=rkT[:, h, :], rhs=rqT[:, h, :],
                  start=True, stop=True)
            I("dve", "tensor_tensor", r=[gk, "mask"], w=["pm" + T], out=pm[:].rearrange("p h c -> p (h c)"), in0=G[gi][:, :],
              in1=mask[:].rearrange("p h c -> p (h c)"), op=ALU.mult)
            yield
            oak = "oabuf" + T
            crs = []
            for hp in range(2):
                g2, g2k = r_G.next()
                for hh in range(2):
                    h = hp * 2 + hh
                    I("pe", "matmul", r=["rqT" + T, f"Rf{n}", f"Rb{n}"], w=[g2k], out=G[g2][:, hh * 256:(hh + 1) * 256],
                      lhsT=rqT[:, h, :], rhs=Rfb[:, n, h].rearrange("p f e -> p (f e)"), start=True, stop=True)
                crs.append((g2, g2k))
            gi, gk = r_G.next()
            for h in range(4):
                I("pe", "matmul", r=["pm" + T, "v_bf" + T], w=[gk], out=G[gi][:, h * 128:(h + 1) * 128], lhsT=pm[:, h, :], rhs=v_bf[:, h, :],
                  start=True, stop=True)
            u, uk = r_t3.next()
            for hp, (g2, g2k) in enumerate(crs):
                C4 = G[g2][:, :].rearrange("p (h f e) -> p h f e", h=2, f=2)
                hs = slice(hp * 2, hp * 2 + 2)
                o3 = v3(oab[:, hp * 256:(hp + 1) * 256], 2)
                u3 = v3(t3[u][:, hp * 256:(hp + 1) * 256], 2)
                I("dve", "tensor_tensor", r=[g2k, "dec"], w=[oak], out=o3, in0=C4[:, :, 0, :],
                  in1=bc(dec[:, 2, hs], [128, 2, 128], 2), op=ALU.mult)
                I("dve", "tensor_tensor", r=[g2k, "dec"], w=[uk], out=u3, in0=C4[:, :, 1, :],
                  in1=bc(dec[:, 3, hs], [128, 2, 128], 2), op=ALU.mult)
            I("dve", "tensor_tensor", r=[oak, uk], w=[oak], out=oab[:], in0=oab[:], in1=t3[u][:], op=ALU.add)
            I("dve", "tensor_tensor", r=[gk, oak], w=[oak], out=oab[:], in0=G[gi][:, :], in1=oab[:], op=ALU.add)
            sq, sqk = r_t3.next()
            st_i, stk = r_stat.next()
            I("dve", "tensor_tensor", r=[oak], w=[sqk], out=t3[sq][:], in0=oab[:], in1=oab[:], op=ALU.mult)
            I("dve", "tensor_reduce", r=[sqk], w=[stk], out=stat[st_i][:, 4:8], in_=v3(t3[sq][:], 4), axis=AX.X, op=ALU.add)
            I("act", "activation", r=[stk, "eps"], w=[stk], out=stat[st_i][:, 4:8], in_=stat[st_i][:, 4:8], func=AF.Ln,
              scale=1.0 / 128, bias=eps_t[:])
            I("act", "activation", r=[stk], w=[stk], out=stat[st_i][:, 4:8], in_=stat[st_i][:, 4:8], func=AF.Exp, scale=-0.5)
            I("dve", "tensor_tensor", r=[oak, stk], w=[oak], out=v3(oab[:], 4), in0=v3(oab[:], 4),
              in1=bc(stat[st_i][:, 4:8], [128, 4, 128], 2), op=ALU.mult)
            I("dve", "tensor_tensor", r=[oak, sgk], w=[tk1], out=tb1[:], in0=oab[:], in1=sgb[:], op=ALU.mult)
            yield
            transpose_to(tb1, tk1, 4, lambda i: mixT_r[:, i, ts_], "mixT_r" + P)
            yield

        def g_AT(ntile, nkt, par):
            LA = N_ST - 1
            nq = ntile * 128
            aqT, sgT = aqT2[par], sgT2[par]
            P = f"_{par}"
            its = [(g, ip, j) for g in range(2) for ip in range(2) for j in range(nkt)]

            def qk(it):
                g, ip, j = it
                rows = slice(g * 64, (g + 1) * 64)
                si, sk = r_sT.next()
                I("pe", "matmul", r=[f"akT{j}", "aqT" + P], w=[sk], out=sT[si][:, :, 0:nq], lhsT=akT[rows, j * 128:(j + 1) * 128],
                  rhs=aqT[rows, 2 * ip:2 * ip + 2, 0:nq], start=True, stop=True)
                pi, pk = r_pT.next()
                I("act", "activation", r=[sk], w=[pk], out=pT[pi][:, :, 0:nq], in_=sT[si][:, :, 0:nq], func=AF.Exp, scale=0.125)
                return pi, pk

            pend = [qk(its[k]) for k in range(min(LA, len(its)))]
            for n_it, (g, ip, j) in enumerate(its):
                pi, pk = pend.pop(0)
                if n_it + LA < len(its):
                    pend.append(qk(its[n_it + LA]))
                rows = slice(g * 64, (g + 1) * 64)
                orow = slice((1 - g) * 64, (2 - g) * 64)
                ext = slice(g * 64, g * 64 + 128)
                o3 = oT[:, :].rearrange("p (a q) -> p a q", a=2)
                I("pe", "matmul", r=[pk, f"av{j}"], w=["oT"], out=o3[:, :, 0:nq], lhsT=av_ext[:, j, ext], rhs=pT[pi][:, :, 0:nq],
                  start=(j == 0), stop=(j == nkt - 1))
                yield
                if j == nkt - 1:
                    a, ak_ = r_t1.next()
                    r3 = t1[a][:, :].rearrange("p (a q) -> p a q", a=2)
                    I("dve", "reciprocal", r=["oT"], w=[ak_], out=r3[orow, :, 0:nq], in_=o3[orow, :, 0:nq])
                    I("dve", "tensor_tensor", r=["oT", ak_], w=[ak_], out=r3[rows, :, 0:nq], in0=o3[rows, :, 0:nq],
                      in1=r3[orow, :, 0:nq], op=ALU.mult)
                    I("dve", "tensor_tensor", r=[ak_, f"sgT{2 * ip}" + P, f"sgT{2 * ip + 1}" + P], w=["mixT_a"],
                      out=mixT_a[rows, 2 * ip:2 * ip + 2, 0:nq], in0=r3[rows, :, 0:nq], in1=sgT[rows, 2 * ip:2 * ip + 2, 0:nq], op=ALU.mult)
                    yield

        def g_O(x_src, row0, ntile, y_dst, par):
            mixT_r = mixT_r2[par]
            P = f"_{par}"
            for t in range(ntile):
                ts_ = slice(t * 128, (t + 1) * 128)
                xi, xk = r_xo.next()
                I("sp", "dma_start", w=[xk], dma=True, out=xo[xi][:], in_=x_src[row0 + t * 128:row0 + (t + 1) * 128, :])
                for half in range(2):
                    gi, gk = r_G.next()
                    cs = slice(half * 512, (half + 1) * 512)
                    for k in range(KC):
                        lhs = mixT_r[:, k, ts_] if k < 4 else mixT_a[:, k - 4, ts_]
                        I("pe", "matmul", r=["mixT_r" + P, "mixT_a", "w_out"], w=[gk], out=G[gi][:, :], lhsT=lhs, rhs=w_out[:, k, cs],
                          start=(k == 0), stop=(k == KC - 1))
                    a, ak_ = r_t1.next()
                    I("dve", "tensor_tensor", r=[gk, "gate_rep"], w=[ak_], out=t1[a][:], in0=G[gi][:, :], in1=gate_rep[:, cs],
                      op=ALU.mult)
                    I("dve", "tensor_tensor", r=[ak_, xk], w=[xk], out=xo[xi][:, cs], in0=xo[xi][:, cs], in1=t1[a][:],
                      op=ALU.add)
                skey = f"YOUT{len(store_keys)}"
                store_keys.append(skey)
                I("sp", "dma_start", r=[xk], w=[skey], dma=True, out=y_dst[row0 + t * 128:row0 + (t + 1) * 128, :], in_=xo[xi][:])
                yield

        def record(gen):
            REC[0] = [[]]
            for _ in gen:
                REC[0].append([])
            chunks = [c for c in REC[0] if c]
            REC[0] = None
            return chunks

        def replay(chunks):
            for c in chunks:
                for (eng, fn, r, w, dma) in c:
                    S.add(eng, fn, reads=r, writes=w, dma=dma)

        def split_pe(chunks):
            out = []
            for c in chunks:
                cur, cur_pe = [], None
                for op in c:
                    is_pe = op[0] == "pe"
                    if cur and is_pe and not cur_pe:
                        out.append(cur)
                        cur = []
                    if cur and (not is_pe) and cur_pe:
                        out.append(cur)
                        cur = []
                    cur.append(op)
                    cur_pe = is_pe
                if cur:
                    out.append(cur)
            return out

        def zip_chunks(a, b):
            out = []
            for i in range(max(len(a), len(b))):
                if i < len(a):
                    out.append(a[i])
                if i < len(b):
                    out.append(b[i])
            return out

        def chunk_cost(c):
            t = 0.0
            prev = None
            for (eng, (m, kw), r, w, dma) in c:
                n = 1
                o = kw.get("out", kw.get("ap"))
                if o is not None:
                    for d_ in o.shape[1:]:
                        n *= d_
                if eng == "pe":
                    t += 0.15 + n / 900.0
                    continue
                if dma:
                    t += 2.0
                elif eng == "act":
                    t += 0.25 + n / 1300.0
                elif eng == "dve":
                    t += 0.12 + n / 900.0
                else:
                    t += 0.15 + n / 480.0
                if prev is not None and prev != eng:
                    t += 0.3
                prev = eng
            return t

        def merge_weighted(main, other):
            if not other:
                return list(main)
            wts = [chunk_cost(c) for c in other]
            tot = sum(wts) or 1.0
            out = []
            mi = 0
            cum = 0.0
            for c, wv in zip(other, wts):
                out.append(c)
                cum += wv
                tgt = int(round(cum / tot * len(main)))
                while mi < tgt and mi < len(main):
                    out.append(main[mi])
                    mi += 1
            out.extend(main[mi:])
            return out

        def merge_even(main, other):
            if not other:
                return list(main)
            out = []
            acc = 0.0
            oi = 0
            ratio = len(other) / float(len(main))
            for c in main:
                out.append(c)
                acc += ratio
                while acc >= 1.0 and oi < len(other):
                    out.append(other[oi])
                    oi += 1
                    acc -= 1.0
            out.extend(other[oi:])
            return out

        jobs = []
        for j in range(NS):
            jobs.append(dict(x=xs_d, row0=j * SS, tab0=0, n_own=SS // 128, n_oth=0, b=j, y=ys_d))
        jobs.append(dict(x=xp_d, row0=0, tab0=SS, n_own=SPO // 128, n_oth=SPX // 128, b=NS, y=yp_d))
        for jb in jobs:
            n_own, n_oth = jb["n_own"], jb["n_oth"]
            for b0 in range(0, n_oth, BT):
                xseq.append((jb["x"], jb["row0"] + (n_own + b0) * 128))
            for b0 in range(0, n_own, BT):
                xseq.append((jb["x"], jb["row0"] + b0 * 128))
            for b0 in range(0, n_own, BT):
                xseq.append((jb["x"], jb["row0"] + b0 * 128))
        for t in range(BT):
            I("sp", "dma_start", w=[f"xbuf{t}"], dma=True, out=xbuf[t][:], in_=xseq[0][0][xseq[0][1] + t * 128:xseq[0][1] + (t + 1) * 128, :])
        for jb in (jobs if _STAGE >= 1 else []):
            b = jb["b"]
            n_own, n_oth = jb["n_own"], jb["n_oth"]
            nkt = n_own + n_oth
            assert nkt % 4 == 0 and n_own % BT == 0 and n_oth % BT == 0 and BT == 2
            I("sp", "dma_start", r=["mod_d"], w=["gate_rep"], dma=True, out=gate_rep[:],
              in_=mod_d[b:b + 1, 2 * D:3 * D].partition_broadcast(128))
            I("pool", "memset", w=["st_f"], ap=st_f[:], constant=0.0)
            I("pool", "memset", w=["st_b0"], ap=st_b[0][:], constant=0.0)
            order = [("oth", i) for i in range(n_oth)] + [("own", i) for i in range(n_own)]
            p1 = []
            for b0 in range(0, len(order), BT):
                blk = order[b0:b0 + BT]
                kind, first = blk[0]
                off = (n_own * 128 if kind == "oth" else 0) + first * 128
                p1.append((blk, kind, first, off))
            r_G.n = NG1
            hTc[0] = gcount[0] % 2
            gcount[0] += 1
            front(jb["x"], jb["row0"] + p1[0][3], BT, b)
            for k, (blk, kind, first, off) in enumerate(p1):
                if kind == "own" and first == 0 and n_oth > 0:
                    I("pool", "tensor_scalar_mul", r=["st_f", "cst"], w=["st_f"], out=st_f[:], in0=st_f[:], scalar1=flags[:, 0:1])
                    I("pool", "tensor_scalar_mul", r=["st_b0", "cst"], w=["st_b0"], out=st_b[0][:], in0=st_b[0][:], scalar1=flags[:, 1:2])
                tiles = []
                for t, (kd, i) in enumerate(blk):
                    if kd == "own":
                        tiles.append(record(g_ktile(t, t, jb["tab0"] + off + t * 128, i, i, None)))
                    else:
                        tiles.append(record(g_ktile(t, t, jb["tab0"] + off + t * 128, n_own + i, None, i)))
                nxt = []
                if k + 1 < len(p1):
                    hTc[0] = gcount[0] % 2
                    gcount[0] += 1
                    nxt = record(front_g(jb["x"], jb["row0"] + p1[k + 1][3], BT, b))
                kz = zip_chunks(tiles[0], tiles[1])
                mg = []
                for ci, c in enumerate(kz):
                    mg.append(c)
                    if ci < len(nxt):
                        mg.append(nxt[ci])
                mg.extend(nxt[len(kz):])
                replay(mg)
            if _STAGE < 2:
                break
            def g_scan():
                cur = 0
                for n in range(n_own - 1, -1, -1):
                    nxt = 1 - cur
                    for h in range(4):
                        I("dve", "scalar_tensor_tensor", r=[f"st_b{cur}", "dec", f"Rb{n}"], w=[f"st_b{nxt}"], out=st_b[nxt][:, h, :],
                          in0=st_b[cur][:, h, :], scalar=dec[:, 5, h:h + 1], in1=Rfb[:, n, h, 1, :], op0=ALU.mult, op1=ALU.add)
                    I("dve", "tensor_copy", r=[f"st_b{cur}"], w=[f"Rb{n}"], out=Rfb[:, n, :, 1, :], in_=st_b[cur][:])
                    cur = nxt
                    yield

            scan_chunks = record(g_scan())
            blocks = list(range(0, n_own, BT))
            r_G.n = NG2
            r_G.i = -1

            def rec_fp(bi):
                b0 = blocks[bi]
                par = bi % 2
                hTc[0] = gcount[0] % 2
                gcount[0] += 1
                head = record(g_FPhead(jb["x"], jb["row0"] + b0 * 128, BT, b, par))
                tl = [record(g_FPtile(t, t, jb["tab0"] + (b0 + t) * 128, b0 + t, par)) for t in range(BT)]
                return head + zip_chunks(tl[0], tl[1])

            fp0 = rec_fp(0)
            mg0 = []
            si_ = 0
            for ci, c in enumerate(fp0):
                mg0.append(c)
                if ci < 8:
                    take = -(-len(scan_chunks) // 8)
                    mg0.extend(scan_chunks[si_:si_ + take])
                    si_ += take
            assert si_ >= len(scan_chunks) and len(fp0) >= 12
            replay(mg0)
            o_prev = []
            for bi, b0 in enumerate(blocks):
                at = record(g_AT(BT, nkt, bi % 2))
                fp = rec_fp(bi + 1) if bi + 1 < len(blocks) else []
                nhead = min(len(o_prev), nkt - 1)
                headc = []
                for k in range(nhead):
                    headc.append(at[k])
                    headc.append(o_prev[k])
                headc.extend(o_prev[nhead:])
                replay(headc)
                replay(merge_weighted(at[nhead:], split_pe(fp)))
                o_prev = record(g_O(jb["x"], jb["row0"] + b0 * 128, BT, jb["y"], bi % 2))
            replay(o_prev)
        I("sp", "wait_only", r=store_keys)
        S.emit(nc, ctx)
        with nc.Block() as block:
            @block.sync
            def _(e):
                S.run_stream("sp", e)

            @block.scalar
            def _(e):
                S.run_stream("act", e)

            @block.vector
            def _(e):
                S.run_stream("dve", e)

            @block.gpsimd
            def _(e):
                S.run_stream("pool", e)

            @block.tensor
            def _(e):
                S.run_stream("pe", e)
    return nc


def _rope_tab(pos):
    pos = np.asarray(pos)
    row = (pos // 64).astype(np.float32)
    col = (pos % 64).astype(np.float32)
    out = np.zeros((len(pos), 384), np.float32)

    def fill(hd, c_off, s_off):
        half = hd // 2
        freqs = (np.float32(10000.0) ** (-np.arange(0, half, 2, dtype=np.float32) / np.float32(half))).astype(np.float32)
        ar = row[:, None] * freqs[None, :]
        ac = col[:, None] * freqs[None, :]
        q = half // 2
        out[:, c_off:c_off + hd] = np.concatenate([np.cos(ar), np.cos(ar), np.cos(ac), np.cos(ac)], 1)
        out[:, s_off:s_off + hd] = np.concatenate([-np.sin(ar), np.sin(ar), -np.sin(ac), np.sin(ac)], 1)
    fill(128, 0, 128)
    fill(64, 256, 320)
    return out


def _consts(flag_f):
    p = np.arange(128, dtype=np.float32)
    cst = np.zeros((128, 8 + 128 + 128 + 64 + 2), np.float32)
    cst[:, 0] = 127 - p
    cst[:, 1] = p
    cst[:, 2] = p + 1
    cst[:, 3] = 128 - p
    cst[:, 4] = 128
    c = p[None, :]
    m = p[:, None]
    cst[:, 8:136] = np.maximum(c - m, 0)
    cst[:, 136:264] = np.maximum(m - c, 0)
    cst[:, 264:328] = np.repeat(128.0 * np.arange(16, dtype=np.float32), 4)[None, :]
    cst[:, 328] = flag_f
    cst[:, 329] = 1.0 - flag_f
    return cst


_PAIR = np.array([0, 4, 1, 5, 2, 6, 3, 7])


def _perm_cols():
    idx = np.arange(INC)
    for off in (O_AQ, O_AG):
        blk = idx[off:off + 512].reshape(8, 64)[_PAIR].reshape(-1)
        idx[off:off + 512] = blk
    return idx


def run(inputs, n_cores, NS, SS, SP, trace=False):
    f32 = lambda a: np.ascontiguousarray(np.asarray(a, dtype=np.float32))
    x_prompt, x_sample = f32(inputs["x_prompt"]), f32(inputs["x_sample"])
    c_prompt, c_sample = f32(inputs["c_prompt"]), f32(inputs["c_sample"])
    half = SP // 2
    cols = _perm_cols()
    w_in_p = f32(inputs["w_in"][0][:, cols])
    rows = np.arange(D)
    rows[512:] = 512 + np.arange(512).reshape(8, 64)[_PAIR].reshape(-1)
    w_out_p = f32(inputs["w_out"][0][rows, :])
    ngT = f32(inputs["norm_g"][0].reshape(KC, 128).T)
    common = {
        "w_ada": f32(inputs["w_ada"][0]), "b_ada": f32(inputs["b_ada"][0][None, :]),
        "w_in": w_in_p, "w_out": w_out_p, "ngT": ngT,
        "lrf": f32(inputs["ret_log_rate_fwd"][0][None, :]), "lrb": f32(inputs["ret_log_rate_bwd"][0][None, :]),
        "gng": f32(inputs["ret_gn_g"][0].reshape(1, 512)), "qg": f32(inputs["q_norm_g"][0][None, :]),
        "kg": f32(inputs["k_norm_g"][0][None, :]),
    }
    in_maps = []
    for c in range(n_cores):
        pj, hf = c // 2, c % 2
        own = np.arange(hf * half, (hf + 1) * half)
        oth = np.arange((1 - hf) * half, (2 - hf) * half)
        xp = np.concatenate([x_prompt[pj, own], x_prompt[pj, oth]], 0)
        tab = np.concatenate([_rope_tab(np.arange(SS)), _rope_tab(own), _rope_tab(oth)], 0)
        cb = np.concatenate([c_sample[c * NS:(c + 1) * NS], c_prompt[pj:pj + 1]], 0)
        cT = cb.T.reshape(KC, 128, NS + 1).transpose(1, 0, 2).reshape(128, KC * (NS + 1))
        m = dict(common)
        m.update({"xs": f32(x_sample[c * NS:(c + 1) * NS].reshape(NS * SS, D)), "xp": f32(xp), "tab": f32(tab),
                  "cT": f32(cT), "cst": _consts(float(hf))})
        in_maps.append(m)
    nc = build_nc(NS, SS, half, half)
    res = run_bass_kernel_spmd(nc, in_maps, core_ids=list(range(n_cores)), trace=trace)
    y_s = np.concatenate([r["ys"].reshape(NS, SS, D) for r in res.results], 0)
    y_p = np.zeros((n_cores // 2, SP, D), np.float32)
    for c in range(n_cores):
        y_p[c // 2, (c % 2) * half:(c % 2 + 1) * half] = res.results[c]["yp"]
    return (y_p, y_s), res


def kernel(x_prompt, x_sample, c_prompt, c_sample, norm_g, w_ada, b_ada, w_in, ret_log_rate_fwd, ret_log_rate_bwd,
           ret_gn_g, q_norm_g, k_norm_g, w_out):
    inputs = dict(x_prompt=x_prompt, x_sample=x_sample, c_prompt=c_prompt, c_sample=c_sample, norm_g=norm_g, w_ada=w_ada,
                  b_ada=b_ada, w_in=w_in, ret_log_rate_fwd=ret_log_rate_fwd, ret_log_rate_bwd=ret_log_rate_bwd,
                  ret_gn_g=ret_gn_g, q_norm_g=q_norm_g, k_norm_g=k_norm_g, w_out=w_out)
    inputs = {k: np.asarray(v) for k, v in inputs.items()}
    (y_p, y_s), _ = run(inputs, 8, 4, 2048, 4096)
    return (y_p.astype(np.float32), y_s.astype(np.float32))
```
